# Optimizing a Trainium2 kernel written in Bass

```python
import math
import jax
import jax.numpy as jnp
from jax import lax
import numpy as np

D_MODEL = 2048
BATCH = 2
SEQ = 4096
DEPTH = 1

ATTN_HEADS = 8
ATTN_HEAD_DIM = 64
Q_COLS = 2 * ATTN_HEADS * ATTN_HEAD_DIM
K_COLS = 2 * ATTN_HEADS * ATTN_HEAD_DIM
V_COLS = ATTN_HEADS * 2 * ATTN_HEAD_DIM
HYENA_WIDTH = D_MODEL // 2
HYENA_ORDER = 2
HY_COLS = (HYENA_ORDER + 1) * HYENA_WIDTH
GATE_COLS = 2 * D_MODEL
IN_COLS = Q_COLS + K_COLS + V_COLS + HY_COLS + GATE_COLS
FILTER_ORDER = 64
FILTER_EMB_DIM = 33
FILTER_BANDS = (FILTER_EMB_DIM - 1) // 2
FAST_DECAY_PCT = 0.3
SLOW_DECAY_PCT = 1.5
DECAY_TARGET = 1e-2
D_FF = 5632
CONV_WIDTH = 3
N_BUCKETS = 32
MAX_DISTANCE = 128
Q_BLOCK = 128
NORM_EPS = 1e-6
SUBLN_EPS = 1e-5

kernel_name = "hybrid_diffattn_hyena_convffn_block"


def rms_norm(u, g, eps=NORM_EPS):
    u32 = u.astype(jnp.float32)
    y = u32 * lax.rsqrt(jnp.mean(u32 * u32, axis=-1, keepdims=True) + eps)
    return y.astype(u.dtype) * g


def dwconv3(u, w, b):
    up = jnp.pad(u, ((0, 0), (1, 1), (0, 0)))
    return up[:, :-2] * w[0] + up[:, 1:-1] * w[1] + up[:, 2:] * w[2] + b


def t5_bucket(rel):
    half = N_BUCKETS // 2
    max_exact = half // 2
    ret = (rel > 0).astype(jnp.int32) * half
    n = jnp.abs(rel)
    nf = jnp.maximum(n, 1).astype(jnp.float32)
    large = max_exact + (jnp.log(nf / max_exact) / math.log(MAX_DISTANCE / max_exact)
                         * (half - max_exact)).astype(jnp.int32)
    large = jnp.minimum(large, half - 1)
    return ret + jnp.where(n < max_exact, n, large)


def diff_attention(q, k, v, rel_bias, lam):
    b, l = q.shape[0], q.shape[1]
    n_blk = l // Q_BLOCK
    scale = ATTN_HEAD_DIM ** -0.5
    q_blocks = jnp.moveaxis(q.reshape(b, n_blk, Q_BLOCK, 2 * ATTN_HEADS, ATTN_HEAD_DIM), 1, 0)
    key_pos = jnp.arange(l, dtype=jnp.int32)

    def one_block(args):
        q_blk, blk = args
        q_pos = blk * Q_BLOCK + jnp.arange(Q_BLOCK, dtype=jnp.int32)
        bucket = t5_bucket(key_pos[None, :] - q_pos[:, None])
        bias = jnp.moveaxis(rel_bias[bucket], -1, 0).astype(jnp.float32)
        s = jnp.einsum('bqnd,bknd->bnqk', q_blk, k).astype(jnp.float32) * scale
        s = s.reshape(b, ATTN_HEADS, 2, Q_BLOCK, l) + bias[None, :, None]
        p = jax.nn.softmax(s, axis=-1)
        w = p[:, :, 0] - lam * p[:, :, 1]
        return jnp.einsum('bhqk,bkhe->bqhe', w.astype(v.dtype), v)

    out = lax.map(one_block, (q_blocks, jnp.arange(n_blk, dtype=jnp.int32)))
    return jnp.moveaxis(out, 0, 1).reshape(b, l, ATTN_HEADS, 2 * ATTN_HEAD_DIM)


def hyena_filters(l, w1, b1, w2, b2, w3, b3, w4, freq):
    f32 = jnp.float32
    t = jnp.linspace(0.0, 1.0, l, dtype=f32)[:, None]
    t_rescaled = jnp.arange(l, dtype=f32)[:, None]
    ang = 2.0 * math.pi * t_rescaled / l
    bands = jnp.linspace(1e-4, FILTER_BANDS - 1, FILTER_BANDS, dtype=f32)[None, :]
    emb = jnp.concatenate([t, jnp.cos(bands * ang), -jnp.sin(bands * ang)], axis=-1)
    fr = freq.astype(f32)
    h = jnp.sin(fr * (emb @ w1.astype(f32) + b1.astype(f32)))
    h = jnp.sin(fr * (h @ w2.astype(f32) + b2.astype(f32)))
    h = jnp.sin(fr * (h @ w3.astype(f32) + b3.astype(f32)))
    h = h @ w4.astype(f32)
    max_decay = math.log(DECAY_TARGET) / FAST_DECAY_PCT
    min_decay = math.log(DECAY_TARGET) / SLOW_DECAY_PCT
    deltas = jnp.abs(jnp.linspace(min_decay, max_decay, HYENA_WIDTH, dtype=f32))
    h = h * jnp.exp(-t * jnp.tile(deltas, 2)[None, :])
    h_fwd, h_bwd = h[:, :HYENA_WIDTH], h[:, HYENA_WIDTH:]
    zero = jnp.zeros((1, HYENA_WIDTH), f32)
    return jnp.concatenate([h_fwd[:1] + h_bwd[:1], h_fwd[1:], zero, h_bwd[1:][::-1]], axis=0)


def long_conv(u, kern):
    l = u.shape[1]
    n = 2 * l
    u_f = jnp.fft.rfft(u.astype(jnp.float32), n=n, axis=1)
    k_f = jnp.fft.rfft(kern, n=n, axis=0)
    y = jnp.fft.irfft(u_f * k_f[None], n=n, axis=1)[:, :l]
    return y.astype(u.dtype)


def setup_inputs(seed: int = 0) -> dict:
    key = jax.random.key(seed)
    ks = jax.random.split(key, 32)

    def nrm(k, shape, scale):
        return jax.random.normal(k, shape, jnp.float32) * scale

    def gain(k, shape):
        return 1.0 + nrm(k, shape, 0.01)

    nl = DEPTH
    return {
        'x': nrm(ks[0], (BATCH, SEQ, D_MODEL), 1.0),
        'g_mix': gain(ks[1], (nl, D_MODEL)),
        'w_in': nrm(ks[2], (nl, D_MODEL, IN_COLS), D_MODEL ** -0.5),
        'lambda_q1': nrm(ks[3], (nl, ATTN_HEAD_DIM), 0.1),
        'lambda_k1': nrm(ks[4], (nl, ATTN_HEAD_DIM), 0.1),
        'lambda_q2': nrm(ks[5], (nl, ATTN_HEAD_DIM), 0.1),
        'lambda_k2': nrm(ks[6], (nl, ATTN_HEAD_DIM), 0.1),
        'g_subln': gain(ks[7], (nl, 2 * ATTN_HEAD_DIM)),
        'rel_bias': nrm(ks[8], (N_BUCKETS, ATTN_HEADS), 0.5),
        'hy_conv_w': nrm(ks[9], (nl, CONV_WIDTH, HY_COLS), CONV_WIDTH ** -0.5),
        'hy_conv_b': nrm(ks[10], (nl, HY_COLS), 0.01),
        'hy_f_w1': nrm(ks[11], (nl, FILTER_EMB_DIM, FILTER_ORDER), FILTER_EMB_DIM ** -0.5),
        'hy_f_b1': nrm(ks[12], (nl, FILTER_ORDER), 0.01),
        'hy_f_w2': nrm(ks[13], (nl, FILTER_ORDER, FILTER_ORDER), FILTER_ORDER ** -0.5),
        'hy_f_b2': nrm(ks[14], (nl, FILTER_ORDER), 0.01),
        'hy_f_w3': nrm(ks[15], (nl, FILTER_ORDER, FILTER_ORDER), FILTER_ORDER ** -0.5),
        'hy_f_b3': nrm(ks[16], (nl, FILTER_ORDER), 0.01),
        'hy_f_w4': nrm(ks[17], (nl, FILTER_ORDER, 2 * HYENA_WIDTH), 0.02),
        'hy_freq': 1.0 + nrm(ks[18], (nl, FILTER_ORDER), 0.01),
        'hy_d': nrm(ks[19], (nl, HYENA_WIDTH), 1.0),
        'w_attn_branch': nrm(ks[20], (nl, V_COLS, D_MODEL), V_COLS ** -0.5),
        'w_hyena_branch': nrm(ks[21], (nl, HYENA_WIDTH, D_MODEL), HYENA_WIDTH ** -0.5),
        'w_out': nrm(ks[22], (nl, D_MODEL, D_MODEL), D_MODEL ** -0.5),
        'g_ffn': gain(ks[23], (nl, D_MODEL)),
        'w_up': nrm(ks[24], (nl, D_MODEL, 2 * D_FF), D_MODEL ** -0.5),
        'ffn_conv_w': nrm(ks[25], (nl, CONV_WIDTH, 2 * D_FF), CONV_WIDTH ** -0.5),
        'ffn_conv_b': nrm(ks[26], (nl, 2 * D_FF), 0.01),
        'w_down': nrm(ks[27], (nl, D_FF, D_MODEL), D_FF ** -0.5),
        'g_final': gain(ks[28], (D_MODEL,)),
    }


def reference(x, g_mix, w_in, lambda_q1, lambda_k1, lambda_q2, lambda_k2, g_subln, rel_bias,
              hy_conv_w, hy_conv_b, hy_f_w1, hy_f_b1, hy_f_w2, hy_f_b2, hy_f_w3, hy_f_b3,
              hy_f_w4, hy_freq, hy_d, w_attn_branch, w_hyena_branch, w_out, g_ffn, w_up,
              ffn_conv_w, ffn_conv_b, w_down, g_final):
    f32 = jnp.float32
    b, l, _ = x.shape
    splits = [Q_COLS, Q_COLS + K_COLS, Q_COLS + K_COLS + V_COLS,
              Q_COLS + K_COLS + V_COLS + HY_COLS]
    for layer in range(DEPTH):
        lambda_init = 0.8 - 0.6 * math.exp(-0.3 * layer)
        h = rms_norm(x, g_mix[layer])
        proj = jnp.einsum('bld,de->ble', h, w_in[layer])
        q, k, v, hy, gates = jnp.split(proj, splits, axis=-1)

        lam = (jnp.exp(jnp.sum(lambda_q1[layer].astype(f32) * lambda_k1[layer].astype(f32)))
               - jnp.exp(jnp.sum(lambda_q2[layer].astype(f32) * lambda_k2[layer].astype(f32)))
               + lambda_init)
        att = diff_attention(q.reshape(b, l, 2 * ATTN_HEADS, ATTN_HEAD_DIM),
                             k.reshape(b, l, 2 * ATTN_HEADS, ATTN_HEAD_DIM),
                             v.reshape(b, l, ATTN_HEADS, 2 * ATTN_HEAD_DIM),
                             rel_bias, lam)
        att = rms_norm(att, g_subln[layer], SUBLN_EPS) * (1.0 - lambda_init)
        a_branch = jnp.einsum('ble,ed->bld', att.reshape(b, l, V_COLS), w_attn_branch[layer])

        hy = dwconv3(hy, hy_conv_w[layer], hy_conv_b[layer])
        x0, x1, hv = jnp.split(hy, 3, axis=-1)
        kern = hyena_filters(l, hy_f_w1[layer], hy_f_b1[layer], hy_f_w2[layer], hy_f_b2[layer],
                             hy_f_w3[layer], hy_f_b3[layer], hy_f_w4[layer], hy_freq[layer])
        u = x1 * hv
        y_hy = x0 * (long_conv(u, kern) + u * hy_d[layer])
        h_branch = jnp.einsum('blc,cd->bld', y_hy, w_hyena_branch[layer])

        g_a, g_h = jnp.split(gates, 2, axis=-1)
        merged = jax.nn.sigmoid(g_a) * a_branch + jax.nn.sigmoid(g_h) * h_branch
        x = x + jnp.einsum('bld,de->ble', merged, w_out[layer])

        hf = rms_norm(x, g_ffn[layer])
        up = dwconv3(jnp.einsum('bld,df->blf', hf, w_up[layer]), ffn_conv_w[layer], ffn_conv_b[layer])
        gate_u, val_u = jnp.split(up, 2, axis=-1)
        x = x + jnp.einsum('blf,fd->bld', jax.nn.silu(gate_u) * val_u, w_down[layer])
    return rms_norm(x, g_final)
```

```python
import math
from contextlib import ExitStack
import numpy as np
import ml_dtypes
import concourse.bass as bass
import concourse.mybir as mybir
from concourse.bass_utils import run_bass_kernel_spmd

F32 = mybir.dt.float32
BF = mybir.dt.bfloat16
AF = mybir.ActivationFunctionType
ALU = mybir.AluOpType
AX = mybir.AxisListType

D = 2048
L = 4096
NH = 8
DFF = 5632
CH = 1024
NCORES = 8
LQ = 1152
TQ = 384
NOUT = 1024
LAMBDA_INIT = 0.8 - 0.6 * math.exp(0.0)
WT_M0 = 3968
WT_LEN = 5120


class Buf:
    __slots__ = ("w", "r", "name")

    def __init__(self, name=""):
        self.w = {}
        self.r = {}
        self.name = name


class Sched:
    ENG = ("pe", "act", "dve", "pool", "sp")

    def __init__(self, nc, stack, n_dma_sems=12):
        self.nc = nc
        self.ops = {e: [] for e in self.ENG}
        self.esem = {e: stack.enter_context(nc.semaphore("s_" + e)) for e in self.ENG}
        self.ecnt = {e: 0 for e in self.ENG}
        self.dsem = {}
        self.dcnt = {}
        self.dnext = {}
        for q in ("sp", "pool"):
            self.dsem[q] = [stack.enter_context(nc.semaphore("d_%s%d" % (q, i))) for i in range(n_dma_sems)]
            self.dcnt[q] = [0] * n_dma_sems
            self.dnext[q] = 0
        self.waited = {e: {} for e in self.ENG}
        self.allsems = {}

    def _waits(self, eng, deps):
        out = []
        wd = self.waited[eng]
        for key, (sem, val) in deps.items():
            if wd.get(key, 0) < val:
                wd[key] = val
                out.append((sem, val))
        return out

    @staticmethod
    def _merge(dst, src):
        for k, (s, v) in src.items():
            if k not in dst or dst[k][1] < v:
                dst[k] = (s, v)

    def _deps(self, reads, writes):
        deps = {}
        for b in reads:
            self._merge(deps, b.w)
        for b in writes:
            self._merge(deps, b.w)
            self._merge(deps, b.r)
        return deps

    def _mark(self, reads, writes, key, tok):
        for b in writes:
            b.w[key] = tok
        for b in reads:
            b.r[key] = tok
        self.allsems[key] = tok

    def op(self, eng, fn, reads=(), writes=()):
        deps = self._deps(reads, writes)
        if eng == "pe":
            deps.pop("e_pe", None)
        waits = self._waits(eng, deps)
        sem = self.esem[eng]
        self.ecnt[eng] += 1
        val = self.ecnt[eng]

        def run(e, waits=waits, fn=fn, sem=sem):
            for s, v in waits:
                e.wait_ge(s, v)
            fn(e).then_inc(sem, 1)

        self.ops[eng].append(run)
        key = "e_" + eng
        self._mark(reads, writes, key, (sem, val))

    def dma(self, q, out, in_, reads=(), writes=(), slow=False):
        deps = self._deps(reads, writes)
        i = self.dnext[q]
        self.dnext[q] = (i + 1) % len(self.dsem[q])
        sem = self.dsem[q][i]
        key = "d_%s%d" % (q, i)
        if self.dcnt[q][i] > 0:
            self._merge(deps, {key: (sem, self.dcnt[q][i])})
        waits = self._waits(q, deps)
        self.dcnt[q][i] += 16
        val = self.dcnt[q][i]

        def run(e, waits=waits, out=out, in_=in_, sem=sem, slow=slow):
            for s, v in waits:
                e.wait_ge(s, v)
            if slow:
                e.dma_start(out=out, in_=in_, allow_slow_non_contiguous=True).then_inc(sem, 16)
            else:
                e.dma_start(out=out, in_=in_).then_inc(sem, 16)

        self.ops[q].append(run)
        self._mark(reads, writes, key, (sem, val))

    def barrier(self, engs=None):
        for eng in (engs or self.ENG):
            waits = self._waits(eng, dict(self.allsems))
            if waits:
                def run(e, waits=waits):
                    for s, v in waits:
                        e.wait_ge(s, v)
                self.ops[eng].append(run)


class Arena:
    def __init__(self, tensor, n):
        self.t = tensor
        self.n = n
        self.off = 0

    def reset(self):
        self.off = 0

    def take(self, n, pattern=None, **kw):
        n_al = (n + 15) // 16 * 16
        assert self.off + n_al <= self.n, ("arena overflow", self.off, n, self.n)
        ap = self.t[:, self.off:self.off + n]
        self.off += n_al
        if pattern:
            ap = ap.rearrange(pattern, **kw)
        return ap


def dram_bc(ap1d_tensor, offset, n, parts=128):
    return bass.AP(ap1d_tensor, offset, [[0, parts], [1, n]])


def build_program():
    nc = bass.Bass("TRN2", target_bir_lowering=False)
    st = ExitStack()

    def din(name, shape, dt=F32):
        return nc.dram_tensor(name, list(shape), dt, kind="ExternalInput")

    def dscr(name, shape, dt):
        return nc.dram_tensor(name, list(shape), dt)

    x_t = din("x", [L, D])
    xmy_t = din("x_my", [LQ, D])
    mask_t = din("mask_my", [128, LQ // 128])
    g_mix_t = din("g_mix", [1, D])
    w_in_t = din("w_in", [D, 10240])
    lam_t = din("lam4", [4, 64])
    g_subln_t = din("g_subln", [128, 1])
    wt_t = din("wt_bias", [NH, 128, WT_LEN])
    hycw_t = din("hycw", [128, 24, 4])
    fw1_t = din("fw1", [33, 64])
    fw2_t = din("fw2", [64, 64])
    fw3_t = din("fw3", [64, 64])
    fw4_t = din("fw4", [64, 2048])
    fvec_t = din("fvec", [64, 4])
    hyd_t = din("hyd", [128, 8])
    wa_t = din("w_attn_branch", [1024, D])
    wh_t = din("w_hyena_branch", [1024, D])
    wo_t = din("w_out", [D, D])
    g_ffn_t = din("g_ffn", [1, D])
    wup_t = din("w_up", [D, 2 * DFF])
    ffcw_t = din("ffcw", [128, 88, 4])
    wdn_t = din("w_down", [DFF, D])
    g_fin_t = din("g_final", [1, D])
    embT_t = din("c_embT", [33, L])
    delta_t = din("c_delta", [1, CH])
    negt_t = din("c_negt", [128, 32])
    ffwd_t = din("c_ffwd", [64, 128, 32 * 128], BF)
    finv_t = din("c_finv", [LQ // TQ, 128, 64 * TQ], BF)
    ident_t = din("c_ident", [128, 128], BF)
    out_t = nc.dram_tensor("out", [NOUT, D], F32, kind="ExternalOutput")

    hT_d = dscr("hT_d", [16, 128, L], BF)
    qT_d = dscr("qT_d", [8, 128, LQ], BF)
    hTm_d = dscr("hTm_d", [16, 128, LQ], BF)
    kT_d = dscr("kT_d", [8, 128, L], BF)
    v_d = dscr("v_d", [L, 1024], BF)
    hyraw_d = dscr("hyraw_d", [16, 128, L + 2], F32)
    hyrawm_d = dscr("hyrawm_d", [24, 128, LQ + 2], F32)
    gsig_d = dscr("gsig_d", [32, 128, LQ], F32)
    x0c_d = dscr("x0c_d", [8, 128, LQ], F32)
    uT_d = dscr("uT_d", [8, 128, LQ], F32)
    sig_d = dscr("sig_d", [L, 3072], BF)
    spec_d = dscr("spec_d", [4, 32, 128, CH], F32)
    Y_d = dscr("Y_d", [8, 128, 64, 128], BF)
    yhyT_d = dscr("yhyT_d", [8, 128, LQ], BF)
    attT_d = dscr("attT_d", [8, 128, LQ], BF)
    mrgT_d = dscr("mrgT_d", [16, 128, LQ], BF)
    xmid_d = dscr("xmid_d", [LQ, D], F32)
    hfT_d = dscr("hfT_d", [16, 128, LQ], BF)
    upraw_d = dscr("upraw_d", [88, 128, LQ + 2], F32)
    actT_d = dscr("actT_d", [44, 128, LQ], BF)

    NBF = 57344
    NF = 16384
    a_bf_t = st.enter_context(nc.sbuf_tensor("a_bf", [128, NBF], BF))
    a_f_t = st.enter_context(nc.sbuf_tensor("a_f", [128, NF], F32))
    ident = st.enter_context(nc.sbuf_tensor("ident", [128, 128], BF))
    ones = st.enter_context(nc.sbuf_tensor("ones", [128, 128], BF))
    cst = st.enter_context(nc.sbuf_tensor("cst", [128, 16], F32))
    PS = [st.enter_context(nc.psum_tensor("ps%d" % i, [128, 512], F32)) for i in range(7)]
    PSB = st.enter_context(nc.psum_tensor("psb", [128, 1024], BF))
    S = Sched(nc, st)
    ABF = Arena(a_bf_t, NBF)
    AFF = Arena(a_f_t, NF)
    b_ps = [Buf("ps%d" % i) for i in range(7)]
    b_psb = Buf("psb")
    b_const = Buf("const")

    def new_stage():
        S.barrier()
        ABF.reset()
        AFF.reset()

    S.op("dve", lambda e: e.memset(ones[:, :], 1.0), writes=[b_const])
    S.op("dve", lambda e: e.memset(cst[:, 0:1], 1e-6), writes=[b_const])
    S.op("dve", lambda e: e.memset(cst[:, 1:2], 1e-5), writes=[b_const])
    S.op("dve", lambda e: e.memset(cst[:, 2:3], -math.pi), writes=[b_const])
    S.op("dve", lambda e: e.memset(cst[:, 3:4], 0.0), writes=[b_const])
    S.dma("sp", ident[:, :], ident_t.ap(), writes=[b_const])

    def norm_T(src_t, g_t, dst_t, dst_buf, src_buf, tok_out=None, nrows=L, row_off=0, mask=None):
        new_stage()
        gbc = AFF.take(D)
        b_g = Buf()
        S.dma("sp", gbc, dram_bc(g_t, 0, D), writes=[b_g])
        xt = [AFF.take(D) for _ in range(2)]
        b_xt = [Buf() for _ in range(2)]
        junk = ABF.take(D)
        b_junk = Buf()
        st_ = [AFF.take(4) for _ in range(2)]
        b_st = [Buf() for _ in range(2)]
        G = 4 if nrows % 512 == 0 else 3
        mk = None
        if mask is not None:
            mk = AFF.take(16)
            S.dma("sp", mk[:, 0:LQ // 128], mask.ap(), writes=[b_g])
        if tok_out is None:
            hb = [ABF.take(D) for _ in range(2)]
            hTt = [ABF.take(16 * G * 128, "p (k t) -> p k t", t=G * 128) for _ in range(2)]
            b_hT = [Buf() for _ in range(2)]
        else:
            hb = [AFF.take(D) for _ in range(2)]
        b_hb = [Buf() for _ in range(2)]
        src = src_t.ap()
        for i in range(nrows // 128):
            s = i % 2
            S.dma("sp", xt[s], src[row_off + i * 128:row_off + (i + 1) * 128, :], reads=[src_buf], writes=[b_xt[s]])
            S.op("act", lambda e, s=s: e.activation(out=junk, in_=xt[s], func=AF.Square, accum_out=st_[s][:, 0:1]),
                 reads=[b_xt[s]], writes=[b_junk, b_st[s]])
            S.op("act", lambda e, s=s: e.activation(out=st_[s][:, 1:2], in_=st_[s][:, 0:1], func=AF.Sqrt,
                                                    bias=cst[:, 0:1], scale=1.0 / D),
                 reads=[b_st[s], b_const], writes=[b_st[s]])
            S.op("dve", lambda e, s=s: e.reciprocal(out=st_[s][:, 2:3], in_=st_[s][:, 1:2]),
                 reads=[b_st[s]], writes=[b_st[s]])
            if mk is not None:
                S.op("dve", lambda e, s=s, i=i: e.tensor_tensor(out=st_[s][:, 2:3], in0=st_[s][:, 2:3], in1=mk[:, i:i + 1], op=ALU.mult),
                     reads=[b_st[s], b_g], writes=[b_st[s]])
            S.op("dve", lambda e, s=s: e.scalar_tensor_tensor(out=hb[s], in0=xt[s], scalar=st_[s][:, 2:3], in1=gbc,
                                                              op0=ALU.mult, op1=ALU.mult),
                 reads=[b_xt[s], b_st[s], b_g], writes=[b_hb[s]])
            if tok_out is not None:
                S.dma("sp", tok_out.ap()[i * 128:(i + 1) * 128, :], hb[s], reads=[b_hb[s]], writes=[dst_buf])
                continue
            g4 = i // G
            hs = g4 % 2
            tb = i % G
            for half in range(2):
                for kk in range(8):
                    k = half * 8 + kk
                    S.op("pe", lambda e, s=s, k=k, kk=kk: e.transpose(out=PSB[:, kk * 128:(kk + 1) * 128],
                                                                     in_=hb[s][:, k * 128:(k + 1) * 128],
                                                                     identity=ident[:, :]),
                         reads=[b_hb[s], b_const], writes=[b_psb])
                S.op("act", lambda e, hs=hs, half=half, tb=tb: e.activation(
                    out=hTt[hs][:, half * 8:(half + 1) * 8, tb * 128:(tb + 1) * 128],
                    in_=PSB[:, :].rearrange("p (k t) -> p k t", t=128), func=AF.Copy),
                    reads=[b_psb], writes=[b_hT[hs]])
            if tb == G - 1:
                S.dma("sp", dst_t.ap()[:, :, g4 * G * 128:(g4 + 1) * G * 128].rearrange("k p t -> p k t"), hTt[hs],
                      reads=[b_hT[hs]], writes=[dst_buf])

    lin_cache = {}

    def linear(srcs, ncols, mode, evac, cgw=512, TT=512, at_slots=2, prologue=None, T=L, resident=False, chain=False):
        KCs = [s_[4] for s_ in srcs]
        if resident:
            at_slots = T // TT
        key = (tuple(KCs), cgw, TT, at_slots, T, resident, mode, tuple(s_[0].name for s_ in srcs))
        already_resident = False
        if chain and key in lin_cache:
            wts, b_wt, ats, b_at = lin_cache[key]
            already_resident = resident
            if prologue:
                prologue()
        else:
            new_stage()
            lin_cache.clear()
            if prologue:
                prologue()
            wts = [[ABF.take(kc * cgw, "p (k c) -> p k c", c=cgw) for _ in range(2)] for kc in KCs]
            b_wt = [[Buf() for _ in range(2)] for _ in KCs]
            ats = [[ABF.take(kc * TT, "p (k t) -> p k t", t=TT) for _ in range(at_slots)] for kc in KCs]
            b_at = [[Buf() for _ in range(at_slots)] for _ in KCs]
            lin_cache[key] = (wts, b_wt, ats, b_at)
        ncg = ncols // cgw
        ntt = T // TT
        pi = 0
        it = 0
        def load_w(cg):
            ws = cg % 2
            for si, (a_t, a_buf, w_t, c0, kc) in enumerate(srcs):
                wv = w_t.ap()[:, c0 + cg * cgw:c0 + (cg + 1) * cgw].rearrange("(k p) c -> p k c", p=128)
                half = (kc + 1) // 2
                S.dma("pool", wts[si][ws][:, 0:half, :], wv[:, 0:half, :], writes=[b_wt[si][ws]])
                if half < kc:
                    S.dma("pool", wts[si][ws][:, half:kc, :], wv[:, half:kc, :], writes=[b_wt[si][ws]])

        def load_at(n):
            tt_ = n % ntt
            sl = n % at_slots
            for si, (a_t, a_buf, w_t, c0, kc) in enumerate(srcs):
                S.dma("sp", ats[si][sl], a_t.ap()[:, :, tt_ * TT:(tt_ + 1) * TT].rearrange("k p t -> p k t"),
                      reads=[a_buf], writes=[b_at[si][sl]])

        load_w(0)
        for cg in range(ncg):
            ws = cg % 2
            if cg + 1 < ncg:
                load_w(cg + 1)
            for tt in range(ntt):
                as_ = it % at_slots
                if resident:
                    if it == 0 and not already_resident:
                        for n_ in range(ntt):
                            load_at(n_)
                else:
                    if it == 0:
                        load_at(0)
                    if at_slots > 1 and it + 1 < ncg * ntt:
                        load_at(it + 1)
                    elif at_slots == 1 and it > 0:
                        load_at(it)
                it += 1
                if mode == "fm":
                    for cc in range(cgw // 128):
                        ps = PS[pi % 4]
                        pb = b_ps[pi % 4]
                        pi += 1
                        n_mm = sum(KCs)
                        j = 0
                        for si, kc in enumerate(KCs):
                            for k in range(kc):
                                S.op("pe", lambda e, si=si, ws=ws, as_=as_, k=k, cc=cc, j=j, n_mm=n_mm, ps=ps:
                                     e.matmul(ps[:, 0:TT], lhsT=wts[si][ws][:, k, cc * 128:(cc + 1) * 128],
                                              rhs=ats[si][as_][:, k, :], start=(j == 0), stop=(j == n_mm - 1)),
                                     reads=[b_wt[si][ws], b_at[si][as_]], writes=[pb])
                                j += 1
                        evac(cg * (cgw // 128) + cc, tt, ps[:, 0:TT], pb)
                else:
                    for tb in range(TT // 128):
                        ps = PS[pi % 4]
                        pb = b_ps[pi % 4]
                        pi += 1
                        n_mm = sum(KCs)
                        j = 0
                        for si, kc in enumerate(KCs):
                            for k in range(kc):
                                S.op("pe", lambda e, si=si, ws=ws, as_=as_, k=k, tb=tb, j=j, n_mm=n_mm, ps=ps:
                                     e.matmul(ps[:, 0:cgw], lhsT=ats[si][as_][:, k, tb * 128:(tb + 1) * 128],
                                              rhs=wts[si][ws][:, k, :], start=(j == 0), stop=(j == n_mm - 1)),
                                     reads=[b_wt[si][ws], b_at[si][as_]], writes=[pb])
                                j += 1
                        evac(cg, tt * (TT // 128) + tb, ps[:, 0:cgw], pb)

    b_x = Buf("x")
    b_hT = Buf("hT")
    norm_T(x_t, g_mix_t, hT_d, b_hT, b_x)
    b_xmy = Buf("xmy")
    b_hTm = Buf("hTm")
    norm_T(xmy_t, g_mix_t, hTm_d, b_hTm, b_xmy, nrows=LQ)

    b_q, b_k, b_v, b_hyraw, b_gsig = Buf("q"), Buf("k"), Buf("v"), Buf("hyraw"), Buf("gsig")
    ev = {}

    def mk_out_slots(n, dt_arena, width):
        tiles = [dt_arena.take(width) for _ in range(n)]
        bufs = [Buf() for _ in range(n)]
        return tiles, bufs

    def qk_stage(c0, src_t, src_buf, dst_t, dst_buf, scale, T, TT, chain=False):
        cnt = [0]
        slots = {}

        def pro():
            slots["t"], slots["b"] = mk_out_slots(4, ABF, 512)

        def evac(ci, ti, ps, pb):
            s = cnt[0] % 4
            cnt[0] += 1
            o, ob = slots["t"][s], slots["b"][s]
            S.op("act", lambda e: e.activation(out=o[:, 0:TT], in_=ps, func=AF.Copy, scale=scale), reads=[pb], writes=[ob])
            S.dma("sp", dst_t.ap()[ci, :, ti * TT:(ti + 1) * TT], o[:, 0:TT], reads=[ob], writes=[dst_buf])

        linear([(src_t, src_buf, w_in_t, c0, 16)], 1024, "fm", evac, prologue=pro, T=T, TT=TT, resident=(T == LQ), chain=chain)


    def v_stage():
        cnt = [0]
        slots = {}

        def pro():
            slots["t"], slots["b"] = mk_out_slots(4, ABF, 512)

        def evac(cg, tb, ps, pb):
            s = cnt[0] % 4
            cnt[0] += 1
            o, ob = slots["t"][s], slots["b"][s]
            S.op("act", lambda e: e.activation(out=o, in_=ps, func=AF.Copy), reads=[pb], writes=[ob])
            S.dma("sp", v_d.ap()[tb * 128:(tb + 1) * 128, cg * 512:(cg + 1) * 512], o, reads=[ob], writes=[b_v])

        linear([(hT_d, b_hT, w_in_t, 2048, 16)], 1024, "tm", evac, prologue=pro)


    def raw_stage(src_t, src_buf, w_t, c0, ncols, dst_t, dst_buf, func=AF.Copy, pad=1, T=L, TT=512, chain=False):
        cnt = [0]
        slots = {}

        def pro():
            slots["t"], slots["b"] = mk_out_slots(4, AFF, 512)
            if pad:
                z = AFF.take(2)
                bz = Buf()
                S.op("dve", lambda e: e.memset(z, 0.0), writes=[bz])
                nchunks = ncols // 128
                for c in range(nchunks):
                    S.dma("sp", dst_t.ap()[c, :, 0:1], z[:, 0:1], reads=[bz], writes=[dst_buf], slow=True)
                    S.dma("sp", dst_t.ap()[c, :, T + 1:T + 2], z[:, 1:2], reads=[bz], writes=[dst_buf], slow=True)

        def evac(ci, ti, ps, pb):
            s = cnt[0] % 4
            cnt[0] += 1
            o, ob = slots["t"][s], slots["b"][s]
            S.op("act", lambda e: e.activation(out=o[:, 0:TT], in_=ps, func=func), reads=[pb], writes=[ob])
            S.dma("sp", dst_t.ap()[ci, :, pad + ti * TT:pad + (ti + 1) * TT], o[:, 0:TT], reads=[ob], writes=[dst_buf])

        linear([(src_t, src_buf, w_t, c0, 16)], ncols, "fm", evac, prologue=pro, T=T, TT=TT, resident=(T == LQ), chain=chain)

    b_hyrawm = Buf("hyrawm")
    qk_stage(1024, hT_d, b_hT, kT_d, b_k, 1.0, L, 512)
    raw_stage(hT_d, b_hT, w_in_t, 4096, 2048, hyraw_d, b_hyraw, chain=True)
    v_stage()
    qk_stage(0, hTm_d, b_hTm, qT_d, b_q, 0.125, LQ, TQ)
    raw_stage(hTm_d, b_hTm, w_in_t, 3072, 3072, hyrawm_d, b_hyrawm, T=LQ, TT=TQ, chain=True)
    raw_stage(hTm_d, b_hTm, w_in_t, 6144, 4096, gsig_d, b_gsig, func=AF.Sigmoid, pad=0, T=LQ, TT=TQ, chain=True)

    b_x0c, b_uT, b_sig = Buf("x0c"), Buf("uT"), Buf("sig")
    new_stage()
    cw = AFF.take(24 * 4, "p (c j) -> p c j", j=4)
    b_cw = Buf()
    S.dma("sp", cw, hycw_t.ap(), writes=[b_cw])

    def conv3(eng, dst, src, wts_, c, b_src, b_dst, n=512):
        dst = dst[:, 0:n]
        S.op(eng, lambda e: e.tensor_scalar(out=dst, in0=src[:, 0:n], scalar1=wts_[:, c, 0:1], scalar2=wts_[:, c, 3:4],
                                            op0=ALU.mult, op1=ALU.add), reads=[b_src, b_cw], writes=[b_dst])
        S.op(eng, lambda e: e.scalar_tensor_tensor(out=dst, in0=src[:, 1:n + 1], scalar=wts_[:, c, 1:2], in1=dst,
                                                   op0=ALU.mult, op1=ALU.add), reads=[b_src, b_cw, b_dst], writes=[b_dst])
        S.op(eng, lambda e: e.scalar_tensor_tensor(out=dst, in0=src[:, 2:n + 2], scalar=wts_[:, c, 2:3], in1=dst,
                                                   op0=ALU.mult, op1=ALU.add), reads=[b_src, b_cw, b_dst], writes=[b_dst])

    raw = [[AFF.take(514) for _ in range(3)] for _ in range(2)]
    b_raw = [[Buf() for _ in range(3)] for _ in range(2)]
    cv = [[AFF.take(512) for _ in range(3)] for _ in range(2)]
    b_cv = [[Buf() for _ in range(3)] for _ in range(2)]
    ubf = [ABF.take(512) for _ in range(2)]
    b_ubf = [Buf() for _ in range(2)]
    utm = [ABF.take(4 * 1024, "p (b c) -> p b c", c=1024) for _ in range(2)]
    b_utm = [Buf() for _ in range(2)]
    it = 0
    for tt in range(8):
        us = tt % 2
        for c in range(8):
            s = it % 2
            it += 1
            for j in (1, 2):
                S.dma("sp", raw[s][j], hyraw_d.ap()[(j - 1) * 8 + c, :, tt * 512:tt * 512 + 514],
                      reads=[b_hyraw], writes=[b_raw[s][j]])
            conv3("dve", cv[s][1], raw[s][1], cw, 8 + c, b_raw[s][1], b_cv[s][1])
            conv3("dve", cv[s][2], raw[s][2], cw, 16 + c, b_raw[s][2], b_cv[s][2])
            S.op("pool", lambda e, s=s: e.tensor_tensor(out=ubf[s], in0=cv[s][1], in1=cv[s][2], op=ALU.mult),
                 reads=[b_cv[s][1], b_cv[s][2]], writes=[b_ubf[s]])
            for tb in range(4):
                S.op("pe", lambda e, s=s, tb=tb: e.transpose(out=PSB[:, tb * 128:(tb + 1) * 128],
                                                            in_=ubf[s][:, tb * 128:(tb + 1) * 128], identity=ident[:, :]),
                     reads=[b_ubf[s], b_const], writes=[b_psb])
            S.op("act", lambda e, us=us, c=c: e.activation(out=utm[us][:, :, c * 128:(c + 1) * 128],
                                                           in_=PSB[:, 0:512].rearrange("p (b c) -> p b c", c=128),
                                                           func=AF.Copy), reads=[b_psb], writes=[b_utm[us]])
        S.dma("sp", sig_d.ap()[tt * 512:(tt + 1) * 512, 0:1024].rearrange("(b p) c -> p b c", p=128), utm[us],
              reads=[b_utm[us]], writes=[b_sig])
    for tt in range(LQ // TQ):
        for c in range(8):
            s = it % 2
            it += 1
            for j in range(3):
                S.dma("sp", raw[s][j][:, 0:TQ + 2], hyrawm_d.ap()[j * 8 + c, :, tt * TQ:tt * TQ + TQ + 2],
                      reads=[b_hyrawm], writes=[b_raw[s][j]])
            conv3("dve", cv[s][0], raw[s][0], cw, c, b_raw[s][0], b_cv[s][0], n=TQ)
            conv3("dve", cv[s][1], raw[s][1], cw, 8 + c, b_raw[s][1], b_cv[s][1], n=TQ)
            conv3("dve", cv[s][2], raw[s][2], cw, 16 + c, b_raw[s][2], b_cv[s][2], n=TQ)
            S.dma("sp", x0c_d.ap()[c, :, tt * TQ:(tt + 1) * TQ], cv[s][0][:, 0:TQ], reads=[b_cv[s][0]], writes=[b_x0c])
            S.op("pool", lambda e, s=s: e.tensor_tensor(out=cv[s][1][:, 0:TQ], in0=cv[s][1][:, 0:TQ], in1=cv[s][2][:, 0:TQ], op=ALU.mult),
                 reads=[b_cv[s][1], b_cv[s][2]], writes=[b_cv[s][1]])
            S.dma("sp", uT_d.ap()[c, :, tt * TQ:(tt + 1) * TQ], cv[s][1][:, 0:TQ], reads=[b_cv[s][1]], writes=[b_uT])

    new_stage()
    embT = [AFF.take(512) for _ in range(2)]
    b_emb = [Buf() for _ in range(2)]
    w1 = AFF.take(64)
    w2 = AFF.take(64)
    w3 = AFF.take(64)
    w4 = AFF.take(2048)
    fv = AFF.take(8)
    dl = AFF.take(CH)
    ngt = AFF.take(32)
    b_f = Buf()
    S.dma("sp", w1[0:33, :], fw1_t.ap(), writes=[b_f])
    S.dma("sp", w2[0:64, :], fw2_t.ap(), writes=[b_f])
    S.dma("sp", w3[0:64, :], fw3_t.ap(), writes=[b_f])
    S.dma("sp", w4[0:64, :], fw4_t.ap(), writes=[b_f])
    S.dma("sp", fv[0:64, 0:4], fvec_t.ap(), writes=[b_f])
    S.dma("sp", dl, dram_bc(delta_t, 0, CH), writes=[b_f])
    S.dma("sp", ngt, negt_t.ap(), writes=[b_f])
    for j in range(3):
        S.op("dve", lambda e, j=j: e.tensor_tensor(out=fv[0:64, 4 + j:5 + j], in0=fv[0:64, j:j + 1], in1=fv[0:64, 3:4],
                                                   op=ALU.mult), reads=[b_f], writes=[b_f])
    hcur = [AFF.take(512) for _ in range(2)]
    b_h = [Buf() for _ in range(2)]
    H3 = AFF.take(L)
    b_H3 = Buf()
    arg = AFF.take(512)
    b_arg = Buf()
    sA = AFF.take(512)
    sB = AFF.take(512)
    b_sA, b_sB = Buf(), Buf()
    for pt in range(8):
        S.dma("sp", embT[pt % 2][0:33, :], embT_t.ap()[:, pt * 512:(pt + 1) * 512], writes=[b_emb[pt % 2]])
        srcs_ = [(embT[pt % 2][0:33, :], w1[0:33, 0:64]), None, None]
        for ly in range(3):
            ps = PS[(pt * 3 + ly) % 4]
            pb = b_ps[(pt * 3 + ly) % 4]
            if ly == 0:
                rhs, lhsT = srcs_[0]
                rb = b_emb[pt % 2]
            else:
                rhs = hcur[(ly - 1) % 2][0:64, :]
                lhsT = (w2 if ly == 1 else w3)[0:64, 0:64]
                rb = b_h[(ly - 1) % 2]
            S.op("pe", lambda e, ps=ps, lhsT=lhsT, rhs=rhs: e.matmul(ps[0:64, :], lhsT=lhsT, rhs=rhs, start=True, stop=True),
                 reads=[b_f, rb], writes=[pb])
            S.op("dve", lambda e, ps=ps, ly=ly: e.tensor_scalar(out=arg[0:64, :], in0=ps[0:64, :], scalar1=fv[0:64, 3:4],
                                                                scalar2=fv[0:64, 4 + ly:5 + ly], op0=ALU.mult, op1=ALU.add),
                 reads=[pb, b_f], writes=[b_arg])
            if ly < 2:
                dst, db = hcur[ly % 2][0:64, :], b_h[ly % 2]
            else:
                dst, db = H3[0:64, pt * 512:(pt + 1) * 512], b_H3
            S.op("act", lambda e: e.activation(out=sA[0:64, :], in_=arg[0:64, :], func=AF.Sin, scale=0.5),
                 reads=[b_arg], writes=[b_sA])
            S.op("act", lambda e: e.activation(out=sB[0:64, :], in_=arg[0:64, :], func=AF.Sin, scale=0.25),
                 reads=[b_arg], writes=[b_sB])
            S.op("dve", lambda e: e.tensor_tensor(out=sB[0:64, :], in0=sB[0:64, :], in1=sB[0:64, :], op=ALU.mult),
                 reads=[b_sB], writes=[b_sB])
            S.op("dve", lambda e: e.tensor_scalar(out=sB[0:64, :], in0=sB[0:64, :], scalar1=-4.0, scalar2=2.0,
                                                  op0=ALU.mult, op1=ALU.add), reads=[b_sB], writes=[b_sB])
            S.op("dve", lambda e, dst=dst: e.tensor_tensor(out=dst, in0=sA[0:64, :], in1=sB[0:64, :], op=ALU.mult),
                 reads=[b_sA, b_sB], writes=[db])
    dec = [AFF.take(CH) for _ in range(2)]
    b_dec = [Buf() for _ in range(2)]
    hfb = [AFF.take(2048)] * 2
    b_hfb = [Buf()] * 2
    hpm = [ABF.take(2048) for _ in range(2)]
    b_hpm = [Buf() for _ in range(2)]
    for pc in range(32):
        s = pc % 2
        for ct in range(4):
            S.op("pe", lambda e, pc=pc, ct=ct: e.matmul(PS[ct][:, :], lhsT=H3[0:64, pc * 128:(pc + 1) * 128],
                                                        rhs=w4[0:64, ct * 512:(ct + 1) * 512], start=True, stop=True),
                 reads=[b_H3, b_f], writes=[b_ps[ct]])
        S.op("act", lambda e, s=s, pc=pc: e.activation(out=dec[s], in_=dl, func=AF.Exp, scale=ngt[:, pc:pc + 1]),
             reads=[b_f], writes=[b_dec[s]])
        for ct in range(4):
            S.op("dve", lambda e, s=s, ct=ct: e.tensor_tensor(out=hfb[s][:, ct * 512:(ct + 1) * 512], in0=PS[ct][:, :],
                                                              in1=dec[s][:, (ct % 2) * 512:(ct % 2 + 1) * 512], op=ALU.mult),
                 reads=[b_ps[ct], b_dec[s]], writes=[b_hfb[s]])
        S.op("pool", lambda e, s=s: e.tensor_tensor(out=hpm[s][:, 0:1024], in0=hfb[s][:, 0:1024], in1=hfb[s][:, 1024:2048],
                                                    op=ALU.add), reads=[b_hfb[s]], writes=[b_hpm[s]])
        S.op("pool", lambda e, s=s: e.tensor_tensor(out=hpm[s][:, 1024:2048], in0=hfb[s][:, 0:1024], in1=hfb[s][:, 1024:2048],
                                                    op=ALU.subtract), reads=[b_hfb[s]], writes=[b_hpm[s]])
        S.dma("sp", sig_d.ap()[pc * 128:(pc + 1) * 128, 1024:3072], hpm[s], reads=[b_hpm[s]], writes=[b_sig])

    b_spec = Buf("spec")
    b_Y = Buf("Y")
    new_stage()
    sigt = [ABF.take(32 * 512, "p (s c) -> p s c", c=512) for _ in range(2)]
    b_sigt = [Buf() for _ in range(2)]
    ft = [ABF.take(32 * 128, "p (s g) -> p s g", g=128) for _ in range(4)]
    b_ft = [Buf() for _ in range(4)]
    so = [AFF.take(512) for _ in range(4)]
    b_so = [Buf() for _ in range(4)]
    cts = (4, 5, 2, 3, 0, 1)

    def fcs_of(ct):
        if ct < 2:
            return list(range(64))
        if ct < 4:
            return list(range(32)) + [32]
        return list(range(32, 64))

    def load_sig(cti):
        ct = cts[cti]
        S.dma("sp", sigt[cti % 2], sig_d.ap()[:, ct * 512:(ct + 1) * 512].rearrange("(s p) c -> p s c", p=128),
              reads=[b_sig], writes=[b_sigt[cti % 2]])

    itsK = [(cti, ct, fc) for cti, ct in enumerate(cts[:4]) for fc in fcs_of(ct)]

    def load_ftK(n):
        S.dma("sp", ft[n % 3], ffwd_t.ap()[itsK[n][2]].rearrange("p (s g) -> p s g", g=128), writes=[b_ft[n % 3]])

    load_sig(0)
    load_ftK(0)
    load_ftK(1)
    for n, (cti, ct, fc) in enumerate(itsK):
        ss_ = cti % 2
        fs = n % 3
        if n + 2 < len(itsK):
            load_ftK(n + 2)
        if fc == fcs_of(ct)[0]:
            load_sig(cti + 1)
        ps = PS[n % 4]
        pb = b_ps[n % 4]
        for sc in range(32):
            S.op("pe", lambda e, ps=ps, fs=fs, sc=sc, ss_=ss_: e.matmul(ps[:, :], lhsT=ft[fs][:, sc, :], rhs=sigt[ss_][:, sc, :],
                                                                       start=(sc == 0), stop=(sc == 31)),
                 reads=[b_ft[fs], b_sigt[ss_]], writes=[pb])
        o, ob = so[n % 4], b_so[n % 4]
        S.op("act", lambda e, o=o, ps=ps: e.activation(out=o, in_=ps[:, :], func=AF.Copy), reads=[pb], writes=[ob])
        if ct in (2, 3) and fc == 32:
            S.dma("sp", spec_d.ap()[3, 0, 0:1, (ct % 2) * 512:(ct % 2 + 1) * 512], o[0:1, :], reads=[ob], writes=[b_spec])
        else:
            S.dma("sp", spec_d.ap()[2 if ct < 4 else 3, fc % 32, :, (ct % 2) * 512:(ct % 2 + 1) * 512], o, reads=[ob], writes=[b_spec])

    kin = [[AFF.take(512) for _ in range(2)] for _ in range(3)]
    b_kin = [[Buf() for _ in range(2)] for _ in range(3)]
    tq = [[AFF.take(512) for _ in range(4)] for _ in range(2)]
    b_tq = [[Buf() for _ in range(4)] for _ in range(2)]
    yo = [[ABF.take(512) for _ in range(2)] for _ in range(2)]
    b_yo = [[Buf() for _ in range(2)] for _ in range(2)]
    itsU = [(cti, ct, gc) for cti, ct in ((4, 0), (5, 1)) for gc in range(32)]

    def load_U(n):
        cti, ct, gc = itsU[n]
        for j, fc in enumerate((gc, 32 + gc)):
            sl = (n % 2) * 2 + j
            S.dma("sp", ft[sl], ffwd_t.ap()[fc].rearrange("p (s g) -> p s g", g=128), writes=[b_ft[sl]])
        for w_ in range(2):
            S.dma("sp", kin[n % 3][w_], spec_d.ap()[2 + w_, gc, :, ct * 512:(ct + 1) * 512], reads=[b_spec], writes=[b_kin[n % 3][w_]])

    load_U(0)
    for n, (cti, ct, gc) in enumerate(itsU):
        ss_ = cti % 2
        if n + 1 < len(itsU):
            load_U(n + 1)
        if gc == 0 and ct == 0:
            load_sig(5)
        pp = n % 2
        for j in range(2):
            ps, pb = PS[2 * pp + j], b_ps[2 * pp + j]
            sl = pp * 2 + j
            for sc in range(32):
                S.op("pe", lambda e, ps=ps, sl=sl, sc=sc, ss_=ss_: e.matmul(ps[:, :], lhsT=ft[sl][:, sc, :], rhs=sigt[ss_][:, sc, :],
                                                                           start=(sc == 0), stop=(sc == 31)),
                     reads=[b_ft[sl], b_sigt[ss_]], writes=[pb])
        pUc, bUc = PS[2 * pp], b_ps[2 * pp]
        pUs, bUs = PS[2 * pp + 1], b_ps[2 * pp + 1]
        Kc, Ks = kin[n % 3]
        bKc, bKs = b_kin[n % 3]
        t = tq[pp]
        bt = b_tq[pp]
        S.op("dve", lambda e, t=t, pUc=pUc, Kc=Kc: e.tensor_tensor(out=t[0], in0=pUc[:, :], in1=Kc, op=ALU.mult), reads=[bUc, bKc], writes=[bt[0]])
        S.op("dve", lambda e, t=t, pUs=pUs, Ks=Ks: e.tensor_tensor(out=t[1], in0=pUs[:, :], in1=Ks, op=ALU.mult), reads=[bUs, bKs], writes=[bt[1]])
        S.op("dve", lambda e, t=t, pUc=pUc, Ks=Ks: e.tensor_tensor(out=t[2], in0=pUc[:, :], in1=Ks, op=ALU.mult), reads=[bUc, bKs], writes=[bt[2]])
        S.op("dve", lambda e, t=t, pUs=pUs, Kc=Kc: e.tensor_tensor(out=t[3], in0=pUs[:, :], in1=Kc, op=ALU.mult), reads=[bUs, bKc], writes=[bt[3]])
        S.op("pool", lambda e, t=t, pp=pp: e.tensor_tensor(out=yo[pp][0], in0=t[0], in1=t[1], op=ALU.subtract),
             reads=[bt[0], bt[1]], writes=[b_yo[pp][0]])
        S.op("pool", lambda e, t=t, pp=pp: e.tensor_tensor(out=yo[pp][1], in0=t[2], in1=t[3], op=ALU.add),
             reads=[bt[2], bt[3]], writes=[b_yo[pp][1]])
        if gc == 0:
            S.op("pool", lambda e, t=t, pp=pp: e.tensor_copy(out=yo[pp][0][0:1, :], in_=t[0][0:1, :]), reads=[bt[0], b_yo[pp][0]], writes=[b_yo[pp][0]])
            S.op("pool", lambda e, t=t, pp=pp: e.tensor_copy(out=yo[pp][1][0:1, :], in_=t[1][0:1, :]), reads=[bt[1], b_yo[pp][1]], writes=[b_yo[pp][1]])
        S.dma("sp", Y_d.ap()[ct * 4:(ct + 1) * 4, :, gc, :].rearrange("c p e -> p c e"),
              yo[pp][0].rearrange("p (c e) -> p c e", e=128), reads=[b_yo[pp][0]], writes=[b_Y])
        S.dma("sp", Y_d.ap()[ct * 4:(ct + 1) * 4, :, 32 + gc, :].rearrange("c p e -> p c e"),
              yo[pp][1].rearrange("p (c e) -> p c e", e=128), reads=[b_yo[pp][1]], writes=[b_Y])

    b_yhy = Buf("yhy")
    new_stage()
    fv_ = ABF.take(64 * TQ, "p (f t) -> p f t", t=TQ)
    b_fv = Buf()
    yc = [ABF.take(64 * 128, "p (f c) -> p f c", c=128) for _ in range(2)]
    b_yc = [Buf() for _ in range(2)]
    hd = AFF.take(8)
    b_hd = Buf()
    S.dma("sp", hd, hyd_t.ap(), writes=[b_hd])
    xin = [[AFF.take(512) for _ in range(2)] for _ in range(2)]
    b_xin = [[Buf() for _ in range(2)] for _ in range(2)]
    yout = [ABF.take(512) for _ in range(2)]
    b_yout = [Buf() for _ in range(2)]
    it = 0

    def load7(n):
        tt_, c_ = divmod(n, 8)
        sl = n % 2
        S.dma("sp", yc[sl], Y_d.ap()[c_], reads=[b_Y], writes=[b_yc[sl]])
        S.dma("sp", xin[sl][0][:, 0:TQ], uT_d.ap()[c_, :, tt_ * TQ:(tt_ + 1) * TQ], reads=[b_uT], writes=[b_xin[sl][0]])
        S.dma("sp", xin[sl][1][:, 0:TQ], x0c_d.ap()[c_, :, tt_ * TQ:(tt_ + 1) * TQ], reads=[b_x0c], writes=[b_xin[sl][1]])

    for tt in range(LQ // TQ):
        S.dma("sp", fv_, finv_t.ap()[tt].rearrange("p (f t) -> p f t", t=TQ), writes=[b_fv])
        for c in range(8):
            s = it % 2
            if it == 0:
                load7(0)
            if it + 1 < 8 * (LQ // TQ):
                load7(it + 1)
            it += 1
            ps = PS[s]
            pb = b_ps[s]
            for f in range(64):
                S.op("pe", lambda e, ps=ps, s=s, f=f: e.matmul(ps[:, 0:TQ], lhsT=yc[s][:, f, :], rhs=fv_[:, f, :],
                                                               start=(f == 0), stop=(f == 63)),
                     reads=[b_yc[s], b_fv], writes=[pb])
            S.op("dve", lambda e, ps=ps, s=s, c=c: e.scalar_tensor_tensor(out=xin[s][0][:, 0:TQ], in0=xin[s][0][:, 0:TQ], scalar=hd[:, c:c + 1],
                                                                         in1=ps[:, 0:TQ], op0=ALU.mult, op1=ALU.add),
                 reads=[pb, b_xin[s][0], b_hd], writes=[b_xin[s][0]])
            S.op("dve", lambda e, s=s: e.tensor_tensor(out=yout[s][:, 0:TQ], in0=xin[s][0][:, 0:TQ], in1=xin[s][1][:, 0:TQ], op=ALU.mult),
                 reads=[b_xin[s][0], b_xin[s][1]], writes=[b_yout[s]])
            S.dma("sp", yhyT_d.ap()[c, :, tt * TQ:(tt + 1) * TQ], yout[s][:, 0:TQ], reads=[b_yout[s]], writes=[b_yhy])

    b_att = Buf("att")
    new_stage()
    lam = AFF.take(64 * 4 + 8)
    b_lam = Buf()
    for j in range(4):
        S.dma("sp", lam[:, j * 64:(j + 1) * 64], dram_bc(lam_t, j * 64, 64), writes=[b_lam])
    gs = AFF.take(2)
    S.dma("sp", gs[:, 0:1], g_subln_t.ap(), writes=[b_lam])
    S.op("dve", lambda e: e.tensor_scalar(out=gs[:, 1:2], in0=gs[:, 0:1], scalar1=1.0 - LAMBDA_INIT, scalar2=None, op0=ALU.mult),
         reads=[b_lam], writes=[b_lam])
    for j in range(2):
        S.op("dve", lambda e, j=j: e.tensor_tensor(out=lam[:, j * 128:j * 128 + 64], in0=lam[:, j * 128:j * 128 + 64],
                                                   in1=lam[:, j * 128 + 64:j * 128 + 128], op=ALU.mult), reads=[b_lam], writes=[b_lam])
        S.op("dve", lambda e, j=j: e.reduce_sum(out=lam[:, 256 + j:257 + j], in_=lam[:, j * 128:j * 128 + 64], axis=AX.X),
             reads=[b_lam], writes=[b_lam])
        S.op("act", lambda e, j=j: e.activation(out=lam[:, 258 + j:259 + j], in_=lam[:, 256 + j:257 + j], func=AF.Exp),
             reads=[b_lam], writes=[b_lam])
    S.op("dve", lambda e: e.tensor_tensor(out=lam[:, 260:261], in0=lam[:, 259:260], in1=lam[:, 258:259], op=ALU.subtract),
         reads=[b_lam], writes=[b_lam])
    S.op("dve", lambda e: e.tensor_scalar(out=lam[:, 260:261], in0=lam[:, 260:261], scalar1=-LAMBDA_INIT, scalar2=None, op0=ALU.add),
         reads=[b_lam], writes=[b_lam])
    neglam = lam[:, 260:261]

    qh = [ABF.take(2 * LQ, "p (m t) -> p m t", m=2) for _ in range(2)]
    kh = [ABF.take(L) for _ in range(2)]
    vh = [ABF.take(32 * 128, "p (k e) -> p k e", e=128) for _ in range(2)]
    wth = [ABF.take(WT_LEN) for _ in range(2)]
    b_hd_ = [Buf() for _ in range(2)]
    E = [ABF.take(512) for _ in range(3)]
    b_E = [Buf() for _ in range(3)]
    atth = [ABF.take(LQ) for _ in range(2)]
    b_atth = [Buf() for _ in range(2)]
    sq = [ABF.take(256) for _ in range(2)]
    b_sq = [Buf() for _ in range(2)]
    rz = [AFF.take(512) for _ in range(2)]
    o12 = [AFF.take(512) for _ in range(2)]
    at_ = [AFF.take(256) for _ in range(2)]
    rs = [AFF.take(256) for _ in range(2)]
    b_ev = [Buf() for _ in range(2)]
    QT = 192
    W2 = 2 * QT

    b_qz = Buf()
    for hs_ in range(2):
        S.op("dve", lambda e, hs_=hs_: e.memset(qh[hs_][0:64, 1, :], 0.0), writes=[b_qz])
        S.op("dve", lambda e, hs_=hs_: e.memset(qh[hs_][64:128, 0, :], 0.0), writes=[b_qz])

    def load_head(h):
        hs = h % 2
        S.dma("sp", qh[hs][0:64, 0, :], qT_d.ap()[h, 0:64, :], reads=[b_q], writes=[b_hd_[hs]])
        S.dma("sp", qh[hs][64:128, 1, :], qT_d.ap()[h, 64:128, :], reads=[b_q], writes=[b_hd_[hs]])
        S.dma("sp", kh[hs], kT_d.ap()[h], reads=[b_k], writes=[b_hd_[hs]])
        S.dma("sp", vh[hs], v_d.ap()[:, h * 128:(h + 1) * 128].rearrange("(k p) e -> p k e", p=128), reads=[b_v], writes=[b_hd_[hs]])
        S.dma("pool", wth[hs], wt_t.ap()[h], writes=[b_hd_[hs]])

    iters = [(h, qt, kc) for h in range(NH) for qt in range(LQ // QT) for kc in range(32)]

    def is_near(qt, kc):
        return True

    def emit_S(i):
        h, qt, kc = iters[i]
        hs = h % 2
        pS, bS = PS[i % 2], b_ps[i % 2]
        near = is_near(qt, kc)
        c0 = WT_M0 - kc * 128 + qt * QT
        assert 0 <= c0 <= WT_LEN - QT or not near
        S.op("pe", lambda e: e.matmul(
            pS[:, 0:W2].rearrange("p (m q) -> p m q", m=2), lhsT=kh[hs][:, kc * 128:(kc + 1) * 128],
            rhs=qh[hs][:, :, qt * QT:(qt + 1) * QT], start=True, stop=(not near)),
            reads=[b_hd_[hs], b_qz], writes=[bS])
        if near:
            for mp in range(2):
                S.op("pe", lambda e, mp=mp: e.matmul(
                    pS[:, mp * QT:(mp + 1) * QT], lhsT=ident[:, :], rhs=wth[hs][:, c0:c0 + QT], start=False, stop=(mp == 1)),
                    reads=[b_hd_[hs], b_const], writes=[bS])

    def emit_post1(h, qt):
        hs = h % 2
        os_ = qt % 2
        pO, bO = PS[2 + os_], b_ps[2 + os_]
        pZ, bZ = PS[4 + os_], b_ps[4 + os_]
        S.op("dve", lambda e: e.reciprocal(out=rz[os_][:, 0:W2], in_=pZ[:, 0:W2]), reads=[bZ], writes=[b_ev[os_]])
        S.op("dve", lambda e: e.tensor_tensor(out=o12[os_][:, 0:W2], in0=pO[:, 0:W2], in1=rz[os_][:, 0:W2], op=ALU.mult),
             reads=[bO, b_ev[os_]], writes=[b_ev[os_]])
        S.op("dve", lambda e: e.scalar_tensor_tensor(out=at_[os_][:, 0:QT], in0=o12[os_][:, QT:2 * QT], scalar=neglam,
                                                     in1=o12[os_][:, 0:QT], op0=ALU.mult, op1=ALU.add),
             reads=[b_ev[os_], b_lam], writes=[b_ev[os_]])
        S.op("dve", lambda e: e.tensor_tensor(out=sq[os_][:, 0:QT], in0=at_[os_][:, 0:QT], in1=at_[os_][:, 0:QT], op=ALU.mult),
             reads=[b_ev[os_]], writes=[b_sq[os_]])

    def emit_post2(h, qt):
        hs = h % 2
        os_ = qt % 2
        S.op("pe", lambda e: e.matmul(PS[6][:, 0:QT], lhsT=ones[:, :], rhs=sq[os_][:, 0:QT], start=True, stop=True),
             reads=[b_const, b_sq[os_]], writes=[b_ps[6]])
        S.op("act", lambda e: e.activation(out=rs[os_][:, 0:QT], in_=PS[6][:, 0:QT], func=AF.Sqrt, bias=cst[:, 1:2], scale=1.0 / 128),
             reads=[b_ps[6], b_const], writes=[b_ev[os_]])
        S.op("dve", lambda e: e.reciprocal(out=rs[os_][:, 0:QT], in_=rs[os_][:, 0:QT]), reads=[b_ev[os_]], writes=[b_ev[os_]])
        S.op("dve", lambda e: e.scalar_tensor_tensor(out=atth[hs][:, qt * QT:(qt + 1) * QT], in0=at_[os_][:, 0:QT],
                                                     scalar=gs[:, 1:2], in1=rs[os_][:, 0:QT], op0=ALU.mult, op1=ALU.mult),
             reads=[b_ev[os_], b_lam], writes=[b_atth[hs]])
        if qt == LQ // QT - 1:
            S.dma("sp", attT_d.ap()[h], atth[hs], reads=[b_atth[hs]], writes=[b_att])

    load_head(0)
    emit_S(0)
    pending = []
    for i, (h, qt, kc) in enumerate(iters):
        hs = h % 2
        if qt == 0 and kc == 0 and h + 1 < NH:
            load_head(h + 1)
        if i + 1 < len(iters):
            emit_S(i + 1)
        os_ = qt % 2
        pS, bS = PS[i % 2], b_ps[i % 2]
        pO, bO = PS[2 + os_], b_ps[2 + os_]
        pZ, bZ = PS[4 + os_], b_ps[4 + os_]
        es = i % 3
        S.op("act", lambda e, es=es, pS=pS: e.activation(out=E[es][:, 0:W2], in_=pS[:, 0:W2], func=AF.Exp), reads=[bS], writes=[b_E[es]])
        S.op("pe", lambda e, pO=pO, hs=hs, kc=kc, es=es: e.matmul(pO[:, 0:W2], lhsT=vh[hs][:, kc, :], rhs=E[es][:, 0:W2],
                                                                 start=(kc == 0), stop=(kc == 31)),
             reads=[b_hd_[hs], b_E[es]], writes=[bO])
        S.op("pe", lambda e, pZ=pZ, es=es, kc=kc: e.matmul(pZ[:, 0:W2], lhsT=ones[:, :], rhs=E[es][:, 0:W2],
                                                          start=(kc == 0), stop=(kc == 31)),
             reads=[b_const, b_E[es]], writes=[bZ])
        if pending and pending[0][0] <= i:
            _, hh, qq = pending.pop(0)
            emit_post2(hh, qq)
        if kc == 31:
            emit_post1(h, qt)
            pending.append((i + 3, h, qt))
    for _, hh, qq in pending:
        emit_post2(hh, qq)

    b_mrg = Buf("mrg")
    b_gsig_r = b_gsig
    b_stash = Buf("stash")
    new_stage()
    NT9 = LQ // TQ
    aT9 = [ABF.take(8 * LQ, "p (k t) -> p k t", t=LQ) for _ in range(2)]
    b_aT9 = [Buf() for _ in range(2)]
    S.dma("sp", aT9[0], attT_d.ap().rearrange("k p t -> p k t"), reads=[b_att], writes=[b_aT9[0]])
    S.dma("sp", aT9[1], yhyT_d.ap().rearrange("k p t -> p k t"), reads=[b_yhy], writes=[b_aT9[1]])
    w9 = [[ABF.take(8 * 512, "p (k c) -> p k c", c=512) for _ in range(2)] for _ in range(2)]
    b_w9 = [[Buf() for _ in range(2)] for _ in range(2)]
    NS9 = 6
    g9 = [[AFF.take(TQ) for _ in range(NS9)] for _ in range(2)]
    b_g9 = [[Buf() for _ in range(NS9)] for _ in range(2)]
    t9 = [[AFF.take(TQ) for _ in range(2)] for _ in range(2)]
    b_t9 = [[Buf() for _ in range(2)] for _ in range(2)]
    o9 = [ABF.take(TQ) for _ in range(3)]
    b_o9 = [Buf() for _ in range(3)]
    order9 = [(cg, tt, cc) for cg in range(4) for tt in range(NT9) for cc in range(4)]
    issued9 = [0]

    def prefetch9(upto):
        while issued9[0] <= min(upto, len(order9) - 1):
            n = issued9[0]
            cg_, tt_, cc_ = order9[n]
            ci = cg_ * 4 + cc_
            for br in range(2):
                S.dma("sp", g9[br][n % NS9], gsig_d.ap()[br * 16 + ci, :, tt_ * TQ:(tt_ + 1) * TQ], reads=[b_gsig],
                      writes=[b_g9[br][n % NS9]])
            issued9[0] += 1

    def load_w9(cg):
        for br, w_t in enumerate((wa_t, wh_t)):
            S.dma("pool", w9[br][cg % 2], w_t.ap()[:, cg * 512:(cg + 1) * 512].rearrange("(k p) c -> p k c", p=128),
                  writes=[b_w9[br][cg % 2]])

    load_w9(0)
    prefetch9(2)
    for n, (cg, tt, cc) in enumerate(order9):
        if tt == 0 and cc == 0 and cg + 1 < 4:
            load_w9(cg + 1)
        prefetch9(n + 3)
        ci = cg * 4 + cc
        pp = n % 2
        for br in range(2):
            ps, pb = PS[2 * pp + br], b_ps[2 * pp + br]
            for k in range(8):
                S.op("pe", lambda e, ps=ps, br=br, cg=cg, k=k, cc=cc, tt=tt: e.matmul(
                    ps[:, 0:TQ], lhsT=w9[br][cg % 2][:, k, cc * 128:(cc + 1) * 128], rhs=aT9[br][:, k, tt * TQ:(tt + 1) * TQ],
                    start=(k == 0), stop=(k == 7)), reads=[b_w9[br][cg % 2], b_aT9[br]], writes=[pb])
        for br in range(2):
            ps, pb = PS[2 * pp + br], b_ps[2 * pp + br]
            S.op("dve", lambda e, ps=ps, br=br, pp=pp, n=n: e.tensor_tensor(out=t9[pp][br], in0=ps[:, 0:TQ], in1=g9[br][n % NS9], op=ALU.mult),
                 reads=[pb, b_g9[br][n % NS9]], writes=[b_t9[pp][br]])
        S.op("pool", lambda e, pp=pp, n=n: e.tensor_tensor(out=o9[n % 3], in0=t9[pp][0], in1=t9[pp][1], op=ALU.add),
             reads=[b_t9[pp][0], b_t9[pp][1]], writes=[b_o9[n % 3]])
        S.dma("sp", mrgT_d.ap()[ci, :, tt * TQ:(tt + 1) * TQ], o9[n % 3], reads=[b_o9[n % 3]], writes=[b_mrg])

    b_xmid = Buf("xmid")

    def resid_stage(src_t, src_buf, w_t, kc, res_t, res_buf, dst_t, dst_buf, cgw, at_slots, TT=512):
        cnt = [0]
        slots = {}

        NSR = 6
        orderR = [(cg, tt * (TT // 128) + tb) for cg in range(D // cgw) for tt in range(LQ // TT) for tb in range(TT // 128)]
        issued = [0]

        def pro():
            slots["r"], slots["rb"] = mk_out_slots(NSR, AFF, cgw)

        def prefetch(upto):
            while issued[0] <= min(upto, len(orderR) - 1):
                n = issued[0]
                cg_, tb_ = orderR[n]
                S.dma("sp", slots["r"][n % NSR], res_t.ap()[tb_ * 128:(tb_ + 1) * 128, cg_ * cgw:(cg_ + 1) * cgw],
                      reads=[res_buf], writes=[slots["rb"][n % NSR]])
                issued[0] += 1

        def evac(cg, tb, ps, pb):
            n = cnt[0]
            cnt[0] += 1
            assert orderR[n] == (cg, tb)
            prefetch(n + 3)
            r, rb = slots["r"][n % NSR], slots["rb"][n % NSR]
            S.op("dve", lambda e: e.tensor_tensor(out=r, in0=ps, in1=r, op=ALU.add), reads=[pb, rb], writes=[rb])
            S.dma("sp", dst_t.ap()[tb * 128:(tb + 1) * 128, cg * cgw:(cg + 1) * cgw], r, reads=[rb], writes=[dst_buf])

        linear([(src_t, src_buf, w_t, 0, kc)], D, "tm", evac, cgw=cgw, at_slots=at_slots, prologue=pro, TT=TT, T=LQ, resident=(kc <= 16))

    resid_stage(mrgT_d, b_mrg, wo_t, 16, xmy_t, b_xmy, xmid_d, b_xmid, 512, 2, TT=TQ)

    b_hfT = Buf("hfT")
    norm_T(xmid_d, g_ffn_t, hfT_d, b_hfT, b_xmid, nrows=LQ, mask=mask_t)
    b_act = Buf("act")
    new_stage()
    NT11 = LQ // TQ
    fcw = AFF.take(88 * 4, "p (c j) -> p c j", j=4)
    S.dma("sp", fcw, ffcw_t.ap(), writes=[b_cw])
    hf11 = ABF.take(16 * LQ, "p (k t) -> p k t", t=LQ)
    b_hf11 = Buf()
    S.dma("sp", hf11, hfT_d.ap().rearrange("k p t -> p k t"), reads=[b_hfT], writes=[b_hf11])
    w11 = [ABF.take(16 * 512, "p (k c) -> p k c", c=512) for _ in range(2)]
    b_w11 = [Buf() for _ in range(2)]
    rawb = [[AFF.take(LQ + 2) for _ in range(4)] for _ in range(2)]
    b_rawb = [[Buf() for _ in range(4)] for _ in range(2)]
    for sl_ in range(2):
        for cc_ in range(4):
            S.op("dve", lambda e, sl_=sl_, cc_=cc_: e.memset(rawb[sl_][cc_][:, 0:1], 0.0), writes=[b_rawb[sl_][cc_]])
            S.op("dve", lambda e, sl_=sl_, cc_=cc_: e.memset(rawb[sl_][cc_][:, LQ + 1:LQ + 2], 0.0), writes=[b_rawb[sl_][cc_]])
    cvg = AFF.take(LQ)
    cvv = AFF.take(LQ)
    sil = AFF.take(LQ)
    b_cvg, b_cvv, b_sil = Buf(), Buf(), Buf()
    ao = [ABF.take(LQ) for _ in range(2)]
    b_ao = [Buf() for _ in range(2)]

    def load_w11(cg):
        wv = wup_t.ap()[:, cg * 512:(cg + 1) * 512].rearrange("(k p) c -> p k c", p=128)
        S.dma("pool", w11[cg % 2][:, 0:8, :], wv[:, 0:8, :], writes=[b_w11[cg % 2]])
        S.dma("pool", w11[cg % 2][:, 8:16, :], wv[:, 8:16, :], writes=[b_w11[cg % 2]])

    def conv_full(dst, src, cidx, b_src, b_dst):
        n = LQ
        S.op("dve", lambda e: e.tensor_scalar(out=dst, in0=src[:, 0:n], scalar1=fcw[:, cidx, 0:1], scalar2=fcw[:, cidx, 3:4],
                                              op0=ALU.mult, op1=ALU.add), reads=[b_src, b_cw], writes=[b_dst])
        S.op("dve", lambda e: e.scalar_tensor_tensor(out=dst, in0=src[:, 1:n + 1], scalar=fcw[:, cidx, 1:2], in1=dst,
                                                     op0=ALU.mult, op1=ALU.add), reads=[b_src, b_cw, b_dst], writes=[b_dst])
        S.op("dve", lambda e: e.scalar_tensor_tensor(out=dst, in0=src[:, 2:n + 2], scalar=fcw[:, cidx, 2:3], in1=dst,
                                                     op0=ALU.mult, op1=ALU.add), reads=[b_src, b_cw, b_dst], writes=[b_dst])

    load_w11(0)
    pi11 = 0
    npair = 0
    for cg in range(22):
        sl = cg % 2
        if cg + 1 < 22:
            load_w11(cg + 1)
        for tt in range(NT11):
            for cc in range(4):
                ps, pb = PS[pi11 % 4], b_ps[pi11 % 4]
                pi11 += 1
                for k in range(16):
                    S.op("pe", lambda e, ps=ps, sl=sl, k=k, cc=cc, tt=tt: e.matmul(
                        ps[:, 0:TQ], lhsT=w11[sl][:, k, cc * 128:(cc + 1) * 128], rhs=hf11[:, k, tt * TQ:(tt + 1) * TQ],
                        start=(k == 0), stop=(k == 15)), reads=[b_w11[sl], b_hf11], writes=[pb])
                S.op("act", lambda e, ps=ps, sl=sl, cc=cc, tt=tt: e.activation(out=rawb[sl][cc][:, 1 + tt * TQ:1 + (tt + 1) * TQ],
                                                                              in_=ps[:, 0:TQ], func=AF.Copy),
                     reads=[pb], writes=[b_rawb[sl][cc]])
        for pr in range(2):
            c = cg * 2 + pr
            conv_full(cvg, rawb[sl][2 * pr], c, b_rawb[sl][2 * pr], b_cvg)
            conv_full(cvv, rawb[sl][2 * pr + 1], 44 + c, b_rawb[sl][2 * pr + 1], b_cvv)
            S.op("act", lambda e: e.activation(out=sil, in_=cvg, func=AF.Silu), reads=[b_cvg], writes=[b_sil])
            S.op("pool", lambda e, npair=npair: e.tensor_tensor(out=ao[npair % 2], in0=sil, in1=cvv, op=ALU.mult),
                 reads=[b_sil, b_cvv], writes=[b_ao[npair % 2]])
            S.dma("sp", actT_d.ap()[c], ao[npair % 2], reads=[b_ao[npair % 2]], writes=[b_act])
            npair += 1

    b_xout = b_xmid
    resid_stage(actT_d, b_act, wdn_t, 44, xmid_d, b_xmid, xmid_d, b_xout, 512, 2, TT=128)

    b_out = Buf("out")
    norm_T(xmid_d, g_fin_t, None, b_out, b_xout, tok_out=out_t, nrows=NOUT, row_off=2)
    S.barrier()

    with nc.Block() as block:
        @block.tensor
        def _(e):
            for f in S.ops["pe"]:
                f(e)

        @block.scalar
        def _(e):
            for f in S.ops["act"]:
                f(e)

        @block.vector
        def _(e):
            for f in S.ops["dve"]:
                f(e)

        @block.gpsimd
        def _(e):
            for f in S.ops["pool"]:
                f(e)

        @block.sync
        def _(e):
            for f in S.ops["sp"]:
                f(e)
    st.close()
    return nc


_CONST = {}


def _t5_bucket(rel):
    half, max_exact = 16, 8
    ret = (rel > 0).astype(np.int32) * half
    n = np.abs(rel)
    nf = np.maximum(n, 1).astype(np.float32)
    large = max_exact + (np.log(nf / max_exact) / math.log(128 / max_exact) * (half - max_exact)).astype(np.int32)
    large = np.minimum(large, half - 1)
    return ret + np.where(n < max_exact, n, large)


def _constants():
    if _CONST:
        return _CONST
    bf = ml_dtypes.bfloat16
    N = 2 * L
    ang = 2.0 * np.pi * np.arange(N) / N
    ctab = np.cos(ang)
    stab = np.sin(ang)
    g = np.arange(L).reshape(32, 1, 1, 128)
    s = (np.arange(32).reshape(1, 1, 32, 1) * 128 + np.arange(128).reshape(1, 128, 1, 1))
    idx = (g * s) % N
    fc_cos = ctab[idx]
    fc_sin = stab[idx]
    nyq = np.where(s % 2 == 0, 1.0, -1.0)[0, :, :, 0]
    fc_sin[0, :, :, 0] = nyq
    ffwd = np.concatenate([fc_cos, fc_sin], 0).astype(np.float32).astype(bf).reshape(64, 128, 32 * 128)
    gg = (np.arange(32).reshape(1, 1, 32, 1) * 128 + np.arange(128).reshape(1, 128, 1, 1))
    finvs, masks, buckets = [], [], []
    for j in range(4):
        mloc = (np.arange(LQ // TQ).reshape(-1, 1, 1, 1) * TQ + np.arange(TQ).reshape(1, 1, 1, TQ))
        t = 1024 * j - 2 + mloc
        valid = (t >= 0) & (t < L) & (mloc < NOUT + 4)
        tc = np.where(valid, t, 0)
        idx = (gg * tc) % N
        ic = ctab[idx] * (2.0 / N)
        isn = stab[idx] * (2.0 / N)
        ic[:, 0:1, 0:1, :] = 1.0 / N
        isn[:, 0:1, 0:1, :] = (np.where(tc % 2 == 0, 1.0, -1.0) / N)
        ic = ic * valid
        isn = isn * valid
        finvs.append(np.concatenate([ic, isn], 2).astype(np.float32).astype(bf).reshape(LQ // TQ, 128, 64 * TQ))
        mv = valid.reshape(-1).astype(np.float32)
        masks.append(np.ascontiguousarray(mv.reshape(LQ // 128, 128).T))
        rel = np.arange(128).reshape(128, 1) - np.arange(WT_LEN).reshape(1, WT_LEN) + WT_M0 - (1024 * j - 2)
        buckets.append(_t5_bucket(np.clip(rel, -(L - 1), L - 1)))
    f32 = np.float32
    tt_ = np.linspace(0.0, 1.0, L, dtype=f32)[:, None]
    tr = np.arange(L, dtype=f32)[:, None]
    an = (f32(2.0 * math.pi) * tr / f32(L)).astype(f32)
    bands = np.linspace(1e-4, 15, 16, dtype=f32)[None, :]
    emb = np.concatenate([tt_, np.cos(bands * an), -np.sin(bands * an)], axis=-1).astype(f32)
    max_decay = math.log(1e-2) / 0.3
    min_decay = math.log(1e-2) / 1.5
    deltas = np.abs(np.linspace(min_decay, max_decay, CH, dtype=f32)).astype(f32)
    negt = (-tt_[:, 0]).reshape(32, 128).T.copy()
    _CONST.update(dict(c_ffwd=ffwd, finvs=finvs, masks=masks, buckets=buckets, c_embT=np.ascontiguousarray(emb.T),
                       c_delta=deltas.reshape(1, CH), c_negt=negt.astype(f32), c_ident=np.eye(128, dtype=f32).astype(bf)))
    return _CONST


_NC = {}


def kernel(x, g_mix, w_in, lambda_q1, lambda_k1, lambda_q2, lambda_k2, g_subln, rel_bias,
           hy_conv_w, hy_conv_b, hy_f_w1, hy_f_b1, hy_f_w2, hy_f_b2, hy_f_w3, hy_f_b3,
           hy_f_w4, hy_freq, hy_d, w_attn_branch, w_hyena_branch, w_out, g_ffn, w_up,
           ffn_conv_w, ffn_conv_b, w_down, g_final):
    f = lambda a: np.ascontiguousarray(np.asarray(a, dtype=np.float32))
    C = _constants()
    rb = f(rel_bias)
    wt_biases = [np.ascontiguousarray(np.transpose(rb[bk], (2, 0, 1))) for bk in C["buckets"]]
    hcw = f(hy_conv_w)[0]
    hcb = f(hy_conv_b)[0]
    hycw = np.ascontiguousarray(np.concatenate([hcw, hcb[None]], 0).reshape(4, 24, 128).transpose(2, 1, 0))
    fw = f(ffn_conv_w)[0]
    fb = f(ffn_conv_b)[0]
    ffcw = np.ascontiguousarray(np.concatenate([fw, fb[None]], 0).reshape(4, 88, 128).transpose(2, 1, 0))
    common = {
        "g_mix": f(g_mix), "w_in": f(w_in)[0],
        "lam4": np.ascontiguousarray(np.stack([f(lambda_q1)[0], f(lambda_k1)[0], f(lambda_q2)[0], f(lambda_k2)[0]], 0)),
        "g_subln": f(g_subln)[0].reshape(128, 1), "hycw": hycw,
        "fw1": f(hy_f_w1)[0], "fw2": f(hy_f_w2)[0], "fw3": f(hy_f_w3)[0], "fw4": f(hy_f_w4)[0],
        "fvec": np.ascontiguousarray(np.stack([f(hy_f_b1)[0], f(hy_f_b2)[0], f(hy_f_b3)[0], f(hy_freq)[0]], 1)),
        "hyd": np.ascontiguousarray(f(hy_d)[0].reshape(8, 128).T),
        "w_attn_branch": f(w_attn_branch)[0], "w_hyena_branch": f(w_hyena_branch)[0], "w_out": f(w_out)[0],
        "g_ffn": f(g_ffn), "w_up": np.ascontiguousarray(f(w_up)[0].reshape(D, 2, 44, 128).transpose(0, 2, 1, 3).reshape(D, 2 * DFF)), "ffcw": ffcw, "w_down": f(w_down)[0],
        "g_final": f(g_final).reshape(1, D),
    }
    for k in ("c_embT", "c_delta", "c_negt", "c_ffwd", "c_ident"):
        common[k] = C[k]
    xs = f(x)
    in_maps = []
    for c in range(NCORES):
        b, j = divmod(c, 4)
        m = dict(common)
        m["x"] = xs[b]
        xm = np.zeros((LQ, D), np.float32)
        lo, hi = 1024 * j - 2, 1024 * j - 2 + NOUT + 4
        slo, shi = max(lo, 0), min(hi, L)
        xm[slo - lo:shi - lo] = xs[b, slo:shi]
        m["x_my"] = xm
        m["mask_my"] = C["masks"][j]
        m["wt_bias"] = wt_biases[j]
        m["c_finv"] = C["finvs"][j]
        in_maps.append(m)
    if "nc" not in _NC:
        _NC["nc"] = build_program()
    res = run_bass_kernel_spmd(_NC["nc"], in_maps, core_ids=list(range(NCORES)))
    out = np.empty((2, L, D), np.float32)
    for c in range(NCORES):
        b, j = divmod(c, 4)
        out[b, 1024 * j:1024 * (j + 1)] = np.asarray(res.results[c]["out"], dtype=np.float32)
    return out
```

```python
import math
from contextlib import ExitStack
import numpy as np
import ml_dtypes
import concourse.bass as bass
import concourse.mybir as mybir
from concourse.bass_utils import run_bass_kernel_spmd

F32 = mybir.dt.float32
BF = mybir.dt.bfloat16
AF = mybir.ActivationFunctionType
ALU = mybir.AluOpType
AX = mybir.AxisListType

D = 2048
L = 4096
NH = 8
DFF = 5632
CH = 1024
NCORES = 8
LQ = 1152
TQ = 384
NOUT = 1024
LAMBDA_INIT = 0.8 - 0.6 * math.exp(0.0)
WT_M0 = 3968
WT_LEN = 5120


class Buf:
    __slots__ = ("w", "r", "name")

    def __init__(self, name=""):
        self.w = {}
        self.r = {}
        self.name = name


class Sched:
    ENG = ("pe", "act", "dve", "pool", "sp")

    def __init__(self, nc, stack, n_dma_sems=12):
        self.nc = nc
        self.ops = {e: [] for e in self.ENG}
        self.esem = {e: stack.enter_context(nc.semaphore("s_" + e)) for e in self.ENG}
        self.ecnt = {e: 0 for e in self.ENG}
        self.dsem = {}
        self.dcnt = {}
        self.dnext = {}
        for q in ("sp", "pool"):
            self.dsem[q] = [stack.enter_context(nc.semaphore("d_%s%d" % (q, i))) for i in range(n_dma_sems)]
            self.dcnt[q] = [0] * n_dma_sems
            self.dnext[q] = 0
        self.waited = {e: {} for e in self.ENG}
        self.allsems = {}

    def _waits(self, eng, deps):
        out = []
        wd = self.waited[eng]
        for key, (sem, val) in deps.items():
            if wd.get(key, 0) < val:
                wd[key] = val
                out.append((sem, val))
        return out

    @staticmethod
    def _merge(dst, src):
        for k, (s, v) in src.items():
            if k not in dst or dst[k][1] < v:
                dst[k] = (s, v)

    def _deps(self, reads, writes):
        deps = {}
        for b in reads:
            self._merge(deps, b.w)
        for b in writes:
            self._merge(deps, b.w)
            self._merge(deps, b.r)
        return deps

    def _mark(self, reads, writes, key, tok):
        for b in writes:
            b.w[key] = tok
        for b in reads:
            b.r[key] = tok
        self.allsems[key] = tok

    def op(self, eng, fn, reads=(), writes=()):
        deps = self._deps(reads, writes)
        if eng == "pe":
            deps.pop("e_pe", None)
        waits = self._waits(eng, deps)
        sem = self.esem[eng]
        self.ecnt[eng] += 1
        val = self.ecnt[eng]

        def run(e, waits=waits, fn=fn, sem=sem):
            for s, v in waits:
                e.wait_ge(s, v)
            fn(e).then_inc(sem, 1)

        self.ops[eng].append(run)
        key = "e_" + eng
        self._mark(reads, writes, key, (sem, val))

    def dma(self, q, out, in_, reads=(), writes=(), slow=False):
        deps = self._deps(reads, writes)
        i = self.dnext[q]
        self.dnext[q] = (i + 1) % len(self.dsem[q])
        sem = self.dsem[q][i]
        key = "d_%s%d" % (q, i)
        if self.dcnt[q][i] > 0:
            self._merge(deps, {key: (sem, self.dcnt[q][i])})
        waits = self._waits(q, deps)
        self.dcnt[q][i] += 16
        val = self.dcnt[q][i]

        def run(e, waits=waits, out=out, in_=in_, sem=sem, slow=slow):
            for s, v in waits:
                e.wait_ge(s, v)
            if slow:
                e.dma_start(out=out, in_=in_, allow_slow_non_contiguous=True).then_inc(sem, 16)
            else:
                e.dma_start(out=out, in_=in_).then_inc(sem, 16)

        self.ops[q].append(run)
        self._mark(reads, writes, key, (sem, val))

    def barrier(self, engs=None):
        for eng in (engs or self.ENG):
            waits = self._waits(eng, dict(self.allsems))
            if waits:
                def run(e, waits=waits):
                    for s, v in waits:
                        e.wait_ge(s, v)
                self.ops[eng].append(run)


class Arena:
    def __init__(self, tensor, n):
        self.t = tensor
        self.n = n
        self.off = 0

    def reset(self):
        self.off = 0

    def take(self, n, pattern=None, **kw):
        n_al = (n + 15) // 16 * 16
        assert self.off + n_al <= self.n, ("arena overflow", self.off, n, self.n)
        ap = self.t[:, self.off:self.off + n]
        self.off += n_al
        if pattern:
            ap = ap.rearrange(pattern, **kw)
        return ap


def dram_bc(ap1d_tensor, offset, n, parts=128):
    return bass.AP(ap1d_tensor, offset, [[0, parts], [1, n]])


def build_program():
    nc = bass.Bass("TRN2", target_bir_lowering=False)
    st = ExitStack()

    def din(name, shape, dt=F32):
        return nc.dram_tensor(name, list(shape), dt, kind="ExternalInput")

    def dscr(name, shape, dt):
        return nc.dram_tensor(name, list(shape), dt)

    x_t = din("x", [L, D])
    xmy_t = din("x_my", [LQ, D])
    mask_t = din("mask_my", [128, LQ // 128])
    g_mix_t = din("g_mix", [1, D])
    w_in_t = din("w_in", [D, 10240])
    lam_t = din("lam4", [4, 64])
    g_subln_t = din("g_subln", [128, 1])
    wt_t = din("wt_bias", [NH, 128, WT_LEN])
    hycw_t = din("hycw", [128, 24, 4])
    fw1_t = din("fw1", [33, 64])
    fw2_t = din("fw2", [64, 64])
    fw3_t = din("fw3", [64, 64])
    fw4_t = din("fw4", [64, 2048])
    fvec_t = din("fvec", [64, 4])
    hyd_t = din("hyd", [128, 8])
    wa_t = din("w_attn_branch", [1024, D])
    wh_t = din("w_hyena_branch", [1024, D])
    wo_t = din("w_out", [D, D])
    g_ffn_t = din("g_ffn", [1, D])
    wup_t = din("w_up", [D, 2 * DFF])
    ffcw_t = din("ffcw", [128, 88, 4])
    wdn_t = din("w_down", [DFF, D])
    g_fin_t = din("g_final", [1, D])
    embT_t = din("c_embT", [33, L])
    delta_t = din("c_delta", [1, CH])
    negt_t = din("c_negt", [128, 32])
    ffwd_t = din("c_ffwd", [64, 128, 32 * 128], BF)
    finv_t = din("c_finv", [LQ // TQ, 128, 64 * TQ], BF)
    ident_t = din("c_ident", [128, 128], BF)
    out_t = nc.dram_tensor("out", [NOUT, D], F32, kind="ExternalOutput")

    hT_d = dscr("hT_d", [16, 128, L], BF)
    qT_d = dscr("qT_d", [8, 128, LQ], BF)
    hTm_d = dscr("hTm_d", [16, 128, LQ], BF)
    kT_d = dscr("kT_d", [8, 128, L], BF)
    v_d = dscr("v_d", [L, 1024], BF)
    hyraw_d = dscr("hyraw_d", [16, 128, L + 2], F32)
    hyrawm_d = dscr("hyrawm_d", [24, 128, LQ + 2], F32)
    gsig_d = dscr("gsig_d", [32, 128, LQ], F32)
    x0c_d = dscr("x0c_d", [8, 128, LQ], F32)
    uT_d = dscr("uT_d", [8, 128, LQ], F32)
    sig_d = dscr("sig_d", [L, 3072], BF)
    spec_d = dscr("spec_d", [4, 32, 128, CH], F32)
    Y_d = dscr("Y_d", [8, 128, 64, 128], BF)
    yhyT_d = dscr("yhyT_d", [8, 128, LQ], BF)
    attT_d = dscr("attT_d", [8, 128, LQ], BF)
    mrgT_d = dscr("mrgT_d", [16, 128, LQ], BF)
    xmid_d = dscr("xmid_d", [LQ, D], F32)
    hfT_d = dscr("hfT_d", [16, 128, LQ], BF)
    upraw_d = dscr("upraw_d", [88, 128, LQ + 2], F32)
    actT_d = dscr("actT_d", [LQ // 128, 128, 44, 128], BF)

    NBF = 57344
    NF = 16384
    a_bf_t = st.enter_context(nc.sbuf_tensor("a_bf", [128, NBF], BF))
    a_f_t = st.enter_context(nc.sbuf_tensor("a_f", [128, NF], F32))
    ident = st.enter_context(nc.sbuf_tensor("ident", [128, 128], BF))
    ones = st.enter_context(nc.sbuf_tensor("ones", [128, 128], BF))
    cst = st.enter_context(nc.sbuf_tensor("cst", [128, 16], F32))
    PS = [st.enter_context(nc.psum_tensor("ps%d" % i, [128, 512], F32)) for i in range(7)]
    PSB = st.enter_context(nc.psum_tensor("psb", [128, 1024], BF))
    S = Sched(nc, st)
    ABF = Arena(a_bf_t, NBF)
    AFF = Arena(a_f_t, NF)
    b_ps = [Buf("ps%d" % i) for i in range(7)]
    b_psb = Buf("psb")
    b_const = Buf("const")

    def new_stage():
        S.barrier()
        ABF.reset()
        AFF.reset()

    S.op("dve", lambda e: e.memset(ones[:, :], 1.0), writes=[b_const])
    S.op("dve", lambda e: e.memset(cst[:, 0:1], 1e-6), writes=[b_const])
    S.op("dve", lambda e: e.memset(cst[:, 1:2], 1e-5), writes=[b_const])
    S.op("dve", lambda e: e.memset(cst[:, 2:3], -math.pi), writes=[b_const])
    S.op("dve", lambda e: e.memset(cst[:, 3:4], 0.0), writes=[b_const])
    S.dma("sp", ident[:, :], ident_t.ap(), writes=[b_const])

    def norm_T(src_t, g_t, dst_t, dst_buf, src_buf, tok_out=None, nrows=L, row_off=0, mask=None):
        new_stage()
        gbc = AFF.take(D)
        b_g = Buf()
        S.dma("sp", gbc, dram_bc(g_t, 0, D), writes=[b_g])
        xt = [AFF.take(D) for _ in range(2)]
        b_xt = [Buf() for _ in range(2)]
        junk = ABF.take(D)
        b_junk = Buf()
        st_ = [AFF.take(4) for _ in range(2)]
        b_st = [Buf() for _ in range(2)]
        G = 4 if nrows % 512 == 0 else 3
        mk = None
        if mask is not None:
            mk = AFF.take(16)
            S.dma("sp", mk[:, 0:LQ // 128], mask.ap(), writes=[b_g])
        if tok_out is None:
            hb = [ABF.take(D) for _ in range(2)]
            hTt = [ABF.take(16 * G * 128, "p (k t) -> p k t", t=G * 128) for _ in range(2)]
            b_hT = [Buf() for _ in range(2)]
        else:
            hb = [AFF.take(D) for _ in range(2)]
        b_hb = [Buf() for _ in range(2)]
        src = src_t.ap()
        for i in range(nrows // 128):
            s = i % 2
            S.dma("sp", xt[s], src[row_off + i * 128:row_off + (i + 1) * 128, :], reads=[src_buf], writes=[b_xt[s]])
            S.op("act", lambda e, s=s: e.activation(out=junk, in_=xt[s], func=AF.Square, accum_out=st_[s][:, 0:1]),
                 reads=[b_xt[s]], writes=[b_junk, b_st[s]])
            S.op("act", lambda e, s=s: e.activation(out=st_[s][:, 1:2], in_=st_[s][:, 0:1], func=AF.Sqrt,
                                                    bias=cst[:, 0:1], scale=1.0 / D),
                 reads=[b_st[s], b_const], writes=[b_st[s]])
            S.op("dve", lambda e, s=s: e.reciprocal(out=st_[s][:, 2:3], in_=st_[s][:, 1:2]),
                 reads=[b_st[s]], writes=[b_st[s]])
            if mk is not None:
                S.op("dve", lambda e, s=s, i=i: e.tensor_tensor(out=st_[s][:, 2:3], in0=st_[s][:, 2:3], in1=mk[:, i:i + 1], op=ALU.mult),
                     reads=[b_st[s], b_g], writes=[b_st[s]])
            S.op("dve", lambda e, s=s: e.scalar_tensor_tensor(out=hb[s], in0=xt[s], scalar=st_[s][:, 2:3], in1=gbc,
                                                              op0=ALU.mult, op1=ALU.mult),
                 reads=[b_xt[s], b_st[s], b_g], writes=[b_hb[s]])
            if tok_out is not None:
                S.dma("sp", tok_out.ap()[i * 128:(i + 1) * 128, :], hb[s], reads=[b_hb[s]], writes=[dst_buf])
                continue
            g4 = i // G
            hs = g4 % 2
            tb = i % G
            for half in range(2):
                for kk in range(8):
                    k = half * 8 + kk
                    S.op("pe", lambda e, s=s, k=k, kk=kk: e.transpose(out=PSB[:, kk * 128:(kk + 1) * 128],
                                                                     in_=hb[s][:, k * 128:(k + 1) * 128],
                                                                     identity=ident[:, :]),
                         reads=[b_hb[s], b_const], writes=[b_psb])
                S.op("act", lambda e, hs=hs, half=half, tb=tb: e.activation(
                    out=hTt[hs][:, half * 8:(half + 1) * 8, tb * 128:(tb + 1) * 128],
                    in_=PSB[:, :].rearrange("p (k t) -> p k t", t=128), func=AF.Copy),
                    reads=[b_psb], writes=[b_hT[hs]])
            if tb == G - 1:
                S.dma("sp", dst_t.ap()[:, :, g4 * G * 128:(g4 + 1) * G * 128].rearrange("k p t -> p k t"), hTt[hs],
                      reads=[b_hT[hs]], writes=[dst_buf])

    lin_cache = {}

    def linear(srcs, ncols, mode, evac, cgw=512, TT=512, at_slots=2, prologue=None, T=L, resident=False, chain=False, at_ap=None):
        KCs = [s_[4] for s_ in srcs]
        if resident:
            at_slots = T // TT
        key = (tuple(KCs), cgw, TT, at_slots, T, resident, mode, tuple(s_[0].name for s_ in srcs))
        already_resident = False
        if chain and key in lin_cache:
            wts, b_wt, ats, b_at = lin_cache[key]
            already_resident = resident
            if prologue:
                prologue()
        else:
            new_stage()
            lin_cache.clear()
            if prologue:
                prologue()
            wts = [[ABF.take(kc * cgw, "p (k c) -> p k c", c=cgw) for _ in range(2)] for kc in KCs]
            b_wt = [[Buf() for _ in range(2)] for _ in KCs]
            ats = [[ABF.take(kc * TT, "p (k t) -> p k t", t=TT) for _ in range(at_slots)] for kc in KCs]
            b_at = [[Buf() for _ in range(at_slots)] for _ in KCs]
            lin_cache[key] = (wts, b_wt, ats, b_at)
        ncg = ncols // cgw
        ntt = T // TT
        pi = 0
        it = 0
        def load_w(cg):
            ws = cg % 2
            for si, (a_t, a_buf, w_t, c0, kc) in enumerate(srcs):
                wv = w_t.ap()[:, c0 + cg * cgw:c0 + (cg + 1) * cgw].rearrange("(k p) c -> p k c", p=128)
                half = (kc + 1) // 2
                S.dma("pool", wts[si][ws][:, 0:half, :], wv[:, 0:half, :], writes=[b_wt[si][ws]])
                if half < kc:
                    S.dma("pool", wts[si][ws][:, half:kc, :], wv[:, half:kc, :], writes=[b_wt[si][ws]])

        def load_at(n):
            tt_ = n % ntt
            sl = n % at_slots
            for si, (a_t, a_buf, w_t, c0, kc) in enumerate(srcs):
                src_ap = at_ap(si, tt_) if at_ap else a_t.ap()[:, :, tt_ * TT:(tt_ + 1) * TT].rearrange("k p t -> p k t")
                S.dma("sp", ats[si][sl], src_ap, reads=[a_buf], writes=[b_at[si][sl]])

        load_w(0)
        for cg in range(ncg):
            ws = cg % 2
            if cg + 1 < ncg:
                load_w(cg + 1)
            for tt in range(ntt):
                as_ = it % at_slots
                if resident:
                    if it == 0 and not already_resident:
                        for n_ in range(ntt):
                            load_at(n_)
                else:
                    if it == 0:
                        load_at(0)
                    if at_slots > 1 and it + 1 < ncg * ntt:
                        load_at(it + 1)
                    elif at_slots == 1 and it > 0:
                        load_at(it)
                it += 1
                if mode == "fm":
                    for cc in range(cgw // 128):
                        ps = PS[pi % 4]
                        pb = b_ps[pi % 4]
                        pi += 1
                        n_mm = sum(KCs)
                        j = 0
                        for si, kc in enumerate(KCs):
                            for k in range(kc):
                                S.op("pe", lambda e, si=si, ws=ws, as_=as_, k=k, cc=cc, j=j, n_mm=n_mm, ps=ps:
                                     e.matmul(ps[:, 0:TT], lhsT=wts[si][ws][:, k, cc * 128:(cc + 1) * 128],
                                              rhs=ats[si][as_][:, k, :], start=(j == 0), stop=(j == n_mm - 1)),
                                     reads=[b_wt[si][ws], b_at[si][as_]], writes=[pb])
                                j += 1
                        evac(cg * (cgw // 128) + cc, tt, ps[:, 0:TT], pb)
                else:
                    for tb in range(TT // 128):
                        ps = PS[pi % 4]
                        pb = b_ps[pi % 4]
                        pi += 1
                        n_mm = sum(KCs)
                        j = 0
                        for si, kc in enumerate(KCs):
                            for k in range(kc):
                                S.op("pe", lambda e, si=si, ws=ws, as_=as_, k=k, tb=tb, j=j, n_mm=n_mm, ps=ps:
                                     e.matmul(ps[:, 0:cgw], lhsT=ats[si][as_][:, k, tb * 128:(tb + 1) * 128],
                                              rhs=wts[si][ws][:, k, :], start=(j == 0), stop=(j == n_mm - 1)),
                                     reads=[b_wt[si][ws], b_at[si][as_]], writes=[pb])
                                j += 1
                        evac(cg, tt * (TT // 128) + tb, ps[:, 0:cgw], pb)

    b_x = Buf("x")
    b_hT = Buf("hT")
    norm_T(x_t, g_mix_t, hT_d, b_hT, b_x)
    b_xmy = Buf("xmy")
    b_hTm = Buf("hTm")
    norm_T(xmy_t, g_mix_t, hTm_d, b_hTm, b_xmy, nrows=LQ)

    b_q, b_k, b_v, b_hyraw, b_gsig = Buf("q"), Buf("k"), Buf("v"), Buf("hyraw"), Buf("gsig")
    ev = {}

    def mk_out_slots(n, dt_arena, width):
        tiles = [dt_arena.take(width) for _ in range(n)]
        bufs = [Buf() for _ in range(n)]
        return tiles, bufs

    def qk_stage(c0, src_t, src_buf, dst_t, dst_buf, scale, T, TT, chain=False, cgw=512):
        cnt = [0]
        slots = {}

        def pro():
            slots["t"], slots["b"] = mk_out_slots(4, ABF, 512)

        def evac(ci, ti, ps, pb):
            s = cnt[0] % 4
            cnt[0] += 1
            o, ob = slots["t"][s], slots["b"][s]
            S.op("act", lambda e: e.activation(out=o[:, 0:TT], in_=ps, func=AF.Copy, scale=scale), reads=[pb], writes=[ob])
            S.dma("sp", dst_t.ap()[ci, :, ti * TT:(ti + 1) * TT], o[:, 0:TT], reads=[ob], writes=[dst_buf])

        linear([(src_t, src_buf, w_in_t, c0, 16)], 1024, "fm", evac, prologue=pro, T=T, TT=TT, resident=(T == LQ), chain=chain, cgw=cgw)


    def v_stage():
        cnt = [0]
        slots = {}

        def pro():
            slots["t"], slots["b"] = mk_out_slots(4, ABF, 512)

        def evac(cg, tb, ps, pb):
            s = cnt[0] % 4
            cnt[0] += 1
            o, ob = slots["t"][s], slots["b"][s]
            S.op("act", lambda e: e.activation(out=o, in_=ps, func=AF.Copy), reads=[pb], writes=[ob])
            S.dma("sp", v_d.ap()[tb * 128:(tb + 1) * 128, cg * 512:(cg + 1) * 512], o, reads=[ob], writes=[b_v])

        linear([(hT_d, b_hT, w_in_t, 2048, 16)], 1024, "tm", evac, prologue=pro)


    def raw_stage(src_t, src_buf, w_t, c0, ncols, dst_t, dst_buf, func=AF.Copy, pad=1, T=L, TT=512, chain=False, cgw=512):
        cnt = [0]
        slots = {}

        def pro():
            slots["t"], slots["b"] = mk_out_slots(4, AFF, 512)
            if pad:
                z = AFF.take(2)
                bz = Buf()
                S.op("dve", lambda e: e.memset(z, 0.0), writes=[bz])
                nchunks = ncols // 128
                for c in range(nchunks):
                    S.dma("sp", dst_t.ap()[c, :, 0:1], z[:, 0:1], reads=[bz], writes=[dst_buf], slow=True)
                    S.dma("sp", dst_t.ap()[c, :, T + 1:T + 2], z[:, 1:2], reads=[bz], writes=[dst_buf], slow=True)

        def evac(ci, ti, ps, pb):
            s = cnt[0] % 4
            cnt[0] += 1
            o, ob = slots["t"][s], slots["b"][s]
            S.op("act", lambda e: e.activation(out=o[:, 0:TT], in_=ps, func=func), reads=[pb], writes=[ob])
            S.dma("sp", dst_t.ap()[ci, :, pad + ti * TT:pad + (ti + 1) * TT], o[:, 0:TT], reads=[ob], writes=[dst_buf])

        linear([(src_t, src_buf, w_t, c0, 16)], ncols, "fm", evac, prologue=pro, T=T, TT=TT, resident=(T == LQ), chain=chain, cgw=cgw)

    b_hyrawm = Buf("hyrawm")
    qk_stage(1024, hT_d, b_hT, kT_d, b_k, 1.0, L, 512, cgw=1024)
    raw_stage(hT_d, b_hT, w_in_t, 4096, 2048, hyraw_d, b_hyraw, chain=True, cgw=1024)
    v_stage()
    qk_stage(0, hTm_d, b_hTm, qT_d, b_q, 0.125, LQ, TQ)
    raw_stage(hTm_d, b_hTm, w_in_t, 3072, 3072, hyrawm_d, b_hyrawm, T=LQ, TT=TQ, chain=True)
    raw_stage(hTm_d, b_hTm, w_in_t, 6144, 4096, gsig_d, b_gsig, func=AF.Sigmoid, pad=0, T=LQ, TT=TQ, chain=True)

    b_x0c, b_uT, b_sig = Buf("x0c"), Buf("uT"), Buf("sig")
    new_stage()
    cw = AFF.take(24 * 4, "p (c j) -> p c j", j=4)
    b_cw = Buf()
    S.dma("sp", cw, hycw_t.ap(), writes=[b_cw])

    def conv3(eng, dst, src, wts_, c, b_src, b_dst, n=512):
        dst = dst[:, 0:n]
        S.op(eng, lambda e: e.tensor_scalar(out=dst, in0=src[:, 0:n], scalar1=wts_[:, c, 0:1], scalar2=wts_[:, c, 3:4],
                                            op0=ALU.mult, op1=ALU.add), reads=[b_src, b_cw], writes=[b_dst])
        S.op(eng, lambda e: e.scalar_tensor_tensor(out=dst, in0=src[:, 1:n + 1], scalar=wts_[:, c, 1:2], in1=dst,
                                                   op0=ALU.mult, op1=ALU.add), reads=[b_src, b_cw, b_dst], writes=[b_dst])
        S.op(eng, lambda e: e.scalar_tensor_tensor(out=dst, in0=src[:, 2:n + 2], scalar=wts_[:, c, 2:3], in1=dst,
                                                   op0=ALU.mult, op1=ALU.add), reads=[b_src, b_cw, b_dst], writes=[b_dst])

    raw = [[AFF.take(514) for _ in range(3)] for _ in range(2)]
    b_raw = [[Buf() for _ in range(3)] for _ in range(2)]
    cv = [[AFF.take(512) for _ in range(3)] for _ in range(2)]
    b_cv = [[Buf() for _ in range(3)] for _ in range(2)]
    ubf = [ABF.take(512) for _ in range(2)]
    b_ubf = [Buf() for _ in range(2)]
    utm = [ABF.take(4 * 1024, "p (b c) -> p b c", c=1024) for _ in range(2)]
    b_utm = [Buf() for _ in range(2)]
    it = 0
    for tt in range(8):
        us = tt % 2
        for c in range(8):
            s = it % 2
            it += 1
            for j in (1, 2):
                S.dma("sp", raw[s][j], hyraw_d.ap()[(j - 1) * 8 + c, :, tt * 512:tt * 512 + 514],
                      reads=[b_hyraw], writes=[b_raw[s][j]])
            conv3("dve", cv[s][1], raw[s][1], cw, 8 + c, b_raw[s][1], b_cv[s][1])
            conv3("dve", cv[s][2], raw[s][2], cw, 16 + c, b_raw[s][2], b_cv[s][2])
            S.op("pool", lambda e, s=s: e.tensor_tensor(out=ubf[s], in0=cv[s][1], in1=cv[s][2], op=ALU.mult),
                 reads=[b_cv[s][1], b_cv[s][2]], writes=[b_ubf[s]])
            for tb in range(4):
                S.op("pe", lambda e, s=s, tb=tb: e.transpose(out=PSB[:, tb * 128:(tb + 1) * 128],
                                                            in_=ubf[s][:, tb * 128:(tb + 1) * 128], identity=ident[:, :]),
                     reads=[b_ubf[s], b_const], writes=[b_psb])
            S.op("act", lambda e, us=us, c=c: e.activation(out=utm[us][:, :, c * 128:(c + 1) * 128],
                                                           in_=PSB[:, 0:512].rearrange("p (b c) -> p b c", c=128),
                                                           func=AF.Copy), reads=[b_psb], writes=[b_utm[us]])
        S.dma("sp", sig_d.ap()[tt * 512:(tt + 1) * 512, 0:1024].rearrange("(b p) c -> p b c", p=128), utm[us],
              reads=[b_utm[us]], writes=[b_sig])
    for tt in range(LQ // TQ):
        for c in range(8):
            s = it % 2
            it += 1
            for j in range(3):
                S.dma("sp", raw[s][j][:, 0:TQ + 2], hyrawm_d.ap()[j * 8 + c, :, tt * TQ:tt * TQ + TQ + 2],
                      reads=[b_hyrawm], writes=[b_raw[s][j]])
            conv3("dve", cv[s][0], raw[s][0], cw, c, b_raw[s][0], b_cv[s][0], n=TQ)
            conv3("dve", cv[s][1], raw[s][1], cw, 8 + c, b_raw[s][1], b_cv[s][1], n=TQ)
            conv3("dve", cv[s][2], raw[s][2], cw, 16 + c, b_raw[s][2], b_cv[s][2], n=TQ)
            S.dma("sp", x0c_d.ap()[c, :, tt * TQ:(tt + 1) * TQ], cv[s][0][:, 0:TQ], reads=[b_cv[s][0]], writes=[b_x0c])
            S.op("pool", lambda e, s=s: e.tensor_tensor(out=cv[s][1][:, 0:TQ], in0=cv[s][1][:, 0:TQ], in1=cv[s][2][:, 0:TQ], op=ALU.mult),
                 reads=[b_cv[s][1], b_cv[s][2]], writes=[b_cv[s][1]])
            S.dma("sp", uT_d.ap()[c, :, tt * TQ:(tt + 1) * TQ], cv[s][1][:, 0:TQ], reads=[b_cv[s][1]], writes=[b_uT])

    new_stage()
    embT = [AFF.take(512) for _ in range(2)]
    b_emb = [Buf() for _ in range(2)]
    w1 = AFF.take(64)
    w2 = AFF.take(64)
    w3 = AFF.take(64)
    w4 = AFF.take(2048)
    fv = AFF.take(8)
    dl = AFF.take(CH)
    ngt = AFF.take(32)
    b_f = Buf()
    S.dma("sp", w1[0:33, :], fw1_t.ap(), writes=[b_f])
    S.dma("sp", w2[0:64, :], fw2_t.ap(), writes=[b_f])
    S.dma("sp", w3[0:64, :], fw3_t.ap(), writes=[b_f])
    S.dma("sp", w4[0:64, :], fw4_t.ap(), writes=[b_f])
    S.dma("sp", fv[0:64, 0:4], fvec_t.ap(), writes=[b_f])
    S.dma("sp", dl, dram_bc(delta_t, 0, CH), writes=[b_f])
    S.dma("sp", ngt, negt_t.ap(), writes=[b_f])
    for j in range(3):
        S.op("dve", lambda e, j=j: e.tensor_tensor(out=fv[0:64, 4 + j:5 + j], in0=fv[0:64, j:j + 1], in1=fv[0:64, 3:4],
                                                   op=ALU.mult), reads=[b_f], writes=[b_f])
    hcur = [AFF.take(512) for _ in range(2)]
    b_h = [Buf() for _ in range(2)]
    H3 = AFF.take(L)
    b_H3 = Buf()
    arg = AFF.take(512)
    b_arg = Buf()
    sA = AFF.take(512)
    sB = AFF.take(512)
    b_sA, b_sB = Buf(), Buf()
    for pt in range(8):
        S.dma("sp", embT[pt % 2][0:33, :], embT_t.ap()[:, pt * 512:(pt + 1) * 512], writes=[b_emb[pt % 2]])
        srcs_ = [(embT[pt % 2][0:33, :], w1[0:33, 0:64]), None, None]
        for ly in range(3):
            ps = PS[(pt * 3 + ly) % 4]
            pb = b_ps[(pt * 3 + ly) % 4]
            if ly == 0:
                rhs, lhsT = srcs_[0]
                rb = b_emb[pt % 2]
            else:
                rhs = hcur[(ly - 1) % 2][0:64, :]
                lhsT = (w2 if ly == 1 else w3)[0:64, 0:64]
                rb = b_h[(ly - 1) % 2]
            S.op("pe", lambda e, ps=ps, lhsT=lhsT, rhs=rhs: e.matmul(ps[0:64, :], lhsT=lhsT, rhs=rhs, start=True, stop=True),
                 reads=[b_f, rb], writes=[pb])
            S.op("dve", lambda e, ps=ps, ly=ly: e.tensor_scalar(out=arg[0:64, :], in0=ps[0:64, :], scalar1=fv[0:64, 3:4],
                                                                scalar2=fv[0:64, 4 + ly:5 + ly], op0=ALU.mult, op1=ALU.add),
                 reads=[pb, b_f], writes=[b_arg])
            if ly < 2:
                dst, db = hcur[ly % 2][0:64, :], b_h[ly % 2]
            else:
                dst, db = H3[0:64, pt * 512:(pt + 1) * 512], b_H3
            S.op("act", lambda e: e.activation(out=sA[0:64, :], in_=arg[0:64, :], func=AF.Sin, scale=0.5),
                 reads=[b_arg], writes=[b_sA])
            S.op("act", lambda e: e.activation(out=sB[0:64, :], in_=arg[0:64, :], func=AF.Sin, scale=0.25),
                 reads=[b_arg], writes=[b_sB])
            S.op("dve", lambda e: e.tensor_tensor(out=sB[0:64, :], in0=sB[0:64, :], in1=sB[0:64, :], op=ALU.mult),
                 reads=[b_sB], writes=[b_sB])
            S.op("dve", lambda e: e.tensor_scalar(out=sB[0:64, :], in0=sB[0:64, :], scalar1=-4.0, scalar2=2.0,
                                                  op0=ALU.mult, op1=ALU.add), reads=[b_sB], writes=[b_sB])
            S.op("dve", lambda e, dst=dst: e.tensor_tensor(out=dst, in0=sA[0:64, :], in1=sB[0:64, :], op=ALU.mult),
                 reads=[b_sA, b_sB], writes=[db])
    dec = [AFF.take(CH) for _ in range(2)]
    b_dec = [Buf() for _ in range(2)]
    hfb = [AFF.take(2048)] * 2
    b_hfb = [Buf()] * 2
    hpm = [ABF.take(2048) for _ in range(2)]
    b_hpm = [Buf() for _ in range(2)]
    for pc in range(32):
        s = pc % 2
        for ct in range(4):
            S.op("pe", lambda e, pc=pc, ct=ct: e.matmul(PS[ct][:, :], lhsT=H3[0:64, pc * 128:(pc + 1) * 128],
                                                        rhs=w4[0:64, ct * 512:(ct + 1) * 512], start=True, stop=True),
                 reads=[b_H3, b_f], writes=[b_ps[ct]])
        S.op("act", lambda e, s=s, pc=pc: e.activation(out=dec[s], in_=dl, func=AF.Exp, scale=ngt[:, pc:pc + 1]),
             reads=[b_f], writes=[b_dec[s]])
        for ct in range(4):
            S.op("dve", lambda e, s=s, ct=ct: e.tensor_tensor(out=hfb[s][:, ct * 512:(ct + 1) * 512], in0=PS[ct][:, :],
                                                              in1=dec[s][:, (ct % 2) * 512:(ct % 2 + 1) * 512], op=ALU.mult),
                 reads=[b_ps[ct], b_dec[s]], writes=[b_hfb[s]])
        S.op("pool", lambda e, s=s: e.tensor_tensor(out=hpm[s][:, 0:1024], in0=hfb[s][:, 0:1024], in1=hfb[s][:, 1024:2048],
                                                    op=ALU.add), reads=[b_hfb[s]], writes=[b_hpm[s]])
        S.op("pool", lambda e, s=s: e.tensor_tensor(out=hpm[s][:, 1024:2048], in0=hfb[s][:, 0:1024], in1=hfb[s][:, 1024:2048],
                                                    op=ALU.subtract), reads=[b_hfb[s]], writes=[b_hpm[s]])
        S.dma("sp", sig_d.ap()[pc * 128:(pc + 1) * 128, 1024:3072], hpm[s], reads=[b_hpm[s]], writes=[b_sig])

    b_spec = Buf("spec")
    b_Y = Buf("Y")
    new_stage()
    sigt = [ABF.take(32 * 512, "p (s c) -> p s c", c=512) for _ in range(2)]
    b_sigt = [Buf() for _ in range(2)]
    ft = [ABF.take(32 * 128, "p (s g) -> p s g", g=128) for _ in range(4)]
    b_ft = [Buf() for _ in range(4)]
    so = [AFF.take(512) for _ in range(4)]
    b_so = [Buf() for _ in range(4)]
    cts = (4, 5, 2, 3, 0, 1)

    def fcs_of(ct):
        if ct < 2:
            return list(range(64))
        if ct < 4:
            return list(range(32)) + [32]
        return list(range(32, 64))

    def load_sig(cti):
        ct = cts[cti]
        S.dma("sp", sigt[cti % 2], sig_d.ap()[:, ct * 512:(ct + 1) * 512].rearrange("(s p) c -> p s c", p=128),
              reads=[b_sig], writes=[b_sigt[cti % 2]])

    itsK = [(cti, ct, fc) for cti, ct in enumerate(cts[:4]) for fc in fcs_of(ct)]

    def load_ftK(n):
        S.dma("sp", ft[n % 3], ffwd_t.ap()[itsK[n][2]].rearrange("p (s g) -> p s g", g=128), writes=[b_ft[n % 3]])

    load_sig(0)
    load_ftK(0)
    load_ftK(1)
    for n, (cti, ct, fc) in enumerate(itsK):
        ss_ = cti % 2
        fs = n % 3
        if n + 2 < len(itsK):
            load_ftK(n + 2)
        if fc == fcs_of(ct)[0]:
            load_sig(cti + 1)
        ps = PS[n % 4]
        pb = b_ps[n % 4]
        for sc in range(32):
            S.op("pe", lambda e, ps=ps, fs=fs, sc=sc, ss_=ss_: e.matmul(ps[:, :], lhsT=ft[fs][:, sc, :], rhs=sigt[ss_][:, sc, :],
                                                                       start=(sc == 0), stop=(sc == 31)),
                 reads=[b_ft[fs], b_sigt[ss_]], writes=[pb])
        o, ob = so[n % 4], b_so[n % 4]
        S.op("act", lambda e, o=o, ps=ps: e.activation(out=o, in_=ps[:, :], func=AF.Copy), reads=[pb], writes=[ob])
        if ct in (2, 3) and fc == 32:
            S.dma("sp", spec_d.ap()[3, 0, 0:1, (ct % 2) * 512:(ct % 2 + 1) * 512], o[0:1, :], reads=[ob], writes=[b_spec])
        else:
            S.dma("sp", spec_d.ap()[2 if ct < 4 else 3, fc % 32, :, (ct % 2) * 512:(ct % 2 + 1) * 512], o, reads=[ob], writes=[b_spec])

    kin = [[AFF.take(512) for _ in range(2)] for _ in range(3)]
    b_kin = [[Buf() for _ in range(2)] for _ in range(3)]
    tq = [[AFF.take(512) for _ in range(4)] for _ in range(2)]
    b_tq = [[Buf() for _ in range(4)] for _ in range(2)]
    yo = [[ABF.take(512) for _ in range(2)] for _ in range(2)]
    b_yo = [[Buf() for _ in range(2)] for _ in range(2)]
    itsU = [(cti, ct, gc) for cti, ct in ((4, 0), (5, 1)) for gc in range(32)]

    def load_U(n):
        cti, ct, gc = itsU[n]
        for j, fc in enumerate((gc, 32 + gc)):
            sl = (n % 2) * 2 + j
            S.dma("sp", ft[sl], ffwd_t.ap()[fc].rearrange("p (s g) -> p s g", g=128), writes=[b_ft[sl]])
        for w_ in range(2):
            S.dma("sp", kin[n % 3][w_], spec_d.ap()[2 + w_, gc, :, ct * 512:(ct + 1) * 512], reads=[b_spec], writes=[b_kin[n % 3][w_]])

    load_U(0)
    for n, (cti, ct, gc) in enumerate(itsU):
        ss_ = cti % 2
        if n + 1 < len(itsU):
            load_U(n + 1)
        if gc == 0 and ct == 0:
            load_sig(5)
        pp = n % 2
        for j in range(2):
            ps, pb = PS[2 * pp + j], b_ps[2 * pp + j]
            sl = pp * 2 + j
            for sc in range(32):
                S.op("pe", lambda e, ps=ps, sl=sl, sc=sc, ss_=ss_: e.matmul(ps[:, :], lhsT=ft[sl][:, sc, :], rhs=sigt[ss_][:, sc, :],
                                                                           start=(sc == 0), stop=(sc == 31)),
                     reads=[b_ft[sl], b_sigt[ss_]], writes=[pb])
        pUc, bUc = PS[2 * pp], b_ps[2 * pp]
        pUs, bUs = PS[2 * pp + 1], b_ps[2 * pp + 1]
        Kc, Ks = kin[n % 3]
        bKc, bKs = b_kin[n % 3]
        t = tq[pp]
        bt = b_tq[pp]
        S.op("dve", lambda e, t=t, pUc=pUc, Kc=Kc: e.tensor_tensor(out=t[0], in0=pUc[:, :], in1=Kc, op=ALU.mult), reads=[bUc, bKc], writes=[bt[0]])
        S.op("dve", lambda e, t=t, pUs=pUs, Ks=Ks: e.tensor_tensor(out=t[1], in0=pUs[:, :], in1=Ks, op=ALU.mult), reads=[bUs, bKs], writes=[bt[1]])
        S.op("dve", lambda e, t=t, pUc=pUc, Ks=Ks: e.tensor_tensor(out=t[2], in0=pUc[:, :], in1=Ks, op=ALU.mult), reads=[bUc, bKs], writes=[bt[2]])
        S.op("dve", lambda e, t=t, pUs=pUs, Kc=Kc: e.tensor_tensor(out=t[3], in0=pUs[:, :], in1=Kc, op=ALU.mult), reads=[bUs, bKc], writes=[bt[3]])
        S.op("pool", lambda e, t=t, pp=pp: e.tensor_tensor(out=yo[pp][0], in0=t[0], in1=t[1], op=ALU.subtract),
             reads=[bt[0], bt[1]], writes=[b_yo[pp][0]])
        S.op("pool", lambda e, t=t, pp=pp: e.tensor_tensor(out=yo[pp][1], in0=t[2], in1=t[3], op=ALU.add),
             reads=[bt[2], bt[3]], writes=[b_yo[pp][1]])
        if gc == 0:
            S.op("pool", lambda e, t=t, pp=pp: e.tensor_copy(out=yo[pp][0][0:1, :], in_=t[0][0:1, :]), reads=[bt[0], b_yo[pp][0]], writes=[b_yo[pp][0]])
            S.op("pool", lambda e, t=t, pp=pp: e.tensor_copy(out=yo[pp][1][0:1, :], in_=t[1][0:1, :]), reads=[bt[1], b_yo[pp][1]], writes=[b_yo[pp][1]])
        S.dma("sp", Y_d.ap()[ct * 4:(ct + 1) * 4, :, gc, :].rearrange("c p e -> p c e"),
              yo[pp][0].rearrange("p (c e) -> p c e", e=128), reads=[b_yo[pp][0]], writes=[b_Y])
        S.dma("sp", Y_d.ap()[ct * 4:(ct + 1) * 4, :, 32 + gc, :].rearrange("c p e -> p c e"),
              yo[pp][1].rearrange("p (c e) -> p c e", e=128), reads=[b_yo[pp][1]], writes=[b_Y])

    b_yhy = Buf("yhy")
    new_stage()
    fv_ = ABF.take(64 * TQ, "p (f t) -> p f t", t=TQ)
    b_fv = Buf()
    yc = [ABF.take(64 * 128, "p (f c) -> p f c", c=128) for _ in range(2)]
    b_yc = [Buf() for _ in range(2)]
    hd = AFF.take(8)
    b_hd = Buf()
    S.dma("sp", hd, hyd_t.ap(), writes=[b_hd])
    xin = [[AFF.take(512) for _ in range(2)] for _ in range(2)]
    b_xin = [[Buf() for _ in range(2)] for _ in range(2)]
    yout = [ABF.take(512) for _ in range(2)]
    b_yout = [Buf() for _ in range(2)]
    it = 0

    def load7(n):
        tt_, c_ = divmod(n, 8)
        sl = n % 2
        S.dma("sp", yc[sl], Y_d.ap()[c_], reads=[b_Y], writes=[b_yc[sl]])
        S.dma("sp", xin[sl][0][:, 0:TQ], uT_d.ap()[c_, :, tt_ * TQ:(tt_ + 1) * TQ], reads=[b_uT], writes=[b_xin[sl][0]])
        S.dma("sp", xin[sl][1][:, 0:TQ], x0c_d.ap()[c_, :, tt_ * TQ:(tt_ + 1) * TQ], reads=[b_x0c], writes=[b_xin[sl][1]])

    for tt in range(LQ // TQ):
        S.dma("sp", fv_, finv_t.ap()[tt].rearrange("p (f t) -> p f t", t=TQ), writes=[b_fv])
        for c in range(8):
            s = it % 2
            if it == 0:
                load7(0)
            if it + 1 < 8 * (LQ // TQ):
                load7(it + 1)
            it += 1
            ps = PS[s]
            pb = b_ps[s]
            for f in range(64):
                S.op("pe", lambda e, ps=ps, s=s, f=f: e.matmul(ps[:, 0:TQ], lhsT=yc[s][:, f, :], rhs=fv_[:, f, :],
                                                               start=(f == 0), stop=(f == 63)),
                     reads=[b_yc[s], b_fv], writes=[pb])
            S.op("dve", lambda e, ps=ps, s=s, c=c: e.scalar_tensor_tensor(out=xin[s][0][:, 0:TQ], in0=xin[s][0][:, 0:TQ], scalar=hd[:, c:c + 1],
                                                                         in1=ps[:, 0:TQ], op0=ALU.mult, op1=ALU.add),
                 reads=[pb, b_xin[s][0], b_hd], writes=[b_xin[s][0]])
            S.op("dve", lambda e, s=s: e.tensor_tensor(out=yout[s][:, 0:TQ], in0=xin[s][0][:, 0:TQ], in1=xin[s][1][:, 0:TQ], op=ALU.mult),
                 reads=[b_xin[s][0], b_xin[s][1]], writes=[b_yout[s]])
            S.dma("sp", yhyT_d.ap()[c, :, tt * TQ:(tt + 1) * TQ], yout[s][:, 0:TQ], reads=[b_yout[s]], writes=[b_yhy])

    b_att = Buf("att")
    new_stage()
    lam = AFF.take(64 * 4 + 8)
    b_lam = Buf()
    for j in range(4):
        S.dma("sp", lam[:, j * 64:(j + 1) * 64], dram_bc(lam_t, j * 64, 64), writes=[b_lam])
    gs = AFF.take(2)
    S.dma("sp", gs[:, 0:1], g_subln_t.ap(), writes=[b_lam])
    S.op("dve", lambda e: e.tensor_scalar(out=gs[:, 1:2], in0=gs[:, 0:1], scalar1=1.0 - LAMBDA_INIT, scalar2=None, op0=ALU.mult),
         reads=[b_lam], writes=[b_lam])
    for j in range(2):
        S.op("dve", lambda e, j=j: e.tensor_tensor(out=lam[:, j * 128:j * 128 + 64], in0=lam[:, j * 128:j * 128 + 64],
                                                   in1=lam[:, j * 128 + 64:j * 128 + 128], op=ALU.mult), reads=[b_lam], writes=[b_lam])
        S.op("dve", lambda e, j=j: e.reduce_sum(out=lam[:, 256 + j:257 + j], in_=lam[:, j * 128:j * 128 + 64], axis=AX.X),
             reads=[b_lam], writes=[b_lam])
        S.op("act", lambda e, j=j: e.activation(out=lam[:, 258 + j:259 + j], in_=lam[:, 256 + j:257 + j], func=AF.Exp),
             reads=[b_lam], writes=[b_lam])
    S.op("dve", lambda e: e.tensor_tensor(out=lam[:, 260:261], in0=lam[:, 259:260], in1=lam[:, 258:259], op=ALU.subtract),
         reads=[b_lam], writes=[b_lam])
    S.op("dve", lambda e: e.tensor_scalar(out=lam[:, 260:261], in0=lam[:, 260:261], scalar1=-LAMBDA_INIT, scalar2=None, op0=ALU.add),
         reads=[b_lam], writes=[b_lam])
    neglam = lam[:, 260:261]

    qh = [ABF.take(2 * LQ, "p (m t) -> p m t", m=2) for _ in range(2)]
    kh = [ABF.take(L) for _ in range(2)]
    vh = [ABF.take(32 * 128, "p (k e) -> p k e", e=128) for _ in range(2)]
    wth = [ABF.take(WT_LEN) for _ in range(2)]
    b_hd_ = [Buf() for _ in range(2)]
    E = [ABF.take(512) for _ in range(3)]
    b_E = [Buf() for _ in range(3)]
    atth = [ABF.take(LQ) for _ in range(2)]
    b_atth = [Buf() for _ in range(2)]
    sq = [ABF.take(256) for _ in range(2)]
    b_sq = [Buf() for _ in range(2)]
    rz = [AFF.take(512) for _ in range(2)]
    o12 = [AFF.take(512) for _ in range(2)]
    at_ = [AFF.take(256) for _ in range(2)]
    rs = [AFF.take(256) for _ in range(2)]
    b_ev = [Buf() for _ in range(2)]
    QT = 192
    W2 = 2 * QT

    b_qz = Buf()
    for hs_ in range(2):
        S.op("dve", lambda e, hs_=hs_: e.memset(qh[hs_][0:64, 1, :], 0.0), writes=[b_qz])
        S.op("dve", lambda e, hs_=hs_: e.memset(qh[hs_][64:128, 0, :], 0.0), writes=[b_qz])

    def load_head(h):
        hs = h % 2
        S.dma("sp", qh[hs][0:64, 0, :], qT_d.ap()[h, 0:64, :], reads=[b_q], writes=[b_hd_[hs]])
        S.dma("sp", qh[hs][64:128, 1, :], qT_d.ap()[h, 64:128, :], reads=[b_q], writes=[b_hd_[hs]])
        S.dma("sp", kh[hs], kT_d.ap()[h], reads=[b_k], writes=[b_hd_[hs]])
        S.dma("sp", vh[hs], v_d.ap()[:, h * 128:(h + 1) * 128].rearrange("(k p) e -> p k e", p=128), reads=[b_v], writes=[b_hd_[hs]])
        S.dma("pool", wth[hs], wt_t.ap()[h], writes=[b_hd_[hs]])

    iters = [(h, qt, kc) for h in range(NH) for qt in range(LQ // QT) for kc in range(32)]

    def is_near(qt, kc):
        return True

    def emit_S(i):
        h, qt, kc = iters[i]
        hs = h % 2
        pS, bS = PS[i % 2], b_ps[i % 2]
        near = is_near(qt, kc)
        c0 = WT_M0 - kc * 128 + qt * QT
        assert 0 <= c0 <= WT_LEN - QT or not near
        S.op("pe", lambda e: e.matmul(
            pS[:, 0:W2].rearrange("p (m q) -> p m q", m=2), lhsT=kh[hs][:, kc * 128:(kc + 1) * 128],
            rhs=qh[hs][:, :, qt * QT:(qt + 1) * QT], start=True, stop=(not near)),
            reads=[b_hd_[hs], b_qz], writes=[bS])
        if near:
            for mp in range(2):
                S.op("pe", lambda e, mp=mp: e.matmul(
                    pS[:, mp * QT:(mp + 1) * QT], lhsT=ident[:, :], rhs=wth[hs][:, c0:c0 + QT], start=False, stop=(mp == 1)),
                    reads=[b_hd_[hs], b_const], writes=[bS])

    def emit_post1(h, qt):
        hs = h % 2
        os_ = qt % 2
        pO, bO = PS[2 + os_], b_ps[2 + os_]
        pZ, bZ = PS[4 + os_], b_ps[4 + os_]
        S.op("dve", lambda e: e.reciprocal(out=rz[os_][:, 0:W2], in_=pZ[:, 0:W2]), reads=[bZ], writes=[b_ev[os_]])
        S.op("dve", lambda e: e.tensor_tensor(out=o12[os_][:, 0:W2], in0=pO[:, 0:W2], in1=rz[os_][:, 0:W2], op=ALU.mult),
             reads=[bO, b_ev[os_]], writes=[b_ev[os_]])
        S.op("dve", lambda e: e.scalar_tensor_tensor(out=at_[os_][:, 0:QT], in0=o12[os_][:, QT:2 * QT], scalar=neglam,
                                                     in1=o12[os_][:, 0:QT], op0=ALU.mult, op1=ALU.add),
             reads=[b_ev[os_], b_lam], writes=[b_ev[os_]])
        S.op("dve", lambda e: e.tensor_tensor(out=sq[os_][:, 0:QT], in0=at_[os_][:, 0:QT], in1=at_[os_][:, 0:QT], op=ALU.mult),
             reads=[b_ev[os_]], writes=[b_sq[os_]])

    def emit_post2(h, qt):
        hs = h % 2
        os_ = qt % 2
        S.op("pe", lambda e: e.matmul(PS[6][:, 0:QT], lhsT=ones[:, :], rhs=sq[os_][:, 0:QT], start=True, stop=True),
             reads=[b_const, b_sq[os_]], writes=[b_ps[6]])
        S.op("act", lambda e: e.activation(out=rs[os_][:, 0:QT], in_=PS[6][:, 0:QT], func=AF.Sqrt, bias=cst[:, 1:2], scale=1.0 / 128),
             reads=[b_ps[6], b_const], writes=[b_ev[os_]])
        S.op("dve", lambda e: e.reciprocal(out=rs[os_][:, 0:QT], in_=rs[os_][:, 0:QT]), reads=[b_ev[os_]], writes=[b_ev[os_]])
        S.op("dve", lambda e: e.scalar_tensor_tensor(out=atth[hs][:, qt * QT:(qt + 1) * QT], in0=at_[os_][:, 0:QT],
                                                     scalar=gs[:, 1:2], in1=rs[os_][:, 0:QT], op0=ALU.mult, op1=ALU.mult),
             reads=[b_ev[os_], b_lam], writes=[b_atth[hs]])
        if qt == LQ // QT - 1:
            S.dma("sp", attT_d.ap()[h], atth[hs], reads=[b_atth[hs]], writes=[b_att])

    load_head(0)
    emit_S(0)
    pending = []
    for i, (h, qt, kc) in enumerate(iters):
        hs = h % 2
        if qt == 0 and kc == 0 and h + 1 < NH:
            load_head(h + 1)
        if i + 1 < len(iters):
            emit_S(i + 1)
        os_ = qt % 2
        pS, bS = PS[i % 2], b_ps[i % 2]
        pO, bO = PS[2 + os_], b_ps[2 + os_]
        pZ, bZ = PS[4 + os_], b_ps[4 + os_]
        es = i % 3
        S.op("act", lambda e, es=es, pS=pS: e.activation(out=E[es][:, 0:W2], in_=pS[:, 0:W2], func=AF.Exp), reads=[bS], writes=[b_E[es]])
        S.op("pe", lambda e, pO=pO, hs=hs, kc=kc, es=es: e.matmul(pO[:, 0:W2], lhsT=vh[hs][:, kc, :], rhs=E[es][:, 0:W2],
                                                                 start=(kc == 0), stop=(kc == 31)),
             reads=[b_hd_[hs], b_E[es]], writes=[bO])
        S.op("pe", lambda e, pZ=pZ, es=es, kc=kc: e.matmul(pZ[:, 0:W2], lhsT=ones[:, :], rhs=E[es][:, 0:W2],
                                                          start=(kc == 0), stop=(kc == 31)),
             reads=[b_const, b_E[es]], writes=[bZ])
        if pending and pending[0][0] <= i:
            _, hh, qq = pending.pop(0)
            emit_post2(hh, qq)
        if kc == 31:
            emit_post1(h, qt)
            pending.append((i + 3, h, qt))
    for _, hh, qq in pending:
        emit_post2(hh, qq)

    b_mrg = Buf("mrg")
    b_gsig_r = b_gsig
    b_stash = Buf("stash")
    new_stage()
    NT9 = LQ // TQ
    aT9 = [ABF.take(8 * LQ, "p (k t) -> p k t", t=LQ) for _ in range(2)]
    b_aT9 = [Buf() for _ in range(2)]
    S.dma("sp", aT9[0], attT_d.ap().rearrange("k p t -> p k t"), reads=[b_att], writes=[b_aT9[0]])
    S.dma("sp", aT9[1], yhyT_d.ap().rearrange("k p t -> p k t"), reads=[b_yhy], writes=[b_aT9[1]])
    w9 = [[ABF.take(8 * 512, "p (k c) -> p k c", c=512) for _ in range(2)] for _ in range(2)]
    b_w9 = [[Buf() for _ in range(2)] for _ in range(2)]
    NS9 = 6
    g9 = [[AFF.take(TQ) for _ in range(NS9)] for _ in range(2)]
    b_g9 = [[Buf() for _ in range(NS9)] for _ in range(2)]
    t9 = [[AFF.take(TQ) for _ in range(2)] for _ in range(2)]
    b_t9 = [[Buf() for _ in range(2)] for _ in range(2)]
    o9 = [ABF.take(TQ) for _ in range(3)]
    b_o9 = [Buf() for _ in range(3)]
    order9 = [(cg, tt, cc) for cg in range(4) for tt in range(NT9) for cc in range(4)]
    issued9 = [0]

    def prefetch9(upto):
        while issued9[0] <= min(upto, len(order9) - 1):
            n = issued9[0]
            cg_, tt_, cc_ = order9[n]
            ci = cg_ * 4 + cc_
            for br in range(2):
                S.dma("sp", g9[br][n % NS9], gsig_d.ap()[br * 16 + ci, :, tt_ * TQ:(tt_ + 1) * TQ], reads=[b_gsig],
                      writes=[b_g9[br][n % NS9]])
            issued9[0] += 1

    def load_w9(cg):
        for br, w_t in enumerate((wa_t, wh_t)):
            S.dma("pool", w9[br][cg % 2], w_t.ap()[:, cg * 512:(cg + 1) * 512].rearrange("(k p) c -> p k c", p=128),
                  writes=[b_w9[br][cg % 2]])

    load_w9(0)
    prefetch9(2)
    for n, (cg, tt, cc) in enumerate(order9):
        if tt == 0 and cc == 0 and cg + 1 < 4:
            load_w9(cg + 1)
        prefetch9(n + 3)
        ci = cg * 4 + cc
        pp = n % 2
        for br in range(2):
            ps, pb = PS[2 * pp + br], b_ps[2 * pp + br]
            for k in range(8):
                S.op("pe", lambda e, ps=ps, br=br, cg=cg, k=k, cc=cc, tt=tt: e.matmul(
                    ps[:, 0:TQ], lhsT=w9[br][cg % 2][:, k, cc * 128:(cc + 1) * 128], rhs=aT9[br][:, k, tt * TQ:(tt + 1) * TQ],
                    start=(k == 0), stop=(k == 7)), reads=[b_w9[br][cg % 2], b_aT9[br]], writes=[pb])
        for br in range(2):
            ps, pb = PS[2 * pp + br], b_ps[2 * pp + br]
            S.op("dve", lambda e, ps=ps, br=br, pp=pp, n=n: e.tensor_tensor(out=t9[pp][br], in0=ps[:, 0:TQ], in1=g9[br][n % NS9], op=ALU.mult),
                 reads=[pb, b_g9[br][n % NS9]], writes=[b_t9[pp][br]])
        S.op("pool", lambda e, pp=pp, n=n: e.tensor_tensor(out=o9[n % 3], in0=t9[pp][0], in1=t9[pp][1], op=ALU.add),
             reads=[b_t9[pp][0], b_t9[pp][1]], writes=[b_o9[n % 3]])
        S.dma("sp", mrgT_d.ap()[ci, :, tt * TQ:(tt + 1) * TQ], o9[n % 3], reads=[b_o9[n % 3]], writes=[b_mrg])

    b_xmid = Buf("xmid")

    def resid_stage(src_t, src_buf, w_t, kc, res_t, res_buf, dst_t, dst_buf, cgw, at_slots, TT=512, at_ap=None):
        cnt = [0]
        slots = {}

        NSR = 6
        orderR = [(cg, tt * (TT // 128) + tb) for cg in range(D // cgw) for tt in range(LQ // TT) for tb in range(TT // 128)]
        issued = [0]

        def pro():
            slots["r"], slots["rb"] = mk_out_slots(NSR, AFF, cgw)

        def prefetch(upto):
            while issued[0] <= min(upto, len(orderR) - 1):
                n = issued[0]
                cg_, tb_ = orderR[n]
                S.dma("sp", slots["r"][n % NSR], res_t.ap()[tb_ * 128:(tb_ + 1) * 128, cg_ * cgw:(cg_ + 1) * cgw],
                      reads=[res_buf], writes=[slots["rb"][n % NSR]])
                issued[0] += 1

        def evac(cg, tb, ps, pb):
            n = cnt[0]
            cnt[0] += 1
            assert orderR[n] == (cg, tb)
            prefetch(n + 3)
            r, rb = slots["r"][n % NSR], slots["rb"][n % NSR]
            S.op("dve", lambda e: e.tensor_tensor(out=r, in0=ps, in1=r, op=ALU.add), reads=[pb, rb], writes=[rb])
            S.dma("sp", dst_t.ap()[tb * 128:(tb + 1) * 128, cg * cgw:(cg + 1) * cgw], r, reads=[rb], writes=[dst_buf])

        linear([(src_t, src_buf, w_t, 0, kc)], D, "tm", evac, cgw=cgw, at_slots=at_slots, prologue=pro, TT=TT, T=LQ, resident=(kc <= 16), at_ap=at_ap)

    resid_stage(mrgT_d, b_mrg, wo_t, 16, xmy_t, b_xmy, xmid_d, b_xmid, 512, 2, TT=TQ)

    b_hfT = Buf("hfT")
    norm_T(xmid_d, g_ffn_t, hfT_d, b_hfT, b_xmid, nrows=LQ, mask=mask_t)
    b_act = Buf("act")
    new_stage()
    NT11 = LQ // TQ
    fcw = AFF.take(88 * 4, "p (c j) -> p c j", j=4)
    S.dma("sp", fcw, ffcw_t.ap(), writes=[b_cw])
    hf11 = ABF.take(16 * LQ, "p (k t) -> p k t", t=LQ)
    b_hf11 = Buf()
    S.dma("sp", hf11, hfT_d.ap().rearrange("k p t -> p k t"), reads=[b_hfT], writes=[b_hf11])
    w11 = [ABF.take(16 * 512, "p (k c) -> p k c", c=512) for _ in range(2)]
    b_w11 = [Buf() for _ in range(2)]
    rawb = [[AFF.take(LQ + 2) for _ in range(4)] for _ in range(2)]
    b_rawb = [[Buf() for _ in range(4)] for _ in range(2)]
    for sl_ in range(2):
        for cc_ in range(4):
            S.op("dve", lambda e, sl_=sl_, cc_=cc_: e.memset(rawb[sl_][cc_][:, 0:1], 0.0), writes=[b_rawb[sl_][cc_]])
            S.op("dve", lambda e, sl_=sl_, cc_=cc_: e.memset(rawb[sl_][cc_][:, LQ + 1:LQ + 2], 0.0), writes=[b_rawb[sl_][cc_]])
    cvg = AFF.take(LQ)
    cvv = AFF.take(LQ)
    sil = AFF.take(LQ)
    b_cvg, b_cvv, b_sil = Buf(), Buf(), Buf()
    ao = [ABF.take(LQ) for _ in range(2)]
    b_ao = [Buf() for _ in range(2)]

    def load_w11(cg):
        wv = wup_t.ap()[:, cg * 512:(cg + 1) * 512].rearrange("(k p) c -> p k c", p=128)
        S.dma("pool", w11[cg % 2][:, 0:8, :], wv[:, 0:8, :], writes=[b_w11[cg % 2]])
        S.dma("pool", w11[cg % 2][:, 8:16, :], wv[:, 8:16, :], writes=[b_w11[cg % 2]])

    def conv_full(dst, src, cidx, b_src, b_dst):
        n = LQ
        S.op("dve", lambda e: e.tensor_scalar(out=dst, in0=src[:, 0:n], scalar1=fcw[:, cidx, 0:1], scalar2=fcw[:, cidx, 3:4],
                                              op0=ALU.mult, op1=ALU.add), reads=[b_src, b_cw], writes=[b_dst])
        S.op("dve", lambda e: e.scalar_tensor_tensor(out=dst, in0=src[:, 1:n + 1], scalar=fcw[:, cidx, 1:2], in1=dst,
                                                     op0=ALU.mult, op1=ALU.add), reads=[b_src, b_cw, b_dst], writes=[b_dst])
        S.op("dve", lambda e: e.scalar_tensor_tensor(out=dst, in0=src[:, 2:n + 2], scalar=fcw[:, cidx, 2:3], in1=dst,
                                                     op0=ALU.mult, op1=ALU.add), reads=[b_src, b_cw, b_dst], writes=[b_dst])

    load_w11(0)
    pi11 = 0
    npair = 0
    for cg in range(22):
        sl = cg % 2
        if cg + 1 < 22:
            load_w11(cg + 1)
        for tt in range(NT11):
            for cc in range(4):
                ps, pb = PS[pi11 % 4], b_ps[pi11 % 4]
                pi11 += 1
                for k in range(16):
                    S.op("pe", lambda e, ps=ps, sl=sl, k=k, cc=cc, tt=tt: e.matmul(
                        ps[:, 0:TQ], lhsT=w11[sl][:, k, cc * 128:(cc + 1) * 128], rhs=hf11[:, k, tt * TQ:(tt + 1) * TQ],
                        start=(k == 0), stop=(k == 15)), reads=[b_w11[sl], b_hf11], writes=[pb])
                S.op("act", lambda e, ps=ps, sl=sl, cc=cc, tt=tt: e.activation(out=rawb[sl][cc][:, 1 + tt * TQ:1 + (tt + 1) * TQ],
                                                                              in_=ps[:, 0:TQ], func=AF.Copy),
                     reads=[pb], writes=[b_rawb[sl][cc]])
        for pr in range(2):
            c = cg * 2 + pr
            conv_full(cvg, rawb[sl][2 * pr], c, b_rawb[sl][2 * pr], b_cvg)
            conv_full(cvv, rawb[sl][2 * pr + 1], 44 + c, b_rawb[sl][2 * pr + 1], b_cvv)
            S.op("act", lambda e: e.activation(out=sil, in_=cvg, func=AF.Silu), reads=[b_cvg], writes=[b_sil])
            S.op("pool", lambda e, npair=npair: e.tensor_tensor(out=ao[npair % 2], in0=sil, in1=cvv, op=ALU.mult),
                 reads=[b_sil, b_cvv], writes=[b_ao[npair % 2]])
            S.dma("sp", actT_d.ap()[:, :, c, :].rearrange("b p t -> p b t"), ao[npair % 2].rearrange("p (b t) -> p b t", t=128),
                  reads=[b_ao[npair % 2]], writes=[b_act])
            npair += 1

    b_xout = b_xmid
    resid_stage(actT_d, b_act, wdn_t, 44, xmid_d, b_xmid, xmid_d, b_xout, 512, 2, TT=128, at_ap=lambda si, tt_: actT_d.ap()[tt_])

    b_out = Buf("out")
    norm_T(xmid_d, g_fin_t, None, b_out, b_xout, tok_out=out_t, nrows=NOUT, row_off=2)
    S.barrier()

    with nc.Block() as block:
        @block.tensor
        def _(e):
            for f in S.ops["pe"]:
                f(e)

        @block.scalar
        def _(e):
            for f in S.ops["act"]:
                f(e)

        @block.vector
        def _(e):
            for f in S.ops["dve"]:
                f(e)

        @block.gpsimd
        def _(e):
            for f in S.ops["pool"]:
                f(e)

        @block.sync
        def _(e):
            for f in S.ops["sp"]:
                f(e)
    st.close()
    return nc


_CONST = {}


def _t5_bucket(rel):
    half, max_exact = 16, 8
    ret = (rel > 0).astype(np.int32) * half
    n = np.abs(rel)
    nf = np.maximum(n, 1).astype(np.float32)
    large = max_exact + (np.log(nf / max_exact) / math.log(128 / max_exact) * (half - max_exact)).astype(np.int32)
    large = np.minimum(large, half - 1)
    return ret + np.where(n < max_exact, n, large)


def _constants():
    if _CONST:
        return _CONST
    bf = ml_dtypes.bfloat16
    N = 2 * L
    ang = 2.0 * np.pi * np.arange(N) / N
    ctab = np.cos(ang)
    stab = np.sin(ang)
    g = np.arange(L).reshape(32, 1, 1, 128)
    s = (np.arange(32).reshape(1, 1, 32, 1) * 128 + np.arange(128).reshape(1, 128, 1, 1))
    idx = (g * s) % N
    fc_cos = ctab[idx]
    fc_sin = stab[idx]
    nyq = np.where(s % 2 == 0, 1.0, -1.0)[0, :, :, 0]
    fc_sin[0, :, :, 0] = nyq
    ffwd = np.concatenate([fc_cos, fc_sin], 0).astype(np.float32).astype(bf).reshape(64, 128, 32 * 128)
    gg = (np.arange(32).reshape(1, 1, 32, 1) * 128 + np.arange(128).reshape(1, 128, 1, 1))
    finvs, masks, buckets = [], [], []
    for j in range(4):
        mloc = (np.arange(LQ // TQ).reshape(-1, 1, 1, 1) * TQ + np.arange(TQ).reshape(1, 1, 1, TQ))
        t = 1024 * j - 2 + mloc
        valid = (t >= 0) & (t < L) & (mloc < NOUT + 4)
        tc = np.where(valid, t, 0)
        idx = (gg * tc) % N
        ic = ctab[idx] * (2.0 / N)
        isn = stab[idx] * (2.0 / N)
        ic[:, 0:1, 0:1, :] = 1.0 / N
        isn[:, 0:1, 0:1, :] = (np.where(tc % 2 == 0, 1.0, -1.0) / N)
        ic = ic * valid
        isn = isn * valid
        finvs.append(np.concatenate([ic, isn], 2).astype(np.float32).astype(bf).reshape(LQ // TQ, 128, 64 * TQ))
        mv = valid.reshape(-1).astype(np.float32)
        masks.append(np.ascontiguousarray(mv.reshape(LQ // 128, 128).T))
        rel = np.arange(128).reshape(128, 1) - np.arange(WT_LEN).reshape(1, WT_LEN) + WT_M0 - (1024 * j - 2)
        buckets.append(_t5_bucket(np.clip(rel, -(L - 1), L - 1)))
    f32 = np.float32
    tt_ = np.linspace(0.0, 1.0, L, dtype=f32)[:, None]
    tr = np.arange(L, dtype=f32)[:, None]
    an = (f32(2.0 * math.pi) * tr / f32(L)).astype(f32)
    bands = np.linspace(1e-4, 15, 16, dtype=f32)[None, :]
    emb = np.concatenate([tt_, np.cos(bands * an), -np.sin(bands * an)], axis=-1).astype(f32)
    max_decay = math.log(1e-2) / 0.3
    min_decay = math.log(1e-2) / 1.5
    deltas = np.abs(np.linspace(min_decay, max_decay, CH, dtype=f32)).astype(f32)
    negt = (-tt_[:, 0]).reshape(32, 128).T.copy()
    _CONST.update(dict(c_ffwd=ffwd, finvs=finvs, masks=masks, buckets=buckets, c_embT=np.ascontiguousarray(emb.T),
                       c_delta=deltas.reshape(1, CH), c_negt=negt.astype(f32), c_ident=np.eye(128, dtype=f32).astype(bf)))
    return _CONST


_NC = {}


def kernel(x, g_mix, w_in, lambda_q1, lambda_k1, lambda_q2, lambda_k2, g_subln, rel_bias,
           hy_conv_w, hy_conv_b, hy_f_w1, hy_f_b1, hy_f_w2, hy_f_b2, hy_f_w3, hy_f_b3,
           hy_f_w4, hy_freq, hy_d, w_attn_branch, w_hyena_branch, w_out, g_ffn, w_up,
           ffn_conv_w, ffn_conv_b, w_down, g_final):
    f = lambda a: np.ascontiguousarray(np.asarray(a, dtype=np.float32))
    C = _constants()
    rb = f(rel_bias)
    wt_biases = [np.ascontiguousarray(np.transpose(rb[bk], (2, 0, 1))) for bk in C["buckets"]]
    hcw = f(hy_conv_w)[0]
    hcb = f(hy_conv_b)[0]
    hycw = np.ascontiguousarray(np.concatenate([hcw, hcb[None]], 0).reshape(4, 24, 128).transpose(2, 1, 0))
    fw = f(ffn_conv_w)[0]
    fb = f(ffn_conv_b)[0]
    ffcw = np.ascontiguousarray(np.concatenate([fw, fb[None]], 0).reshape(4, 88, 128).transpose(2, 1, 0))
    common = {
        "g_mix": f(g_mix), "w_in": f(w_in)[0],
        "lam4": np.ascontiguousarray(np.stack([f(lambda_q1)[0], f(lambda_k1)[0], f(lambda_q2)[0], f(lambda_k2)[0]], 0)),
        "g_subln": f(g_subln)[0].reshape(128, 1), "hycw": hycw,
        "fw1": f(hy_f_w1)[0], "fw2": f(hy_f_w2)[0], "fw3": f(hy_f_w3)[0], "fw4": f(hy_f_w4)[0],
        "fvec": np.ascontiguousarray(np.stack([f(hy_f_b1)[0], f(hy_f_b2)[0], f(hy_f_b3)[0], f(hy_freq)[0]], 1)),
        "hyd": np.ascontiguousarray(f(hy_d)[0].reshape(8, 128).T),
        "w_attn_branch": f(w_attn_branch)[0], "w_hyena_branch": f(w_hyena_branch)[0], "w_out": f(w_out)[0],
        "g_ffn": f(g_ffn), "w_up": np.ascontiguousarray(f(w_up)[0].reshape(D, 2, 44, 128).transpose(0, 2, 1, 3).reshape(D, 2 * DFF)), "ffcw": ffcw, "w_down": f(w_down)[0],
        "g_final": f(g_final).reshape(1, D),
    }
    for k in ("c_embT", "c_delta", "c_negt", "c_ffwd", "c_ident"):
        common[k] = C[k]
    xs = f(x)
    in_maps = []
    for c in range(NCORES):
        b, j = divmod(c, 4)
        m = dict(common)
        m["x"] = xs[b]
        xm = np.zeros((LQ, D), np.float32)
        lo, hi = 1024 * j - 2, 1024 * j - 2 + NOUT + 4
        slo, shi = max(lo, 0), min(hi, L)
        xm[slo - lo:shi - lo] = xs[b, slo:shi]
        m["x_my"] = xm
        m["mask_my"] = C["masks"][j]
        m["wt_bias"] = wt_biases[j]
        m["c_finv"] = C["finvs"][j]
        in_maps.append(m)
    if "nc" not in _NC:
        _NC["nc"] = build_program()
    res = run_bass_kernel_spmd(_NC["nc"], in_maps, core_ids=list(range(NCORES)))
    out = np.empty((2, L, D), np.float32)
    for c in range(NCORES):
        b, j = divmod(c, 4)
        out[b, 1024 * j:1024 * (j + 1)] = np.asarray(res.results[c]["out"], dtype=np.float32)
    return out
```

```python
import math
from contextlib import ExitStack
import numpy as np
import ml_dtypes
import concourse.bass as bass
import concourse.mybir as mybir
from concourse.bass_utils import run_bass_kernel_spmd

F32 = mybir.dt.float32
BF = mybir.dt.bfloat16
AF = mybir.ActivationFunctionType
ALU = mybir.AluOpType
AX = mybir.AxisListType

D = 2048
L = 4096
NH = 8
DFF = 5632
CH = 1024
NCORES = 8
LQ = 1152
TQ = 384
NOUT = 1024
LAMBDA_INIT = 0.8 - 0.6 * math.exp(0.0)
WT_M0 = 3968
WT_LEN = 5120


class Buf:
    __slots__ = ("w", "r", "name")

    def __init__(self, name=""):
        self.w = {}
        self.r = {}
        self.name = name


class Sched:
    ENG = ("pe", "act", "dve", "pool", "sp")

    def __init__(self, nc, stack, n_dma_sems=12):
        self.nc = nc
        self.ops = {e: [] for e in self.ENG}
        self.esem = {e: stack.enter_context(nc.semaphore("s_" + e)) for e in self.ENG}
        self.ecnt = {e: 0 for e in self.ENG}
        self.dsem = {}
        self.dcnt = {}
        self.dnext = {}
        for q in ("sp", "pool"):
            self.dsem[q] = [stack.enter_context(nc.semaphore("d_%s%d" % (q, i))) for i in range(n_dma_sems)]
            self.dcnt[q] = [0] * n_dma_sems
            self.dnext[q] = 0
        self.waited = {e: {} for e in self.ENG}
        self.allsems = {}

    def _waits(self, eng, deps):
        out = []
        wd = self.waited[eng]
        for key, (sem, val) in deps.items():
            if wd.get(key, 0) < val:
                wd[key] = val
                out.append((sem, val))
        return out

    @staticmethod
    def _merge(dst, src):
        for k, (s, v) in src.items():
            if k not in dst or dst[k][1] < v:
                dst[k] = (s, v)

    def _deps(self, reads, writes):
        deps = {}
        for b in reads:
            self._merge(deps, b.w)
        for b in writes:
            self._merge(deps, b.w)
            self._merge(deps, b.r)
        return deps

    def _mark(self, reads, writes, key, tok):
        for b in writes:
            b.w[key] = tok
        for b in reads:
            b.r[key] = tok
        self.allsems[key] = tok

    def op(self, eng, fn, reads=(), writes=()):
        deps = self._deps(reads, writes)
        if eng == "pe":
            deps.pop("e_pe", None)
        waits = self._waits(eng, deps)
        sem = self.esem[eng]
        self.ecnt[eng] += 1
        val = self.ecnt[eng]

        def run(e, waits=waits, fn=fn, sem=sem):
            for s, v in waits:
                e.wait_ge(s, v)
            fn(e).then_inc(sem, 1)

        self.ops[eng].append(run)
        key = "e_" + eng
        self._mark(reads, writes, key, (sem, val))

    def dma(self, q, out, in_, reads=(), writes=(), slow=False):
        deps = self._deps(reads, writes)
        i = self.dnext[q]
        self.dnext[q] = (i + 1) % len(self.dsem[q])
        sem = self.dsem[q][i]
        key = "d_%s%d" % (q, i)
        if self.dcnt[q][i] > 0:
            self._merge(deps, {key: (sem, self.dcnt[q][i])})
        waits = self._waits(q, deps)
        self.dcnt[q][i] += 16
        val = self.dcnt[q][i]

        def run(e, waits=waits, out=out, in_=in_, sem=sem, slow=slow):
            for s, v in waits:
                e.wait_ge(s, v)
            if slow:
                e.dma_start(out=out, in_=in_, allow_slow_non_contiguous=True).then_inc(sem, 16)
            else:
                e.dma_start(out=out, in_=in_).then_inc(sem, 16)

        self.ops[q].append(run)
        self._mark(reads, writes, key, (sem, val))

    def barrier(self, engs=None):
        for eng in (engs or self.ENG):
            waits = self._waits(eng, dict(self.allsems))
            if waits:
                def run(e, waits=waits):
                    for s, v in waits:
                        e.wait_ge(s, v)
                self.ops[eng].append(run)


class Arena:
    def __init__(self, tensor, n, base=0):
        self.t = tensor
        self.n = n
        self.off = 0
        self.base = base

    def reset(self):
        self.off = 0

    def take(self, n, pattern=None, **kw):
        n_al = (n + 15) // 16 * 16
        assert self.off + n_al <= self.n, ("arena overflow", self.off, n, self.n)
        ap = self.t[:, self.base + self.off:self.base + self.off + n]
        self.off += n_al
        if pattern:
            ap = ap.rearrange(pattern, **kw)
        return ap


def dram_bc(ap1d_tensor, offset, n, parts=128):
    return bass.AP(ap1d_tensor, offset, [[0, parts], [1, n]])


def build_program():
    nc = bass.Bass("TRN2", target_bir_lowering=False)
    st = ExitStack()

    def din(name, shape, dt=F32):
        return nc.dram_tensor(name, list(shape), dt, kind="ExternalInput")

    def dscr(name, shape, dt):
        return nc.dram_tensor(name, list(shape), dt)

    x_t = din("x", [L, D])
    xmy_t = din("x_my", [LQ, D])
    mask_t = din("mask_my", [128, LQ // 128])
    g_mix_t = din("g_mix", [1, D])
    w_in_t = din("w_in", [D, 10240])
    lam_t = din("lam4", [4, 64])
    g_subln_t = din("g_subln", [128, 1])
    wt_t = din("wt_bias", [NH, 128, WT_LEN])
    hycw_t = din("hycw", [128, 24, 4])
    fw1_t = din("fw1", [33, 64])
    fw2_t = din("fw2", [64, 64])
    fw3_t = din("fw3", [64, 64])
    fw4_t = din("fw4", [64, 2048])
    fvec_t = din("fvec", [64, 4])
    hyd_t = din("hyd", [128, 8])
    wa_t = din("w_attn_branch", [1024, D])
    wh_t = din("w_hyena_branch", [1024, D])
    wo_t = din("w_out", [D, D])
    g_ffn_t = din("g_ffn", [1, D])
    wup_t = din("w_up", [D, 2 * DFF])
    ffcw_t = din("ffcw", [128, 88, 4])
    wdn_t = din("w_down", [DFF, D])
    g_fin_t = din("g_final", [1, D])
    embT_t = din("c_embT", [33, L])
    delta_t = din("c_delta", [1, CH])
    negt_t = din("c_negt", [128, 32])
    ffwd_t = din("c_ffwd", [64, 128, 32 * 128], BF)
    finv_t = din("c_finv", [LQ // TQ, 128, 64 * TQ], BF)
    ident_t = din("c_ident", [128, 128], BF)
    out_t = nc.dram_tensor("out", [NOUT, D], F32, kind="ExternalOutput")

    hT_d = dscr("hT_d", [16, 128, L], BF)
    qT_d = dscr("qT_d", [8, 128, LQ], BF)
    hTm_d = dscr("hTm_d", [16, 128, LQ], BF)
    kT_d = dscr("kT_d", [8, 128, L], BF)
    v_d = dscr("v_d", [L, 1024], BF)
    hyraw_d = dscr("hyraw_d", [16, 128, L + 2], F32)
    hyrawm_d = dscr("hyrawm_d", [24, 128, LQ + 2], F32)
    gsig_d = dscr("gsig_d", [32, 128, LQ], F32)
    x0c_d = dscr("x0c_d", [8, 128, LQ], F32)
    uT_d = dscr("uT_d", [8, 128, LQ], F32)
    sig_d = dscr("sig_d", [L, 3072], BF)
    spec_d = dscr("spec_d", [4, 32, 128, CH], F32)
    Y_d = dscr("Y_d", [8, 128, 64, 128], BF)
    yhyT_d = dscr("yhyT_d", [8, 128, LQ], BF)
    attT_d = dscr("attT_d", [8, 128, LQ], BF)
    mrgT_d = dscr("mrgT_d", [16, 128, LQ], BF)
    xmid_d = dscr("xmid_d", [LQ, D], F32)
    hfT_d = dscr("hfT_d", [16, 128, LQ], BF)
    upraw_d = dscr("upraw_d", [88, 128, LQ + 2], F32)
    actT_d = dscr("actT_d", [LQ // 128, 128, 44, 128], BF)

    NBF = 57344
    NF = 16384
    a_bf_t = st.enter_context(nc.sbuf_tensor("a_bf", [128, NBF], BF))
    a_f_t = st.enter_context(nc.sbuf_tensor("a_f", [128, NF], F32))
    ident = st.enter_context(nc.sbuf_tensor("ident", [128, 128], BF))
    ones = st.enter_context(nc.sbuf_tensor("ones", [128, 128], BF))
    cst = st.enter_context(nc.sbuf_tensor("cst", [128, 16], F32))
    PS = [st.enter_context(nc.psum_tensor("ps%d" % i, [128, 512], F32)) for i in range(7)]
    PSB = st.enter_context(nc.psum_tensor("psb", [128, 1024], BF))
    S = Sched(nc, st)
    ABF = Arena(a_bf_t, NBF)
    AFF = Arena(a_f_t, NF)
    b_ps = [Buf("ps%d" % i) for i in range(7)]
    b_psb = Buf("psb")
    b_const = Buf("const")

    def new_stage():
        S.barrier()
        ABF.reset()
        AFF.reset()

    S.op("dve", lambda e: e.memset(ones[:, :], 1.0), writes=[b_const])
    S.op("dve", lambda e: e.memset(cst[:, 0:1], 1e-6), writes=[b_const])
    S.op("dve", lambda e: e.memset(cst[:, 1:2], 1e-5), writes=[b_const])
    S.op("dve", lambda e: e.memset(cst[:, 2:3], -math.pi), writes=[b_const])
    S.op("dve", lambda e: e.memset(cst[:, 3:4], 0.0), writes=[b_const])
    S.dma("sp", ident[:, :], ident_t.ap(), writes=[b_const])

    def norm_T(src_t, g_t, dst_t, dst_buf, src_buf, tok_out=None, nrows=L, row_off=0, mask=None, interleave=None):
        new_stage()
        gbc = AFF.take(D)
        b_g = Buf()
        S.dma("sp", gbc, dram_bc(g_t, 0, D), writes=[b_g])
        xt = [AFF.take(D) for _ in range(2)]
        b_xt = [Buf() for _ in range(2)]
        junk = ABF.take(D)
        b_junk = Buf()
        st_ = [AFF.take(4) for _ in range(2)]
        b_st = [Buf() for _ in range(2)]
        G = 4 if nrows % 512 == 0 else 3
        mk = None
        if mask is not None:
            mk = AFF.take(16)
            S.dma("sp", mk[:, 0:LQ // 128], mask.ap(), writes=[b_g])
        if tok_out is None:
            hb = [ABF.take(D) for _ in range(2)]
            hTt = [ABF.take(16 * G * 128, "p (k t) -> p k t", t=G * 128) for _ in range(2)]
            b_hT = [Buf() for _ in range(2)]
        else:
            hb = [AFF.take(D) for _ in range(2)]
        b_hb = [Buf() for _ in range(2)]
        src = src_t.ap()
        for i in range(nrows // 128):
            s = i % 2
            S.dma("sp", xt[s], src[row_off + i * 128:row_off + (i + 1) * 128, :], reads=[src_buf], writes=[b_xt[s]])
            S.op("act", lambda e, s=s: e.activation(out=junk, in_=xt[s], func=AF.Square, accum_out=st_[s][:, 0:1]),
                 reads=[b_xt[s]], writes=[b_junk, b_st[s]])
            S.op("act", lambda e, s=s: e.activation(out=st_[s][:, 1:2], in_=st_[s][:, 0:1], func=AF.Sqrt,
                                                    bias=cst[:, 0:1], scale=1.0 / D),
                 reads=[b_st[s], b_const], writes=[b_st[s]])
            S.op("dve", lambda e, s=s: e.reciprocal(out=st_[s][:, 2:3], in_=st_[s][:, 1:2]),
                 reads=[b_st[s]], writes=[b_st[s]])
            if mk is not None:
                S.op("dve", lambda e, s=s, i=i: e.tensor_tensor(out=st_[s][:, 2:3], in0=st_[s][:, 2:3], in1=mk[:, i:i + 1], op=ALU.mult),
                     reads=[b_st[s], b_g], writes=[b_st[s]])
            S.op("dve", lambda e, s=s: e.scalar_tensor_tensor(out=hb[s], in0=xt[s], scalar=st_[s][:, 2:3], in1=gbc,
                                                              op0=ALU.mult, op1=ALU.mult),
                 reads=[b_xt[s], b_st[s], b_g], writes=[b_hb[s]])
            if tok_out is not None:
                S.dma("sp", tok_out.ap()[i * 128:(i + 1) * 128, :], hb[s], reads=[b_hb[s]], writes=[dst_buf])
                continue
            if interleave:
                interleave()
            g4 = i // G
            hs = g4 % 2
            tb = i % G
            for half in range(2):
                for kk in range(8):
                    k = half * 8 + kk
                    S.op("pe", lambda e, s=s, k=k, kk=kk: e.transpose(out=PSB[:, kk * 128:(kk + 1) * 128],
                                                                     in_=hb[s][:, k * 128:(k + 1) * 128],
                                                                     identity=ident[:, :]),
                         reads=[b_hb[s], b_const], writes=[b_psb])
                S.op("act", lambda e, hs=hs, half=half, tb=tb: e.activation(
                    out=hTt[hs][:, half * 8:(half + 1) * 8, tb * 128:(tb + 1) * 128],
                    in_=PSB[:, :].rearrange("p (k t) -> p k t", t=128), func=AF.Copy),
                    reads=[b_psb], writes=[b_hT[hs]])
            if tb == G - 1:
                S.dma("sp", dst_t.ap()[:, :, g4 * G * 128:(g4 + 1) * G * 128].rearrange("k p t -> p k t"), hTt[hs],
                      reads=[b_hT[hs]], writes=[dst_buf])

    lin_cache = {}

    def linear(srcs, ncols, mode, evac, cgw=512, TT=512, at_slots=2, prologue=None, T=L, resident=False, chain=False, at_ap=None):
        KCs = [s_[4] for s_ in srcs]
        if resident:
            at_slots = T // TT
        key = (tuple(KCs), cgw, TT, at_slots, T, resident, mode, tuple(s_[0].name for s_ in srcs))
        already_resident = False
        if chain and key in lin_cache:
            wts, b_wt, ats, b_at = lin_cache[key]
            already_resident = resident
            if prologue:
                prologue()
        else:
            new_stage()
            lin_cache.clear()
            if prologue:
                prologue()
            wts = [[ABF.take(kc * cgw, "p (k c) -> p k c", c=cgw) for _ in range(2)] for kc in KCs]
            b_wt = [[Buf() for _ in range(2)] for _ in KCs]
            ats = [[ABF.take(kc * TT, "p (k t) -> p k t", t=TT) for _ in range(at_slots)] for kc in KCs]
            b_at = [[Buf() for _ in range(at_slots)] for _ in KCs]
            lin_cache[key] = (wts, b_wt, ats, b_at)
        ncg = ncols // cgw
        ntt = T // TT
        pi = 0
        it = 0
        def load_w(cg):
            ws = cg % 2
            for si, (a_t, a_buf, w_t, c0, kc) in enumerate(srcs):
                wv = w_t.ap()[:, c0 + cg * cgw:c0 + (cg + 1) * cgw].rearrange("(k p) c -> p k c", p=128)
                half = (kc + 1) // 2
                S.dma("pool", wts[si][ws][:, 0:half, :], wv[:, 0:half, :], writes=[b_wt[si][ws]])
                if half < kc:
                    S.dma("pool", wts[si][ws][:, half:kc, :], wv[:, half:kc, :], writes=[b_wt[si][ws]])

        def load_at(n):
            tt_ = n % ntt
            sl = n % at_slots
            for si, (a_t, a_buf, w_t, c0, kc) in enumerate(srcs):
                src_ap = at_ap(si, tt_) if at_ap else a_t.ap()[:, :, tt_ * TT:(tt_ + 1) * TT].rearrange("k p t -> p k t")
                S.dma("sp", ats[si][sl], src_ap, reads=[a_buf], writes=[b_at[si][sl]])

        load_w(0)
        for cg in range(ncg):
            ws = cg % 2
            if cg + 1 < ncg:
                load_w(cg + 1)
            for tt in range(ntt):
                as_ = it % at_slots
                if resident:
                    if it == 0 and not already_resident:
                        for n_ in range(ntt):
                            load_at(n_)
                else:
                    if it == 0:
                        load_at(0)
                    if at_slots > 1 and it + 1 < ncg * ntt:
                        load_at(it + 1)
                    elif at_slots == 1 and it > 0:
                        load_at(it)
                it += 1
                if mode == "fm":
                    for cc in range(cgw // 128):
                        ps = PS[pi % 4]
                        pb = b_ps[pi % 4]
                        pi += 1
                        n_mm = sum(KCs)
                        j = 0
                        for si, kc in enumerate(KCs):
                            for k in range(kc):
                                S.op("pe", lambda e, si=si, ws=ws, as_=as_, k=k, cc=cc, j=j, n_mm=n_mm, ps=ps:
                                     e.matmul(ps[:, 0:TT], lhsT=wts[si][ws][:, k, cc * 128:(cc + 1) * 128],
                                              rhs=ats[si][as_][:, k, :], start=(j == 0), stop=(j == n_mm - 1)),
                                     reads=[b_wt[si][ws], b_at[si][as_]], writes=[pb])
                                j += 1
                        evac(cg * (cgw // 128) + cc, tt, ps[:, 0:TT], pb)
                else:
                    for tb in range(TT // 128):
                        ps = PS[pi % 4]
                        pb = b_ps[pi % 4]
                        pi += 1
                        n_mm = sum(KCs)
                        j = 0
                        for si, kc in enumerate(KCs):
                            for k in range(kc):
                                S.op("pe", lambda e, si=si, ws=ws, as_=as_, k=k, tb=tb, j=j, n_mm=n_mm, ps=ps:
                                     e.matmul(ps[:, 0:cgw], lhsT=ats[si][as_][:, k, tb * 128:(tb + 1) * 128],
                                              rhs=wts[si][ws][:, k, :], start=(j == 0), stop=(j == n_mm - 1)),
                                     reads=[b_wt[si][ws], b_at[si][as_]], writes=[pb])
                                j += 1
                        evac(cg, tt * (TT // 128) + tb, ps[:, 0:cgw], pb)

    b_sig = Buf("sig")
    new_stage()
    embT = [AFF.take(512) for _ in range(2)]
    b_emb = [Buf() for _ in range(2)]
    w1 = AFF.take(64)
    w2 = AFF.take(64)
    w3 = AFF.take(64)
    w4 = AFF.take(2048)
    fv = AFF.take(8)
    dl = AFF.take(CH)
    ngt = AFF.take(32)
    b_f = Buf()
    S.dma("sp", w1[0:33, :], fw1_t.ap(), writes=[b_f])
    S.dma("sp", w2[0:64, :], fw2_t.ap(), writes=[b_f])
    S.dma("sp", w3[0:64, :], fw3_t.ap(), writes=[b_f])
    S.dma("sp", w4[0:64, :], fw4_t.ap(), writes=[b_f])
    S.dma("sp", fv[0:64, 0:4], fvec_t.ap(), writes=[b_f])
    S.dma("sp", dl, dram_bc(delta_t, 0, CH), writes=[b_f])
    S.dma("sp", ngt, negt_t.ap(), writes=[b_f])
    for j in range(3):
        S.op("dve", lambda e, j=j: e.tensor_tensor(out=fv[0:64, 4 + j:5 + j], in0=fv[0:64, j:j + 1], in1=fv[0:64, 3:4],
                                                   op=ALU.mult), reads=[b_f], writes=[b_f])
    hcur = [AFF.take(512) for _ in range(2)]
    b_h = [Buf() for _ in range(2)]
    H3 = AFF.take(L)
    b_H3 = Buf()
    arg = AFF.take(512)
    b_arg = Buf()
    sA = AFF.take(512)
    sB = AFF.take(512)
    b_sA, b_sB = Buf(), Buf()
    for pt in range(8):
        S.dma("sp", embT[pt % 2][0:33, :], embT_t.ap()[:, pt * 512:(pt + 1) * 512], writes=[b_emb[pt % 2]])
        srcs_ = [(embT[pt % 2][0:33, :], w1[0:33, 0:64]), None, None]
        for ly in range(3):
            ps = PS[(pt * 3 + ly) % 4]
            pb = b_ps[(pt * 3 + ly) % 4]
            if ly == 0:
                rhs, lhsT = srcs_[0]
                rb = b_emb[pt % 2]
            else:
                rhs = hcur[(ly - 1) % 2][0:64, :]
                lhsT = (w2 if ly == 1 else w3)[0:64, 0:64]
                rb = b_h[(ly - 1) % 2]
            S.op("pe", lambda e, ps=ps, lhsT=lhsT, rhs=rhs: e.matmul(ps[0:64, :], lhsT=lhsT, rhs=rhs, start=True, stop=True),
                 reads=[b_f, rb], writes=[pb])
            S.op("dve", lambda e, ps=ps, ly=ly: e.tensor_scalar(out=arg[0:64, :], in0=ps[0:64, :], scalar1=fv[0:64, 3:4],
                                                                scalar2=fv[0:64, 4 + ly:5 + ly], op0=ALU.mult, op1=ALU.add),
                 reads=[pb, b_f], writes=[b_arg])
            if ly < 2:
                dst, db = hcur[ly % 2][0:64, :], b_h[ly % 2]
            else:
                dst, db = H3[0:64, pt * 512:(pt + 1) * 512], b_H3
            S.op("act", lambda e: e.activation(out=sA[0:64, :], in_=arg[0:64, :], func=AF.Sin, scale=0.5),
                 reads=[b_arg], writes=[b_sA])
            S.op("act", lambda e: e.activation(out=sB[0:64, :], in_=arg[0:64, :], func=AF.Sin, scale=0.25),
                 reads=[b_arg], writes=[b_sB])
            S.op("dve", lambda e: e.tensor_tensor(out=sB[0:64, :], in0=sB[0:64, :], in1=sB[0:64, :], op=ALU.mult),
                 reads=[b_sB], writes=[b_sB])
            S.op("dve", lambda e: e.tensor_scalar(out=sB[0:64, :], in0=sB[0:64, :], scalar1=-4.0, scalar2=2.0,
                                                  op0=ALU.mult, op1=ALU.add), reads=[b_sB], writes=[b_sB])
            S.op("dve", lambda e, dst=dst: e.tensor_tensor(out=dst, in0=sA[0:64, :], in1=sB[0:64, :], op=ALU.mult),
                 reads=[b_sA, b_sB], writes=[db])
    dec = [AFF.take(CH) for _ in range(2)]
    b_dec = [Buf() for _ in range(2)]
    hfb = [AFF.take(2048)] * 2
    b_hfb = [Buf()] * 2
    hpm = [ABF.take(2048) for _ in range(2)]
    b_hpm = [Buf() for _ in range(2)]
    for pc in range(32):
        s = pc % 2
        for ct in range(4):
            S.op("pe", lambda e, pc=pc, ct=ct: e.matmul(PS[ct][:, :], lhsT=H3[0:64, pc * 128:(pc + 1) * 128],
                                                        rhs=w4[0:64, ct * 512:(ct + 1) * 512], start=True, stop=True),
                 reads=[b_H3, b_f], writes=[b_ps[ct]])
        S.op("act", lambda e, s=s, pc=pc: e.activation(out=dec[s], in_=dl, func=AF.Exp, scale=ngt[:, pc:pc + 1]),
             reads=[b_f], writes=[b_dec[s]])
        for ct in range(4):
            S.op("dve", lambda e, s=s, ct=ct: e.tensor_tensor(out=hfb[s][:, ct * 512:(ct + 1) * 512], in0=PS[ct][:, :],
                                                              in1=dec[s][:, (ct % 2) * 512:(ct % 2 + 1) * 512], op=ALU.mult),
                 reads=[b_ps[ct], b_dec[s]], writes=[b_hfb[s]])
        S.op("pool", lambda e, s=s: e.tensor_tensor(out=hpm[s][:, 0:1024], in0=hfb[s][:, 0:1024], in1=hfb[s][:, 1024:2048],
                                                    op=ALU.add), reads=[b_hfb[s]], writes=[b_hpm[s]])
        S.op("pool", lambda e, s=s: e.tensor_tensor(out=hpm[s][:, 1024:2048], in0=hfb[s][:, 0:1024], in1=hfb[s][:, 1024:2048],
                                                    op=ALU.subtract), reads=[b_hfb[s]], writes=[b_hpm[s]])
        S.dma("sp", sig_d.ap()[pc * 128:(pc + 1) * 128, 1024:3072], hpm[s], reads=[b_hpm[s]], writes=[b_sig])

    b_spec = Buf("spec")
    KBF = 28672
    ABF_K = Arena(a_bf_t, KBF, base=NBF - KBF)
    AFF_K = Arena(a_f_t, 2048, base=NF - 2048)

    def fcs_of(ct):
        if ct < 2:
            return list(range(64))
        if ct < 4:
            return list(range(32)) + [32]
        return list(range(32, 64))

    def kpass():
        sig1 = ABF_K.take(32 * 512, "p (s c) -> p s c", c=512)
        b_sig1 = Buf()
        ftk = [ABF_K.take(32 * 128, "p (s g) -> p s g", g=128) for _ in range(3)]
        b_ftk = [Buf() for _ in range(3)]
        sok = [AFF_K.take(512) for _ in range(4)]
        b_sok = [Buf() for _ in range(4)]
        ctsK = (4, 5, 2, 3)
        itsK = [(ct, fc) for ct in ctsK for fc in fcs_of(ct)]

        def load_ftK(n):
            S.dma("sp", ftk[n % 3], ffwd_t.ap()[itsK[n][1]].rearrange("p (s g) -> p s g", g=128), writes=[b_ftk[n % 3]])

        load_ftK(0)
        load_ftK(1)
        for n, (ct, fc) in enumerate(itsK):
            fs = n % 3
            if n + 2 < len(itsK):
                load_ftK(n + 2)
            if fc == fcs_of(ct)[0]:
                S.dma("sp", sig1, sig_d.ap()[:, ct * 512:(ct + 1) * 512].rearrange("(s p) c -> p s c", p=128),
                      reads=[b_sig], writes=[b_sig1])
            ps = PS[n % 4]
            pb = b_ps[n % 4]
            for sc in range(32):
                S.op("pe", lambda e, ps=ps, fs=fs, sc=sc: e.matmul(ps[:, :], lhsT=ftk[fs][:, sc, :], rhs=sig1[:, sc, :],
                                                                  start=(sc == 0), stop=(sc == 31)),
                     reads=[b_ftk[fs], b_sig1], writes=[pb])
            o, ob = sok[n % 4], b_sok[n % 4]
            S.op("act", lambda e, o=o, ps=ps: e.activation(out=o, in_=ps[:, :], func=AF.Copy), reads=[pb], writes=[ob])
            if ct in (2, 3) and fc == 32:
                S.dma("sp", spec_d.ap()[3, 0, 0:1, (ct % 2) * 512:(ct % 2 + 1) * 512], o[0:1, :], reads=[ob], writes=[b_spec])
            else:
                S.dma("sp", spec_d.ap()[2 if ct < 4 else 3, fc % 32, :, (ct % 2) * 512:(ct % 2 + 1) * 512], o, reads=[ob], writes=[b_spec])
            yield

    kgen = kpass()

    def k_steps(k=3):
        for _ in range(k):
            next(kgen, None)

    b_x = Buf("x")
    b_hT = Buf("hT")
    norm_T(x_t, g_mix_t, hT_d, b_hT, b_x, interleave=k_steps)
    b_xmy = Buf("xmy")
    b_hTm = Buf("hTm")
    norm_T(xmy_t, g_mix_t, hTm_d, b_hTm, b_xmy, nrows=LQ, interleave=k_steps)
    for _ in kgen:
        pass

    b_q, b_k, b_v, b_hyraw, b_gsig = Buf("q"), Buf("k"), Buf("v"), Buf("hyraw"), Buf("gsig")
    ev = {}

    def mk_out_slots(n, dt_arena, width):
        tiles = [dt_arena.take(width) for _ in range(n)]
        bufs = [Buf() for _ in range(n)]
        return tiles, bufs

    def qk_stage(c0, src_t, src_buf, dst_t, dst_buf, scale, T, TT, chain=False, cgw=512):
        cnt = [0]
        slots = {}

        def pro():
            slots["t"], slots["b"] = mk_out_slots(4, ABF, 512)

        def evac(ci, ti, ps, pb):
            s = cnt[0] % 4
            cnt[0] += 1
            o, ob = slots["t"][s], slots["b"][s]
            S.op("act", lambda e: e.activation(out=o[:, 0:TT], in_=ps, func=AF.Copy, scale=scale), reads=[pb], writes=[ob])
            S.dma("sp", dst_t.ap()[ci, :, ti * TT:(ti + 1) * TT], o[:, 0:TT], reads=[ob], writes=[dst_buf])

        linear([(src_t, src_buf, w_in_t, c0, 16)], 1024, "fm", evac, prologue=pro, T=T, TT=TT, resident=(T == LQ), chain=chain, cgw=cgw)


    def v_stage():
        cnt = [0]
        slots = {}

        def pro():
            slots["t"], slots["b"] = mk_out_slots(4, ABF, 512)

        def evac(cg, tb, ps, pb):
            s = cnt[0] % 4
            cnt[0] += 1
            o, ob = slots["t"][s], slots["b"][s]
            S.op("act", lambda e: e.activation(out=o, in_=ps, func=AF.Copy), reads=[pb], writes=[ob])
            S.dma("sp", v_d.ap()[tb * 128:(tb + 1) * 128, cg * 512:(cg + 1) * 512], o, reads=[ob], writes=[b_v])

        linear([(hT_d, b_hT, w_in_t, 2048, 16)], 1024, "tm", evac, prologue=pro)


    def raw_stage(src_t, src_buf, w_t, c0, ncols, dst_t, dst_buf, func=AF.Copy, pad=1, T=L, TT=512, chain=False, cgw=512):
        cnt = [0]
        slots = {}

        def pro():
            slots["t"], slots["b"] = mk_out_slots(4, AFF, 512)
            if pad:
                z = AFF.take(2)
                bz = Buf()
                S.op("dve", lambda e: e.memset(z, 0.0), writes=[bz])
                nchunks = ncols // 128
                for c in range(nchunks):
                    S.dma("sp", dst_t.ap()[c, :, 0:1], z[:, 0:1], reads=[bz], writes=[dst_buf], slow=True)
                    S.dma("sp", dst_t.ap()[c, :, T + 1:T + 2], z[:, 1:2], reads=[bz], writes=[dst_buf], slow=True)

        def evac(ci, ti, ps, pb):
            s = cnt[0] % 4
            cnt[0] += 1
            o, ob = slots["t"][s], slots["b"][s]
            S.op("act", lambda e: e.activation(out=o[:, 0:TT], in_=ps, func=func), reads=[pb], writes=[ob])
            S.dma("sp", dst_t.ap()[ci, :, pad + ti * TT:pad + (ti + 1) * TT], o[:, 0:TT], reads=[ob], writes=[dst_buf])

        linear([(src_t, src_buf, w_t, c0, 16)], ncols, "fm", evac, prologue=pro, T=T, TT=TT, resident=(T == LQ), chain=chain, cgw=cgw)

    b_hyrawm = Buf("hyrawm")
    qk_stage(1024, hT_d, b_hT, kT_d, b_k, 1.0, L, 512, cgw=1024)
    raw_stage(hT_d, b_hT, w_in_t, 4096, 2048, hyraw_d, b_hyraw, chain=True, cgw=1024)
    v_stage()
    qk_stage(0, hTm_d, b_hTm, qT_d, b_q, 0.125, LQ, TQ)
    raw_stage(hTm_d, b_hTm, w_in_t, 3072, 3072, hyrawm_d, b_hyrawm, T=LQ, TT=TQ, chain=True)
    raw_stage(hTm_d, b_hTm, w_in_t, 6144, 4096, gsig_d, b_gsig, func=AF.Sigmoid, pad=0, T=LQ, TT=TQ, chain=True)

    b_x0c, b_uT = Buf("x0c"), Buf("uT")
    new_stage()
    cw = AFF.take(24 * 4, "p (c j) -> p c j", j=4)
    b_cw = Buf()
    S.dma("sp", cw, hycw_t.ap(), writes=[b_cw])

    def conv3(eng, dst, src, wts_, c, b_src, b_dst, n=512):
        dst = dst[:, 0:n]
        S.op(eng, lambda e: e.tensor_scalar(out=dst, in0=src[:, 0:n], scalar1=wts_[:, c, 0:1], scalar2=wts_[:, c, 3:4],
                                            op0=ALU.mult, op1=ALU.add), reads=[b_src, b_cw], writes=[b_dst])
        S.op(eng, lambda e: e.scalar_tensor_tensor(out=dst, in0=src[:, 1:n + 1], scalar=wts_[:, c, 1:2], in1=dst,
                                                   op0=ALU.mult, op1=ALU.add), reads=[b_src, b_cw, b_dst], writes=[b_dst])
        S.op(eng, lambda e: e.scalar_tensor_tensor(out=dst, in0=src[:, 2:n + 2], scalar=wts_[:, c, 2:3], in1=dst,
                                                   op0=ALU.mult, op1=ALU.add), reads=[b_src, b_cw, b_dst], writes=[b_dst])

    raw = [[AFF.take(514) for _ in range(3)] for _ in range(2)]
    b_raw = [[Buf() for _ in range(3)] for _ in range(2)]
    cv = [[AFF.take(512) for _ in range(3)] for _ in range(2)]
    b_cv = [[Buf() for _ in range(3)] for _ in range(2)]
    ubf = [ABF.take(512) for _ in range(2)]
    b_ubf = [Buf() for _ in range(2)]
    utm = [ABF.take(4 * 1024, "p (b c) -> p b c", c=1024) for _ in range(2)]
    b_utm = [Buf() for _ in range(2)]
    it = 0
    for tt in range(8):
        us = tt % 2
        for c in range(8):
            s = it % 2
            it += 1
            for j in (1, 2):
                S.dma("sp", raw[s][j], hyraw_d.ap()[(j - 1) * 8 + c, :, tt * 512:tt * 512 + 514],
                      reads=[b_hyraw], writes=[b_raw[s][j]])
            conv3("dve", cv[s][1], raw[s][1], cw, 8 + c, b_raw[s][1], b_cv[s][1])
            conv3("dve", cv[s][2], raw[s][2], cw, 16 + c, b_raw[s][2], b_cv[s][2])
            S.op("pool", lambda e, s=s: e.tensor_tensor(out=ubf[s], in0=cv[s][1], in1=cv[s][2], op=ALU.mult),
                 reads=[b_cv[s][1], b_cv[s][2]], writes=[b_ubf[s]])
            for tb in range(4):
                S.op("pe", lambda e, s=s, tb=tb: e.transpose(out=PSB[:, tb * 128:(tb + 1) * 128],
                                                            in_=ubf[s][:, tb * 128:(tb + 1) * 128], identity=ident[:, :]),
                     reads=[b_ubf[s], b_const], writes=[b_psb])
            S.op("act", lambda e, us=us, c=c: e.activation(out=utm[us][:, :, c * 128:(c + 1) * 128],
                                                           in_=PSB[:, 0:512].rearrange("p (b c) -> p b c", c=128),
                                                           func=AF.Copy), reads=[b_psb], writes=[b_utm[us]])
        S.dma("sp", sig_d.ap()[tt * 512:(tt + 1) * 512, 0:1024].rearrange("(b p) c -> p b c", p=128), utm[us],
              reads=[b_utm[us]], writes=[b_sig])
    for tt in range(LQ // TQ):
        for c in range(8):
            s = it % 2
            it += 1
            for j in range(3):
                S.dma("sp", raw[s][j][:, 0:TQ + 2], hyrawm_d.ap()[j * 8 + c, :, tt * TQ:tt * TQ + TQ + 2],
                      reads=[b_hyrawm], writes=[b_raw[s][j]])
            conv3("dve", cv[s][0], raw[s][0], cw, c, b_raw[s][0], b_cv[s][0], n=TQ)
            conv3("dve", cv[s][1], raw[s][1], cw, 8 + c, b_raw[s][1], b_cv[s][1], n=TQ)
            conv3("dve", cv[s][2], raw[s][2], cw, 16 + c, b_raw[s][2], b_cv[s][2], n=TQ)
            S.dma("sp", x0c_d.ap()[c, :, tt * TQ:(tt + 1) * TQ], cv[s][0][:, 0:TQ], reads=[b_cv[s][0]], writes=[b_x0c])
            S.op("pool", lambda e, s=s: e.tensor_tensor(out=cv[s][1][:, 0:TQ], in0=cv[s][1][:, 0:TQ], in1=cv[s][2][:, 0:TQ], op=ALU.mult),
                 reads=[b_cv[s][1], b_cv[s][2]], writes=[b_cv[s][1]])
            S.dma("sp", uT_d.ap()[c, :, tt * TQ:(tt + 1) * TQ], cv[s][1][:, 0:TQ], reads=[b_cv[s][1]], writes=[b_uT])

    b_Y = Buf("Y")
    new_stage()
    sigt = [ABF.take(32 * 512, "p (s c) -> p s c", c=512) for _ in range(2)]
    b_sigt = [Buf() for _ in range(2)]
    ft = [ABF.take(32 * 128, "p (s g) -> p s g", g=128) for _ in range(4)]
    b_ft = [Buf() for _ in range(4)]
    so = [AFF.take(512) for _ in range(4)]
    b_so = [Buf() for _ in range(4)]
    cts = (0, 1)

    def load_sig(cti):
        ct = cts[cti]
        S.dma("sp", sigt[cti % 2], sig_d.ap()[:, ct * 512:(ct + 1) * 512].rearrange("(s p) c -> p s c", p=128),
              reads=[b_sig], writes=[b_sigt[cti % 2]])

    load_sig(0)

    kin = [[AFF.take(512) for _ in range(2)] for _ in range(3)]
    b_kin = [[Buf() for _ in range(2)] for _ in range(3)]
    tq = [[AFF.take(512) for _ in range(4)] for _ in range(2)]
    b_tq = [[Buf() for _ in range(4)] for _ in range(2)]
    yo = [[ABF.take(512) for _ in range(2)] for _ in range(2)]
    b_yo = [[Buf() for _ in range(2)] for _ in range(2)]
    itsU = [(cti, ct, gc) for cti, ct in ((0, 0), (1, 1)) for gc in range(32)]

    def load_U(n):
        cti, ct, gc = itsU[n]
        for j, fc in enumerate((gc, 32 + gc)):
            sl = (n % 2) * 2 + j
            S.dma("sp", ft[sl], ffwd_t.ap()[fc].rearrange("p (s g) -> p s g", g=128), writes=[b_ft[sl]])
        for w_ in range(2):
            S.dma("sp", kin[n % 3][w_], spec_d.ap()[2 + w_, gc, :, ct * 512:(ct + 1) * 512], reads=[b_spec], writes=[b_kin[n % 3][w_]])

    load_U(0)
    for n, (cti, ct, gc) in enumerate(itsU):
        ss_ = cti % 2
        if n + 1 < len(itsU):
            load_U(n + 1)
        if gc == 0 and ct == 0:
            load_sig(1)
        pp = n % 2
        for j in range(2):
            ps, pb = PS[2 * pp + j], b_ps[2 * pp + j]
            sl = pp * 2 + j
            for sc in range(32):
                S.op("pe", lambda e, ps=ps, sl=sl, sc=sc, ss_=ss_: e.matmul(ps[:, :], lhsT=ft[sl][:, sc, :], rhs=sigt[ss_][:, sc, :],
                                                                           start=(sc == 0), stop=(sc == 31)),
                     reads=[b_ft[sl], b_sigt[ss_]], writes=[pb])
        pUc, bUc = PS[2 * pp], b_ps[2 * pp]
        pUs, bUs = PS[2 * pp + 1], b_ps[2 * pp + 1]
        Kc, Ks = kin[n % 3]
        bKc, bKs = b_kin[n % 3]
        t = tq[pp]
        bt = b_tq[pp]
        S.op("dve", lambda e, t=t, pUc=pUc, Kc=Kc: e.tensor_tensor(out=t[0], in0=pUc[:, :], in1=Kc, op=ALU.mult), reads=[bUc, bKc], writes=[bt[0]])
        S.op("dve", lambda e, t=t, pUs=pUs, Ks=Ks: e.tensor_tensor(out=t[1], in0=pUs[:, :], in1=Ks, op=ALU.mult), reads=[bUs, bKs], writes=[bt[1]])
        S.op("dve", lambda e, t=t, pUc=pUc, Ks=Ks: e.tensor_tensor(out=t[2], in0=pUc[:, :], in1=Ks, op=ALU.mult), reads=[bUc, bKs], writes=[bt[2]])
        S.op("dve", lambda e, t=t, pUs=pUs, Kc=Kc: e.tensor_tensor(out=t[3], in0=pUs[:, :], in1=Kc, op=ALU.mult), reads=[bUs, bKc], writes=[bt[3]])
        S.op("pool", lambda e, t=t, pp=pp: e.tensor_tensor(out=yo[pp][0], in0=t[0], in1=t[1], op=ALU.subtract),
             reads=[bt[0], bt[1]], writes=[b_yo[pp][0]])
        S.op("pool", lambda e, t=t, pp=pp: e.tensor_tensor(out=yo[pp][1], in0=t[2], in1=t[3], op=ALU.add),
             reads=[bt[2], bt[3]], writes=[b_yo[pp][1]])
        if gc == 0:
            S.op("pool", lambda e, t=t, pp=pp: e.tensor_copy(out=yo[pp][0][0:1, :], in_=t[0][0:1, :]), reads=[bt[0], b_yo[pp][0]], writes=[b_yo[pp][0]])
            S.op("pool", lambda e, t=t, pp=pp: e.tensor_copy(out=yo[pp][1][0:1, :], in_=t[1][0:1, :]), reads=[bt[1], b_yo[pp][1]], writes=[b_yo[pp][1]])
        S.dma("sp", Y_d.ap()[ct * 4:(ct + 1) * 4, :, gc, :].rearrange("c p e -> p c e"),
              yo[pp][0].rearrange("p (c e) -> p c e", e=128), reads=[b_yo[pp][0]], writes=[b_Y])
        S.dma("sp", Y_d.ap()[ct * 4:(ct + 1) * 4, :, 32 + gc, :].rearrange("c p e -> p c e"),
              yo[pp][1].rearrange("p (c e) -> p c e", e=128), reads=[b_yo[pp][1]], writes=[b_Y])

    b_yhy = Buf("yhy")
    new_stage()
    fv_ = ABF.take(64 * TQ, "p (f t) -> p f t", t=TQ)
    b_fv = Buf()
    yc = [ABF.take(64 * 128, "p (f c) -> p f c", c=128) for _ in range(2)]
    b_yc = [Buf() for _ in range(2)]
    hd = AFF.take(8)
    b_hd = Buf()
    S.dma("sp", hd, hyd_t.ap(), writes=[b_hd])
    xin = [[AFF.take(512) for _ in range(2)] for _ in range(2)]
    b_xin = [[Buf() for _ in range(2)] for _ in range(2)]
    yout = [ABF.take(512) for _ in range(2)]
    b_yout = [Buf() for _ in range(2)]
    it = 0

    def load7(n):
        tt_, c_ = divmod(n, 8)
        sl = n % 2
        S.dma("sp", yc[sl], Y_d.ap()[c_], reads=[b_Y], writes=[b_yc[sl]])
        S.dma("sp", xin[sl][0][:, 0:TQ], uT_d.ap()[c_, :, tt_ * TQ:(tt_ + 1) * TQ], reads=[b_uT], writes=[b_xin[sl][0]])
        S.dma("sp", xin[sl][1][:, 0:TQ], x0c_d.ap()[c_, :, tt_ * TQ:(tt_ + 1) * TQ], reads=[b_x0c], writes=[b_xin[sl][1]])

    for tt in range(LQ // TQ):
        S.dma("sp", fv_, finv_t.ap()[tt].rearrange("p (f t) -> p f t", t=TQ), writes=[b_fv])
        for c in range(8):
            s = it % 2
            if it == 0:
                load7(0)
            if it + 1 < 8 * (LQ // TQ):
                load7(it + 1)
            it += 1
            ps = PS[s]
            pb = b_ps[s]
            for f in range(64):
                S.op("pe", lambda e, ps=ps, s=s, f=f: e.matmul(ps[:, 0:TQ], lhsT=yc[s][:, f, :], rhs=fv_[:, f, :],
                                                               start=(f == 0), stop=(f == 63)),
                     reads=[b_yc[s], b_fv], writes=[pb])
            S.op("dve", lambda e, ps=ps, s=s, c=c: e.scalar_tensor_tensor(out=xin[s][0][:, 0:TQ], in0=xin[s][0][:, 0:TQ], scalar=hd[:, c:c + 1],
                                                                         in1=ps[:, 0:TQ], op0=ALU.mult, op1=ALU.add),
                 reads=[pb, b_xin[s][0], b_hd], writes=[b_xin[s][0]])
            S.op("dve", lambda e, s=s: e.tensor_tensor(out=yout[s][:, 0:TQ], in0=xin[s][0][:, 0:TQ], in1=xin[s][1][:, 0:TQ], op=ALU.mult),
                 reads=[b_xin[s][0], b_xin[s][1]], writes=[b_yout[s]])
            S.dma("sp", yhyT_d.ap()[c, :, tt * TQ:(tt + 1) * TQ], yout[s][:, 0:TQ], reads=[b_yout[s]], writes=[b_yhy])

    b_att = Buf("att")
    new_stage()
    lam = AFF.take(64 * 4 + 8)
    b_lam = Buf()
    for j in range(4):
        S.dma("sp", lam[:, j * 64:(j + 1) * 64], dram_bc(lam_t, j * 64, 64), writes=[b_lam])
    gs = AFF.take(2)
    S.dma("sp", gs[:, 0:1], g_subln_t.ap(), writes=[b_lam])
    S.op("dve", lambda e: e.tensor_scalar(out=gs[:, 1:2], in0=gs[:, 0:1], scalar1=1.0 - LAMBDA_INIT, scalar2=None, op0=ALU.mult),
         reads=[b_lam], writes=[b_lam])
    for j in range(2):
        S.op("dve", lambda e, j=j: e.tensor_tensor(out=lam[:, j * 128:j * 128 + 64], in0=lam[:, j * 128:j * 128 + 64],
                                                   in1=lam[:, j * 128 + 64:j * 128 + 128], op=ALU.mult), reads=[b_lam], writes=[b_lam])
        S.op("dve", lambda e, j=j: e.reduce_sum(out=lam[:, 256 + j:257 + j], in_=lam[:, j * 128:j * 128 + 64], axis=AX.X),
             reads=[b_lam], writes=[b_lam])
        S.op("act", lambda e, j=j: e.activation(out=lam[:, 258 + j:259 + j], in_=lam[:, 256 + j:257 + j], func=AF.Exp),
             reads=[b_lam], writes=[b_lam])
    S.op("dve", lambda e: e.tensor_tensor(out=lam[:, 260:261], in0=lam[:, 259:260], in1=lam[:, 258:259], op=ALU.subtract),
         reads=[b_lam], writes=[b_lam])
    S.op("dve", lambda e: e.tensor_scalar(out=lam[:, 260:261], in0=lam[:, 260:261], scalar1=-LAMBDA_INIT, scalar2=None, op0=ALU.add),
         reads=[b_lam], writes=[b_lam])
    neglam = lam[:, 260:261]

    qh = [ABF.take(2 * LQ, "p (m t) -> p m t", m=2) for _ in range(2)]
    kh = [ABF.take(L) for _ in range(2)]
    vh = [ABF.take(32 * 128, "p (k e) -> p k e", e=128) for _ in range(2)]
    wth = [ABF.take(WT_LEN) for _ in range(2)]
    b_hd_ = [Buf() for _ in range(2)]
    E = [ABF.take(512) for _ in range(3)]
    b_E = [Buf() for _ in range(3)]
    atth = [ABF.take(LQ) for _ in range(2)]
    b_atth = [Buf() for _ in range(2)]
    sq = [ABF.take(256) for _ in range(2)]
    b_sq = [Buf() for _ in range(2)]
    rz = [AFF.take(512) for _ in range(2)]
    o12 = [AFF.take(512) for _ in range(2)]
    at_ = [AFF.take(256) for _ in range(2)]
    rs = [AFF.take(256) for _ in range(2)]
    b_ev = [Buf() for _ in range(2)]
    QT = 192
    W2 = 2 * QT

    b_qz = Buf()
    for hs_ in range(2):
        S.op("dve", lambda e, hs_=hs_: e.memset(qh[hs_][0:64, 1, :], 0.0), writes=[b_qz])
        S.op("dve", lambda e, hs_=hs_: e.memset(qh[hs_][64:128, 0, :], 0.0), writes=[b_qz])

    def load_head(h):
        hs = h % 2
        S.dma("sp", qh[hs][0:64, 0, :], qT_d.ap()[h, 0:64, :], reads=[b_q], writes=[b_hd_[hs]])
        S.dma("sp", qh[hs][64:128, 1, :], qT_d.ap()[h, 64:128, :], reads=[b_q], writes=[b_hd_[hs]])
        S.dma("sp", kh[hs], kT_d.ap()[h], reads=[b_k], writes=[b_hd_[hs]])
        S.dma("sp", vh[hs], v_d.ap()[:, h * 128:(h + 1) * 128].rearrange("(k p) e -> p k e", p=128), reads=[b_v], writes=[b_hd_[hs]])
        S.dma("pool", wth[hs], wt_t.ap()[h], writes=[b_hd_[hs]])

    iters = [(h, qt, kc) for h in range(NH) for qt in range(LQ // QT) for kc in range(32)]

    def is_near(qt, kc):
        return True

    def emit_S(i):
        h, qt, kc = iters[i]
        hs = h % 2
        pS, bS = PS[i % 2], b_ps[i % 2]
        near = is_near(qt, kc)
        c0 = WT_M0 - kc * 128 + qt * QT
        assert 0 <= c0 <= WT_LEN - QT or not near
        S.op("pe", lambda e: e.matmul(
            pS[:, 0:W2].rearrange("p (m q) -> p m q", m=2), lhsT=kh[hs][:, kc * 128:(kc + 1) * 128],
            rhs=qh[hs][:, :, qt * QT:(qt + 1) * QT], start=True, stop=(not near)),
            reads=[b_hd_[hs], b_qz], writes=[bS])
        if near:
            for mp in range(2):
                S.op("pe", lambda e, mp=mp: e.matmul(
                    pS[:, mp * QT:(mp + 1) * QT], lhsT=ident[:, :], rhs=wth[hs][:, c0:c0 + QT], start=False, stop=(mp == 1)),
                    reads=[b_hd_[hs], b_const], writes=[bS])

    def emit_post1(h, qt):
        hs = h % 2
        os_ = qt % 2
        pO, bO = PS[2 + os_], b_ps[2 + os_]
        pZ, bZ = PS[4 + os_], b_ps[4 + os_]
        S.op("dve", lambda e: e.reciprocal(out=rz[os_][:, 0:W2], in_=pZ[:, 0:W2]), reads=[bZ], writes=[b_ev[os_]])
        S.op("dve", lambda e: e.tensor_tensor(out=o12[os_][:, 0:W2], in0=pO[:, 0:W2], in1=rz[os_][:, 0:W2], op=ALU.mult),
             reads=[bO, b_ev[os_]], writes=[b_ev[os_]])
        S.op("dve", lambda e: e.scalar_tensor_tensor(out=at_[os_][:, 0:QT], in0=o12[os_][:, QT:2 * QT], scalar=neglam,
                                                     in1=o12[os_][:, 0:QT], op0=ALU.mult, op1=ALU.add),
             reads=[b_ev[os_], b_lam], writes=[b_ev[os_]])
        S.op("dve", lambda e: e.tensor_tensor(out=sq[os_][:, 0:QT], in0=at_[os_][:, 0:QT], in1=at_[os_][:, 0:QT], op=ALU.mult),
             reads=[b_ev[os_]], writes=[b_sq[os_]])

    def emit_post2(h, qt):
        hs = h % 2
        os_ = qt % 2
        S.op("pe", lambda e: e.matmul(PS[6][:, 0:QT], lhsT=ones[:, :], rhs=sq[os_][:, 0:QT], start=True, stop=True),
             reads=[b_const, b_sq[os_]], writes=[b_ps[6]])
        S.op("act", lambda e: e.activation(out=rs[os_][:, 0:QT], in_=PS[6][:, 0:QT], func=AF.Sqrt, bias=cst[:, 1:2], scale=1.0 / 128),
             reads=[b_ps[6], b_const], writes=[b_ev[os_]])
        S.op("dve", lambda e: e.reciprocal(out=rs[os_][:, 0:QT], in_=rs[os_][:, 0:QT]), reads=[b_ev[os_]], writes=[b_ev[os_]])
        S.op("dve", lambda e: e.scalar_tensor_tensor(out=atth[hs][:, qt * QT:(qt + 1) * QT], in0=at_[os_][:, 0:QT],
                                                     scalar=gs[:, 1:2], in1=rs[os_][:, 0:QT], op0=ALU.mult, op1=ALU.mult),
             reads=[b_ev[os_], b_lam], writes=[b_atth[hs]])
        if qt == LQ // QT - 1:
            S.dma("sp", attT_d.ap()[h], atth[hs], reads=[b_atth[hs]], writes=[b_att])

    load_head(0)
    emit_S(0)
    pending = []
    for i, (h, qt, kc) in enumerate(iters):
        hs = h % 2
        if qt == 0 and kc == 0 and h + 1 < NH:
            load_head(h + 1)
        if i + 1 < len(iters):
            emit_S(i + 1)
        os_ = qt % 2
        pS, bS = PS[i % 2], b_ps[i % 2]
        pO, bO = PS[2 + os_], b_ps[2 + os_]
        pZ, bZ = PS[4 + os_], b_ps[4 + os_]
        es = i % 3
        S.op("act", lambda e, es=es, pS=pS: e.activation(out=E[es][:, 0:W2], in_=pS[:, 0:W2], func=AF.Exp), reads=[bS], writes=[b_E[es]])
        S.op("pe", lambda e, pO=pO, hs=hs, kc=kc, es=es: e.matmul(pO[:, 0:W2], lhsT=vh[hs][:, kc, :], rhs=E[es][:, 0:W2],
                                                                 start=(kc == 0), stop=(kc == 31)),
             reads=[b_hd_[hs], b_E[es]], writes=[bO])
        S.op("pe", lambda e, pZ=pZ, es=es, kc=kc: e.matmul(pZ[:, 0:W2], lhsT=ones[:, :], rhs=E[es][:, 0:W2],
                                                          start=(kc == 0), stop=(kc == 31)),
             reads=[b_const, b_E[es]], writes=[bZ])
        if pending and pending[0][0] <= i:
            _, hh, qq = pending.pop(0)
            emit_post2(hh, qq)
        if kc == 31:
            emit_post1(h, qt)
            pending.append((i + 3, h, qt))
    for _, hh, qq in pending:
        emit_post2(hh, qq)

    b_mrg = Buf("mrg")
    b_gsig_r = b_gsig
    b_stash = Buf("stash")
    new_stage()
    NT9 = LQ // TQ
    aT9 = [ABF.take(8 * LQ, "p (k t) -> p k t", t=LQ) for _ in range(2)]
    b_aT9 = [Buf() for _ in range(2)]
    S.dma("sp", aT9[0], attT_d.ap().rearrange("k p t -> p k t"), reads=[b_att], writes=[b_aT9[0]])
    S.dma("sp", aT9[1], yhyT_d.ap().rearrange("k p t -> p k t"), reads=[b_yhy], writes=[b_aT9[1]])
    w9 = [[ABF.take(8 * 512, "p (k c) -> p k c", c=512) for _ in range(2)] for _ in range(2)]
    b_w9 = [[Buf() for _ in range(2)] for _ in range(2)]
    NS9 = 6
    g9 = [[AFF.take(TQ) for _ in range(NS9)] for _ in range(2)]
    b_g9 = [[Buf() for _ in range(NS9)] for _ in range(2)]
    t9 = [[AFF.take(TQ) for _ in range(2)] for _ in range(2)]
    b_t9 = [[Buf() for _ in range(2)] for _ in range(2)]
    o9 = [ABF.take(TQ) for _ in range(3)]
    b_o9 = [Buf() for _ in range(3)]
    order9 = [(cg, tt, cc) for cg in range(4) for tt in range(NT9) for cc in range(4)]
    issued9 = [0]

    def prefetch9(upto):
        while issued9[0] <= min(upto, len(order9) - 1):
            n = issued9[0]
            cg_, tt_, cc_ = order9[n]
            ci = cg_ * 4 + cc_
            for br in range(2):
                S.dma("sp", g9[br][n % NS9], gsig_d.ap()[br * 16 + ci, :, tt_ * TQ:(tt_ + 1) * TQ], reads=[b_gsig],
                      writes=[b_g9[br][n % NS9]])
            issued9[0] += 1

    def load_w9(cg):
        for br, w_t in enumerate((wa_t, wh_t)):
            S.dma("pool", w9[br][cg % 2], w_t.ap()[:, cg * 512:(cg + 1) * 512].rearrange("(k p) c -> p k c", p=128),
                  writes=[b_w9[br][cg % 2]])

    load_w9(0)
    prefetch9(2)
    for n, (cg, tt, cc) in enumerate(order9):
        if tt == 0 and cc == 0 and cg + 1 < 4:
            load_w9(cg + 1)
        prefetch9(n + 3)
        ci = cg * 4 + cc
        pp = n % 2
        for br in range(2):
            ps, pb = PS[2 * pp + br], b_ps[2 * pp + br]
            for k in range(8):
                S.op("pe", lambda e, ps=ps, br=br, cg=cg, k=k, cc=cc, tt=tt: e.matmul(
                    ps[:, 0:TQ], lhsT=w9[br][cg % 2][:, k, cc * 128:(cc + 1) * 128], rhs=aT9[br][:, k, tt * TQ:(tt + 1) * TQ],
                    start=(k == 0), stop=(k == 7)), reads=[b_w9[br][cg % 2], b_aT9[br]], writes=[pb])
        for br in range(2):
            ps, pb = PS[2 * pp + br], b_ps[2 * pp + br]
            S.op("dve", lambda e, ps=ps, br=br, pp=pp, n=n: e.tensor_tensor(out=t9[pp][br], in0=ps[:, 0:TQ], in1=g9[br][n % NS9], op=ALU.mult),
                 reads=[pb, b_g9[br][n % NS9]], writes=[b_t9[pp][br]])
        S.op("pool", lambda e, pp=pp, n=n: e.tensor_tensor(out=o9[n % 3], in0=t9[pp][0], in1=t9[pp][1], op=ALU.add),
             reads=[b_t9[pp][0], b_t9[pp][1]], writes=[b_o9[n % 3]])
        S.dma("sp", mrgT_d.ap()[ci, :, tt * TQ:(tt + 1) * TQ], o9[n % 3], reads=[b_o9[n % 3]], writes=[b_mrg])

    b_xmid = Buf("xmid")

    def resid_stage(src_t, src_buf, w_t, kc, res_t, res_buf, dst_t, dst_buf, cgw, at_slots, TT=512, at_ap=None):
        cnt = [0]
        slots = {}

        NSR = 6
        orderR = [(cg, tt * (TT // 128) + tb) for cg in range(D // cgw) for tt in range(LQ // TT) for tb in range(TT // 128)]
        issued = [0]

        def pro():
            slots["r"], slots["rb"] = mk_out_slots(NSR, AFF, cgw)

        def prefetch(upto):
            while issued[0] <= min(upto, len(orderR) - 1):
                n = issued[0]
                cg_, tb_ = orderR[n]
                S.dma("sp", slots["r"][n % NSR], res_t.ap()[tb_ * 128:(tb_ + 1) * 128, cg_ * cgw:(cg_ + 1) * cgw],
                      reads=[res_buf], writes=[slots["rb"][n % NSR]])
                issued[0] += 1

        def evac(cg, tb, ps, pb):
            n = cnt[0]
            cnt[0] += 1
            assert orderR[n] == (cg, tb)
            prefetch(n + 3)
            r, rb = slots["r"][n % NSR], slots["rb"][n % NSR]
            S.op("dve", lambda e: e.tensor_tensor(out=r, in0=ps, in1=r, op=ALU.add), reads=[pb, rb], writes=[rb])
            S.dma("sp", dst_t.ap()[tb * 128:(tb + 1) * 128, cg * cgw:(cg + 1) * cgw], r, reads=[rb], writes=[dst_buf])

        linear([(src_t, src_buf, w_t, 0, kc)], D, "tm", evac, cgw=cgw, at_slots=at_slots, prologue=pro, TT=TT, T=LQ, resident=(kc <= 16), at_ap=at_ap)

    resid_stage(mrgT_d, b_mrg, wo_t, 16, xmy_t, b_xmy, xmid_d, b_xmid, 512, 2, TT=TQ)

    b_hfT = Buf("hfT")
    norm_T(xmid_d, g_ffn_t, hfT_d, b_hfT, b_xmid, nrows=LQ, mask=mask_t)
    b_act = Buf("act")
    new_stage()
    NT11 = LQ // TQ
    fcw = AFF.take(88 * 4, "p (c j) -> p c j", j=4)
    S.dma("sp", fcw, ffcw_t.ap(), writes=[b_cw])
    hf11 = ABF.take(16 * LQ, "p (k t) -> p k t", t=LQ)
    b_hf11 = Buf()
    S.dma("sp", hf11, hfT_d.ap().rearrange("k p t -> p k t"), reads=[b_hfT], writes=[b_hf11])
    w11 = [ABF.take(16 * 512, "p (k c) -> p k c", c=512) for _ in range(2)]
    b_w11 = [Buf() for _ in range(2)]
    rawb = [[AFF.take(LQ + 2) for _ in range(4)] for _ in range(2)]
    b_rawb = [[Buf() for _ in range(4)] for _ in range(2)]
    for sl_ in range(2):
        for cc_ in range(4):
            S.op("dve", lambda e, sl_=sl_, cc_=cc_: e.memset(rawb[sl_][cc_][:, 0:1], 0.0), writes=[b_rawb[sl_][cc_]])
            S.op("dve", lambda e, sl_=sl_, cc_=cc_: e.memset(rawb[sl_][cc_][:, LQ + 1:LQ + 2], 0.0), writes=[b_rawb[sl_][cc_]])
    cvg = AFF.take(LQ)
    cvv = AFF.take(LQ)
    sil = AFF.take(LQ)
    b_cvg, b_cvv, b_sil = Buf(), Buf(), Buf()
    ao = [ABF.take(LQ) for _ in range(2)]
    b_ao = [Buf() for _ in range(2)]

    def load_w11(cg):
        wv = wup_t.ap()[:, cg * 512:(cg + 1) * 512].rearrange("(k p) c -> p k c", p=128)
        S.dma("pool", w11[cg % 2][:, 0:8, :], wv[:, 0:8, :], writes=[b_w11[cg % 2]])
        S.dma("pool", w11[cg % 2][:, 8:16, :], wv[:, 8:16, :], writes=[b_w11[cg % 2]])

    def conv_full(dst, src, cidx, b_src, b_dst):
        n = LQ
        S.op("dve", lambda e: e.tensor_scalar(out=dst, in0=src[:, 0:n], scalar1=fcw[:, cidx, 0:1], scalar2=fcw[:, cidx, 3:4],
                                              op0=ALU.mult, op1=ALU.add), reads=[b_src, b_cw], writes=[b_dst])
        S.op("dve", lambda e: e.scalar_tensor_tensor(out=dst, in0=src[:, 1:n + 1], scalar=fcw[:, cidx, 1:2], in1=dst,
                                                     op0=ALU.mult, op1=ALU.add), reads=[b_src, b_cw, b_dst], writes=[b_dst])
        S.op("dve", lambda e: e.scalar_tensor_tensor(out=dst, in0=src[:, 2:n + 2], scalar=fcw[:, cidx, 2:3], in1=dst,
                                                     op0=ALU.mult, op1=ALU.add), reads=[b_src, b_cw, b_dst], writes=[b_dst])

    load_w11(0)
    pi11 = 0
    npair = 0
    for cg in range(22):
        sl = cg % 2
        if cg + 1 < 22:
            load_w11(cg + 1)
        for tt in range(NT11):
            for cc in range(4):
                ps, pb = PS[pi11 % 4], b_ps[pi11 % 4]
                pi11 += 1
                for k in range(16):
                    S.op("pe", lambda e, ps=ps, sl=sl, k=k, cc=cc, tt=tt: e.matmul(
                        ps[:, 0:TQ], lhsT=w11[sl][:, k, cc * 128:(cc + 1) * 128], rhs=hf11[:, k, tt * TQ:(tt + 1) * TQ],
                        start=(k == 0), stop=(k == 15)), reads=[b_w11[sl], b_hf11], writes=[pb])
                S.op("act", lambda e, ps=ps, sl=sl, cc=cc, tt=tt: e.activation(out=rawb[sl][cc][:, 1 + tt * TQ:1 + (tt + 1) * TQ],
                                                                              in_=ps[:, 0:TQ], func=AF.Copy),
                     reads=[pb], writes=[b_rawb[sl][cc]])
        for pr in range(2):
            c = cg * 2 + pr
            conv_full(cvg, rawb[sl][2 * pr], c, b_rawb[sl][2 * pr], b_cvg)
            conv_full(cvv, rawb[sl][2 * pr + 1], 44 + c, b_rawb[sl][2 * pr + 1], b_cvv)
            S.op("act", lambda e: e.activation(out=sil, in_=cvg, func=AF.Silu), reads=[b_cvg], writes=[b_sil])
            S.op("pool", lambda e, npair=npair: e.tensor_tensor(out=ao[npair % 2], in0=sil, in1=cvv, op=ALU.mult),
                 reads=[b_sil, b_cvv], writes=[b_ao[npair % 2]])
            S.dma("sp", actT_d.ap()[:, :, c, :].rearrange("b p t -> p b t"), ao[npair % 2].rearrange("p (b t) -> p b t", t=128),
                  reads=[b_ao[npair % 2]], writes=[b_act])
            npair += 1

    b_xout = b_xmid
    resid_stage(actT_d, b_act, wdn_t, 44, xmid_d, b_xmid, xmid_d, b_xout, 512, 2, TT=128, at_ap=lambda si, tt_: actT_d.ap()[tt_])

    b_out = Buf("out")
    norm_T(xmid_d, g_fin_t, None, b_out, b_xout, tok_out=out_t, nrows=NOUT, row_off=2)
    S.barrier()

    with nc.Block() as block:
        @block.tensor
        def _(e):
            for f in S.ops["pe"]:
                f(e)

        @block.scalar
        def _(e):
            for f in S.ops["act"]:
                f(e)

        @block.vector
        def _(e):
            for f in S.ops["dve"]:
                f(e)

        @block.gpsimd
        def _(e):
            for f in S.ops["pool"]:
                f(e)

        @block.sync
        def _(e):
            for f in S.ops["sp"]:
                f(e)
    st.close()
    return nc


_CONST = {}


def _t5_bucket(rel):
    half, max_exact = 16, 8
    ret = (rel > 0).astype(np.int32) * half
    n = np.abs(rel)
    nf = np.maximum(n, 1).astype(np.float32)
    large = max_exact + (np.log(nf / max_exact) / math.log(128 / max_exact) * (half - max_exact)).astype(np.int32)
    large = np.minimum(large, half - 1)
    return ret + np.where(n < max_exact, n, large)


def _constants():
    if _CONST:
        return _CONST
    bf = ml_dtypes.bfloat16
    N = 2 * L
    ang = 2.0 * np.pi * np.arange(N) / N
    ctab = np.cos(ang)
    stab = np.sin(ang)
    g = np.arange(L).reshape(32, 1, 1, 128)
    s = (np.arange(32).reshape(1, 1, 32, 1) * 128 + np.arange(128).reshape(1, 128, 1, 1))
    idx = (g * s) % N
    fc_cos = ctab[idx]
    fc_sin = stab[idx]
    nyq = np.where(s % 2 == 0, 1.0, -1.0)[0, :, :, 0]
    fc_sin[0, :, :, 0] = nyq
    ffwd = np.concatenate([fc_cos, fc_sin], 0).astype(np.float32).astype(bf).reshape(64, 128, 32 * 128)
    gg = (np.arange(32).reshape(1, 1, 32, 1) * 128 + np.arange(128).reshape(1, 128, 1, 1))
    finvs, masks, buckets = [], [], []
    for j in range(4):
        mloc = (np.arange(LQ // TQ).reshape(-1, 1, 1, 1) * TQ + np.arange(TQ).reshape(1, 1, 1, TQ))
        t = 1024 * j - 2 + mloc
        valid = (t >= 0) & (t < L) & (mloc < NOUT + 4)
        tc = np.where(valid, t, 0)
        idx = (gg * tc) % N
        ic = ctab[idx] * (2.0 / N)
        isn = stab[idx] * (2.0 / N)
        ic[:, 0:1, 0:1, :] = 1.0 / N
        isn[:, 0:1, 0:1, :] = (np.where(tc % 2 == 0, 1.0, -1.0) / N)
        ic = ic * valid
        isn = isn * valid
        finvs.append(np.concatenate([ic, isn], 2).astype(np.float32).astype(bf).reshape(LQ // TQ, 128, 64 * TQ))
        mv = valid.reshape(-1).astype(np.float32)
        masks.append(np.ascontiguousarray(mv.reshape(LQ // 128, 128).T))
        rel = np.arange(128).reshape(128, 1) - np.arange(WT_LEN).reshape(1, WT_LEN) + WT_M0 - (1024 * j - 2)
        buckets.append(_t5_bucket(np.clip(rel, -(L - 1), L - 1)))
    f32 = np.float32
    tt_ = np.linspace(0.0, 1.0, L, dtype=f32)[:, None]
    tr = np.arange(L, dtype=f32)[:, None]
    an = (f32(2.0 * math.pi) * tr / f32(L)).astype(f32)
    bands = np.linspace(1e-4, 15, 16, dtype=f32)[None, :]
    emb = np.concatenate([tt_, np.cos(bands * an), -np.sin(bands * an)], axis=-1).astype(f32)
    max_decay = math.log(1e-2) / 0.3
    min_decay = math.log(1e-2) / 1.5
    deltas = np.abs(np.linspace(min_decay, max_decay, CH, dtype=f32)).astype(f32)
    negt = (-tt_[:, 0]).reshape(32, 128).T.copy()
    _CONST.update(dict(c_ffwd=ffwd, finvs=finvs, masks=masks, buckets=buckets, c_embT=np.ascontiguousarray(emb.T),
                       c_delta=deltas.reshape(1, CH), c_negt=negt.astype(f32), c_ident=np.eye(128, dtype=f32).astype(bf)))
    return _CONST


_NC = {}


def kernel(x, g_mix, w_in, lambda_q1, lambda_k1, lambda_q2, lambda_k2, g_subln, rel_bias,
           hy_conv_w, hy_conv_b, hy_f_w1, hy_f_b1, hy_f_w2, hy_f_b2, hy_f_w3, hy_f_b3,
           hy_f_w4, hy_freq, hy_d, w_attn_branch, w_hyena_branch, w_out, g_ffn, w_up,
           ffn_conv_w, ffn_conv_b, w_down, g_final):
    f = lambda a: np.ascontiguousarray(np.asarray(a, dtype=np.float32))
    C = _constants()
    rb = f(rel_bias)
    wt_biases = [np.ascontiguousarray(np.transpose(rb[bk], (2, 0, 1))) for bk in C["buckets"]]
    hcw = f(hy_conv_w)[0]
    hcb = f(hy_conv_b)[0]
    hycw = np.ascontiguousarray(np.concatenate([hcw, hcb[None]], 0).reshape(4, 24, 128).transpose(2, 1, 0))
    fw = f(ffn_conv_w)[0]
    fb = f(ffn_conv_b)[0]
    ffcw = np.ascontiguousarray(np.concatenate([fw, fb[None]], 0).reshape(4, 88, 128).transpose(2, 1, 0))
    common = {
        "g_mix": f(g_mix), "w_in": f(w_in)[0],
        "lam4": np.ascontiguousarray(np.stack([f(lambda_q1)[0], f(lambda_k1)[0], f(lambda_q2)[0], f(lambda_k2)[0]], 0)),
        "g_subln": f(g_subln)[0].reshape(128, 1), "hycw": hycw,
        "fw1": f(hy_f_w1)[0], "fw2": f(hy_f_w2)[0], "fw3": f(hy_f_w3)[0], "fw4": f(hy_f_w4)[0],
        "fvec": np.ascontiguousarray(np.stack([f(hy_f_b1)[0], f(hy_f_b2)[0], f(hy_f_b3)[0], f(hy_freq)[0]], 1)),
        "hyd": np.ascontiguousarray(f(hy_d)[0].reshape(8, 128).T),
        "w_attn_branch": f(w_attn_branch)[0], "w_hyena_branch": f(w_hyena_branch)[0], "w_out": f(w_out)[0],
        "g_ffn": f(g_ffn), "w_up": np.ascontiguousarray(f(w_up)[0].reshape(D, 2, 44, 128).transpose(0, 2, 1, 3).reshape(D, 2 * DFF)), "ffcw": ffcw, "w_down": f(w_down)[0],
        "g_final": f(g_final).reshape(1, D),
    }
    for k in ("c_embT", "c_delta", "c_negt", "c_ffwd", "c_ident"):
        common[k] = C[k]
    xs = f(x)
    in_maps = []
    for c in range(NCORES):
        b, j = divmod(c, 4)
        m = dict(common)
        m["x"] = xs[b]
        xm = np.zeros((LQ, D), np.float32)
        lo, hi = 1024 * j - 2, 1024 * j - 2 + NOUT + 4
        slo, shi = max(lo, 0), min(hi, L)
        xm[slo - lo:shi - lo] = xs[b, slo:shi]
        m["x_my"] = xm
        m["mask_my"] = C["masks"][j]
        m["wt_bias"] = wt_biases[j]
        m["c_finv"] = C["finvs"][j]
        in_maps.append(m)
    if "nc" not in _NC:
        _NC["nc"] = build_program()
    res = run_bass_kernel_spmd(_NC["nc"], in_maps, core_ids=list(range(NCORES)))
    out = np.empty((2, L, D), np.float32)
    for c in range(NCORES):
        b, j = divmod(c, 4)
        out[b, 1024 * j:1024 * (j + 1)] = np.asarray(res.results[c]["out"], dtype=np.float32)
    return out
```

```python
import math
from contextlib import ExitStack
import numpy as np
import ml_dtypes
import concourse.bass as bass
import concourse.mybir as mybir
from concourse.bass_utils import run_bass_kernel_spmd

F32 = mybir.dt.float32
BF = mybir.dt.bfloat16
AF = mybir.ActivationFunctionType
ALU = mybir.AluOpType
AX = mybir.AxisListType

D = 2048
L = 4096
NH = 8
DFF = 5632
CH = 1024
NCORES = 8
LQ = 1152
TQ = 384
NOUT = 1024
LAMBDA_INIT = 0.8 - 0.6 * math.exp(0.0)
WT_M0 = 3968
WT_LEN = 5120


class Buf:
    __slots__ = ("w", "r", "name")

    def __init__(self, name=""):
        self.w = {}
        self.r = {}
        self.name = name


class Sched:
    ENG = ("pe", "act", "dve", "pool", "sp")

    def __init__(self, nc, stack, n_dma_sems=12):
        self.nc = nc
        self.ops = {e: [] for e in self.ENG}
        self.esem = {e: stack.enter_context(nc.semaphore("s_" + e)) for e in self.ENG}
        self.ecnt = {e: 0 for e in self.ENG}
        self.dsem = {}
        self.dcnt = {}
        self.dnext = {}
        for q in ("sp", "pool"):
            self.dsem[q] = [stack.enter_context(nc.semaphore("d_%s%d" % (q, i))) for i in range(n_dma_sems)]
            self.dcnt[q] = [0] * n_dma_sems
            self.dnext[q] = 0
        self.waited = {e: {} for e in self.ENG}
        self.allsems = {}

    def _waits(self, eng, deps):
        out = []
        wd = self.waited[eng]
        for key, (sem, val) in deps.items():
            if wd.get(key, 0) < val:
                wd[key] = val
                out.append((sem, val))
        return out

    @staticmethod
    def _merge(dst, src):
        for k, (s, v) in src.items():
            if k not in dst or dst[k][1] < v:
                dst[k] = (s, v)

    def _deps(self, reads, writes):
        deps = {}
        for b in reads:
            self._merge(deps, b.w)
        for b in writes:
            self._merge(deps, b.w)
            self._merge(deps, b.r)
        return deps

    def _mark(self, reads, writes, key, tok):
        for b in writes:
            b.w[key] = tok
        for b in reads:
            b.r[key] = tok
        self.allsems[key] = tok

    def op(self, eng, fn, reads=(), writes=()):
        deps = self._deps(reads, writes)
        if eng == "pe":
            deps.pop("e_pe", None)
        waits = self._waits(eng, deps)
        sem = self.esem[eng]
        self.ecnt[eng] += 1
        val = self.ecnt[eng]

        def run(e, waits=waits, fn=fn, sem=sem):
            for s, v in waits:
                e.wait_ge(s, v)
            fn(e).then_inc(sem, 1)

        self.ops[eng].append(run)
        key = "e_" + eng
        self._mark(reads, writes, key, (sem, val))

    def dma(self, q, out, in_, reads=(), writes=(), slow=False):
        deps = self._deps(reads, writes)
        i = self.dnext[q]
        self.dnext[q] = (i + 1) % len(self.dsem[q])
        sem = self.dsem[q][i]
        key = "d_%s%d" % (q, i)
        if self.dcnt[q][i] > 0:
            self._merge(deps, {key: (sem, self.dcnt[q][i])})
        waits = self._waits(q, deps)
        self.dcnt[q][i] += 16
        val = self.dcnt[q][i]

        def run(e, waits=waits, out=out, in_=in_, sem=sem, slow=slow):
            for s, v in waits:
                e.wait_ge(s, v)
            if slow:
                e.dma_start(out=out, in_=in_, allow_slow_non_contiguous=True).then_inc(sem, 16)
            else:
                e.dma_start(out=out, in_=in_).then_inc(sem, 16)

        self.ops[q].append(run)
        self._mark(reads, writes, key, (sem, val))

    def barrier(self, engs=None):
        for eng in (engs or self.ENG):
            waits = self._waits(eng, dict(self.allsems))
            if waits:
                def run(e, waits=waits):
                    for s, v in waits:
                        e.wait_ge(s, v)
                self.ops[eng].append(run)


class Arena:
    def __init__(self, tensor, n, base=0):
        self.t = tensor
        self.n = n
        self.off = 0
        self.base = base

    def reset(self):
        self.off = 0

    def take(self, n, pattern=None, **kw):
        n_al = (n + 15) // 16 * 16
        assert self.off + n_al <= self.n, ("arena overflow", self.off, n, self.n)
        ap = self.t[:, self.base + self.off:self.base + self.off + n]
        self.off += n_al
        if pattern:
            ap = ap.rearrange(pattern, **kw)
        return ap


def dram_bc(ap1d_tensor, offset, n, parts=128):
    return bass.AP(ap1d_tensor, offset, [[0, parts], [1, n]])


def build_program():
    nc = bass.Bass("TRN2", target_bir_lowering=False)
    st = ExitStack()

    def din(name, shape, dt=F32):
        return nc.dram_tensor(name, list(shape), dt, kind="ExternalInput")

    def dscr(name, shape, dt):
        return nc.dram_tensor(name, list(shape), dt)

    x_t = din("x", [L, D])
    xmy_t = din("x_my", [LQ, D])
    mask_t = din("mask_my", [128, LQ // 128])
    g_mix_t = din("g_mix", [1, D])
    w_in_t = din("w_in", [D, 10240])
    lam_t = din("lam4", [4, 64])
    g_subln_t = din("g_subln", [128, 1])
    wt_t = din("wt_bias", [NH, 128, WT_LEN])
    hycw_t = din("hycw", [128, 24, 4])
    fw1_t = din("fw1", [33, 64])
    fw2_t = din("fw2", [64, 64])
    fw3_t = din("fw3", [64, 64])
    fw4_t = din("fw4", [64, 2048])
    fvec_t = din("fvec", [64, 4])
    hyd_t = din("hyd", [128, 8])
    wa_t = din("w_attn_branch", [1024, D])
    wh_t = din("w_hyena_branch", [1024, D])
    wo_t = din("w_out", [D, D])
    g_ffn_t = din("g_ffn", [1, D])
    wup_t = din("w_up", [D, 2 * DFF])
    ffcw_t = din("ffcw", [128, 88, 4])
    wdn_t = din("w_down", [DFF, D])
    g_fin_t = din("g_final", [1, D])
    embT_t = din("c_embT", [33, L])
    delta_t = din("c_delta", [1, CH])
    negt_t = din("c_negt", [128, 32])
    ffwd_t = din("c_ffwd", [64, 128, 32 * 128], BF)
    finv_t = din("c_finv", [LQ // TQ, 128, 64 * TQ], BF)
    ident_t = din("c_ident", [128, 128], BF)
    out_t = nc.dram_tensor("out", [NOUT, D], F32, kind="ExternalOutput")

    hT_d = dscr("hT_d", [16, 128, L], BF)
    qT_d = dscr("qT_d", [8, 128, LQ], BF)
    hTm_d = dscr("hTm_d", [16, 128, LQ], BF)
    kT_d = dscr("kT_d", [8, 128, L], BF)
    v_d = dscr("v_d", [L, 1024], BF)
    hyraw_d = dscr("hyraw_d", [16, 128, L + 2], F32)
    hyrawm_d = dscr("hyrawm_d", [24, 128, LQ + 2], F32)
    gsig_d = dscr("gsig_d", [32, 128, LQ], F32)
    x0c_d = dscr("x0c_d", [8, 128, LQ], F32)
    uT_d = dscr("uT_d", [8, 128, LQ], F32)
    sig_d = dscr("sig_d", [L, 3072], BF)
    spec_d = dscr("spec_d", [4, 32, 128, CH], F32)
    Y_d = dscr("Y_d", [8, 128, 64, 128], BF)
    yhyT_d = dscr("yhyT_d", [8, 128, LQ], BF)
    attT_d = dscr("attT_d", [8, 128, LQ], BF)
    mrgT_d = dscr("mrgT_d", [16, 128, LQ], BF)
    xmid_d = dscr("xmid_d", [LQ, D], F32)
    hfT_d = dscr("hfT_d", [16, 128, LQ], BF)
    upraw_d = dscr("upraw_d", [88, 128, LQ + 2], F32)
    actT_d = dscr("actT_d", [LQ // 128, 128, 44, 128], BF)

    NBF = 57344
    NF = 16384
    a_bf_t = st.enter_context(nc.sbuf_tensor("a_bf", [128, NBF], BF))
    a_f_t = st.enter_context(nc.sbuf_tensor("a_f", [128, NF], F32))
    ident = st.enter_context(nc.sbuf_tensor("ident", [128, 128], BF))
    ones = st.enter_context(nc.sbuf_tensor("ones", [128, 128], BF))
    cst = st.enter_context(nc.sbuf_tensor("cst", [128, 16], F32))
    PS = [st.enter_context(nc.psum_tensor("ps%d" % i, [128, 512], F32)) for i in range(7)]
    PSB = st.enter_context(nc.psum_tensor("psb", [128, 1024], BF))
    S = Sched(nc, st)
    ABF = Arena(a_bf_t, NBF)
    AFF = Arena(a_f_t, NF)
    b_ps = [Buf("ps%d" % i) for i in range(7)]
    b_psb = Buf("psb")
    b_const = Buf("const")

    def new_stage():
        S.barrier()
        ABF.reset()
        AFF.reset()

    S.op("dve", lambda e: e.memset(ones[:, :], 1.0), writes=[b_const])
    S.op("dve", lambda e: e.memset(cst[:, 0:1], 1e-6), writes=[b_const])
    S.op("dve", lambda e: e.memset(cst[:, 1:2], 1e-5), writes=[b_const])
    S.op("dve", lambda e: e.memset(cst[:, 2:3], -math.pi), writes=[b_const])
    S.op("dve", lambda e: e.memset(cst[:, 3:4], 0.0), writes=[b_const])
    S.dma("sp", ident[:, :], ident_t.ap(), writes=[b_const])

    def norm_T(src_t, g_t, dst_t, dst_buf, src_buf, tok_out=None, nrows=L, row_off=0, mask=None, interleave=None):
        new_stage()
        gbc = AFF.take(D)
        b_g = Buf()
        S.dma("sp", gbc, dram_bc(g_t, 0, D), writes=[b_g])
        xt = [AFF.take(D) for _ in range(2)]
        b_xt = [Buf() for _ in range(2)]
        junk = ABF.take(D)
        b_junk = Buf()
        st_ = [AFF.take(4) for _ in range(2)]
        b_st = [Buf() for _ in range(2)]
        G = 4 if nrows % 512 == 0 else 3
        mk = None
        if mask is not None:
            mk = AFF.take(16)
            S.dma("sp", mk[:, 0:LQ // 128], mask.ap(), writes=[b_g])
        if tok_out is None:
            hb = [ABF.take(D) for _ in range(2)]
            hTt = [ABF.take(16 * G * 128, "p (k t) -> p k t", t=G * 128) for _ in range(2)]
            b_hT = [Buf() for _ in range(2)]
        else:
            hb = [AFF.take(D) for _ in range(2)]
        b_hb = [Buf() for _ in range(2)]
        src = src_t.ap()
        for i in range(nrows // 128):
            s = i % 2
            S.dma("sp", xt[s], src[row_off + i * 128:row_off + (i + 1) * 128, :], reads=[src_buf], writes=[b_xt[s]])
            S.op("act", lambda e, s=s: e.activation(out=junk, in_=xt[s], func=AF.Square, accum_out=st_[s][:, 0:1]),
                 reads=[b_xt[s]], writes=[b_junk, b_st[s]])
            S.op("act", lambda e, s=s: e.activation(out=st_[s][:, 1:2], in_=st_[s][:, 0:1], func=AF.Sqrt,
                                                    bias=cst[:, 0:1], scale=1.0 / D),
                 reads=[b_st[s], b_const], writes=[b_st[s]])
            S.op("dve", lambda e, s=s: e.reciprocal(out=st_[s][:, 2:3], in_=st_[s][:, 1:2]),
                 reads=[b_st[s]], writes=[b_st[s]])
            if mk is not None:
                S.op("dve", lambda e, s=s, i=i: e.tensor_tensor(out=st_[s][:, 2:3], in0=st_[s][:, 2:3], in1=mk[:, i:i + 1], op=ALU.mult),
                     reads=[b_st[s], b_g], writes=[b_st[s]])
            S.op("dve", lambda e, s=s: e.scalar_tensor_tensor(out=hb[s], in0=xt[s], scalar=st_[s][:, 2:3], in1=gbc,
                                                              op0=ALU.mult, op1=ALU.mult),
                 reads=[b_xt[s], b_st[s], b_g], writes=[b_hb[s]])
            if tok_out is not None:
                S.dma("sp", tok_out.ap()[i * 128:(i + 1) * 128, :], hb[s], reads=[b_hb[s]], writes=[dst_buf])
                continue
            if interleave:
                interleave()
            g4 = i // G
            hs = g4 % 2
            tb = i % G
            for half in range(2):
                for kk in range(8):
                    k = half * 8 + kk
                    S.op("pe", lambda e, s=s, k=k, kk=kk: e.transpose(out=PSB[:, kk * 128:(kk + 1) * 128],
                                                                     in_=hb[s][:, k * 128:(k + 1) * 128],
                                                                     identity=ident[:, :]),
                         reads=[b_hb[s], b_const], writes=[b_psb])
                S.op("act", lambda e, hs=hs, half=half, tb=tb: e.activation(
                    out=hTt[hs][:, half * 8:(half + 1) * 8, tb * 128:(tb + 1) * 128],
                    in_=PSB[:, :].rearrange("p (k t) -> p k t", t=128), func=AF.Copy),
                    reads=[b_psb], writes=[b_hT[hs]])
            if tb == G - 1:
                S.dma("sp", dst_t.ap()[:, :, g4 * G * 128:(g4 + 1) * G * 128].rearrange("k p t -> p k t"), hTt[hs],
                      reads=[b_hT[hs]], writes=[dst_buf])

    lin_cache = {}

    def linear(srcs, ncols, mode, evac, cgw=512, TT=512, at_slots=2, prologue=None, T=L, resident=False, chain=False, at_ap=None):
        KCs = [s_[4] for s_ in srcs]
        if resident:
            at_slots = T // TT
        key = (tuple(KCs), cgw, TT, at_slots, T, resident, mode, tuple(s_[0].name for s_ in srcs))
        already_resident = False
        if chain and key in lin_cache:
            wts, b_wt, ats, b_at = lin_cache[key]
            already_resident = resident
            if prologue:
                prologue()
        else:
            new_stage()
            lin_cache.clear()
            if prologue:
                prologue()
            wts = [[ABF.take(kc * cgw, "p (k c) -> p k c", c=cgw) for _ in range(2)] for kc in KCs]
            b_wt = [[Buf() for _ in range(2)] for _ in KCs]
            ats = [[ABF.take(kc * TT, "p (k t) -> p k t", t=TT) for _ in range(at_slots)] for kc in KCs]
            b_at = [[Buf() for _ in range(at_slots)] for _ in KCs]
            lin_cache[key] = (wts, b_wt, ats, b_at)
        ncg = ncols // cgw
        ntt = T // TT
        pi = 0
        it = 0
        def load_w(cg):
            ws = cg % 2
            for si, (a_t, a_buf, w_t, c0, kc) in enumerate(srcs):
                wv = w_t.ap()[:, c0 + cg * cgw:c0 + (cg + 1) * cgw].rearrange("(k p) c -> p k c", p=128)
                half = (kc + 1) // 2
                S.dma("pool", wts[si][ws][:, 0:half, :], wv[:, 0:half, :], writes=[b_wt[si][ws]])
                if half < kc:
                    S.dma("pool", wts[si][ws][:, half:kc, :], wv[:, half:kc, :], writes=[b_wt[si][ws]])

        def load_at(n):
            tt_ = n % ntt
            sl = n % at_slots
            for si, (a_t, a_buf, w_t, c0, kc) in enumerate(srcs):
                src_ap = at_ap(si, tt_) if at_ap else a_t.ap()[:, :, tt_ * TT:(tt_ + 1) * TT].rearrange("k p t -> p k t")
                S.dma("sp", ats[si][sl], src_ap, reads=[a_buf], writes=[b_at[si][sl]])

        load_w(0)
        for cg in range(ncg):
            ws = cg % 2
            if cg + 1 < ncg:
                load_w(cg + 1)
            for tt in range(ntt):
                as_ = it % at_slots
                if resident:
                    if it == 0 and not already_resident:
                        for n_ in range(ntt):
                            load_at(n_)
                else:
                    if it == 0:
                        load_at(0)
                    if at_slots > 1 and it + 1 < ncg * ntt:
                        load_at(it + 1)
                    elif at_slots == 1 and it > 0:
                        load_at(it)
                it += 1
                if mode == "fm":
                    for cc in range(cgw // 128):
                        ps = PS[pi % 4]
                        pb = b_ps[pi % 4]
                        pi += 1
                        n_mm = sum(KCs)
                        j = 0
                        for si, kc in enumerate(KCs):
                            for k in range(kc):
                                S.op("pe", lambda e, si=si, ws=ws, as_=as_, k=k, cc=cc, j=j, n_mm=n_mm, ps=ps:
                                     e.matmul(ps[:, 0:TT], lhsT=wts[si][ws][:, k, cc * 128:(cc + 1) * 128],
                                              rhs=ats[si][as_][:, k, :], start=(j == 0), stop=(j == n_mm - 1)),
                                     reads=[b_wt[si][ws], b_at[si][as_]], writes=[pb])
                                j += 1
                        evac(cg * (cgw // 128) + cc, tt, ps[:, 0:TT], pb)
                else:
                    for tb in range(TT // 128):
                        ps = PS[pi % 4]
                        pb = b_ps[pi % 4]
                        pi += 1
                        n_mm = sum(KCs)
                        j = 0
                        for si, kc in enumerate(KCs):
                            for k in range(kc):
                                S.op("pe", lambda e, si=si, ws=ws, as_=as_, k=k, tb=tb, j=j, n_mm=n_mm, ps=ps:
                                     e.matmul(ps[:, 0:cgw], lhsT=ats[si][as_][:, k, tb * 128:(tb + 1) * 128],
                                              rhs=wts[si][ws][:, k, :], start=(j == 0), stop=(j == n_mm - 1)),
                                     reads=[b_wt[si][ws], b_at[si][as_]], writes=[pb])
                                j += 1
                        evac(cg, tt * (TT // 128) + tb, ps[:, 0:cgw], pb)

    b_sig = Buf("sig")
    new_stage()
    embT = [AFF.take(512) for _ in range(2)]
    b_emb = [Buf() for _ in range(2)]
    w1 = AFF.take(64)
    w2 = AFF.take(64)
    w3 = AFF.take(64)
    w4 = AFF.take(2048)
    fv = AFF.take(8)
    dl = AFF.take(CH)
    ngt = AFF.take(32)
    b_f = Buf()
    S.dma("sp", w1[0:33, :], fw1_t.ap(), writes=[b_f])
    S.dma("sp", w2[0:64, :], fw2_t.ap(), writes=[b_f])
    S.dma("sp", w3[0:64, :], fw3_t.ap(), writes=[b_f])
    S.dma("sp", w4[0:64, :], fw4_t.ap(), writes=[b_f])
    S.dma("sp", fv[0:64, 0:4], fvec_t.ap(), writes=[b_f])
    S.dma("sp", dl, dram_bc(delta_t, 0, CH), writes=[b_f])
    S.dma("sp", ngt, negt_t.ap(), writes=[b_f])
    for j in range(3):
        S.op("dve", lambda e, j=j: e.tensor_tensor(out=fv[0:64, 4 + j:5 + j], in0=fv[0:64, j:j + 1], in1=fv[0:64, 3:4],
                                                   op=ALU.mult), reads=[b_f], writes=[b_f])
    hcur = [AFF.take(512) for _ in range(2)]
    b_h = [Buf() for _ in range(2)]
    H3 = AFF.take(L)
    b_H3 = Buf()
    arg = AFF.take(512)
    b_arg = Buf()
    sA = AFF.take(512)
    sB = AFF.take(512)
    b_sA, b_sB = Buf(), Buf()
    for pt in range(8):
        S.dma("sp", embT[pt % 2][0:33, :], embT_t.ap()[:, pt * 512:(pt + 1) * 512], writes=[b_emb[pt % 2]])
        srcs_ = [(embT[pt % 2][0:33, :], w1[0:33, 0:64]), None, None]
        for ly in range(3):
            ps = PS[(pt * 3 + ly) % 4]
            pb = b_ps[(pt * 3 + ly) % 4]
            if ly == 0:
                rhs, lhsT = srcs_[0]
                rb = b_emb[pt % 2]
            else:
                rhs = hcur[(ly - 1) % 2][0:64, :]
                lhsT = (w2 if ly == 1 else w3)[0:64, 0:64]
                rb = b_h[(ly - 1) % 2]
            S.op("pe", lambda e, ps=ps, lhsT=lhsT, rhs=rhs: e.matmul(ps[0:64, :], lhsT=lhsT, rhs=rhs, start=True, stop=True),
                 reads=[b_f, rb], writes=[pb])
            S.op("dve", lambda e, ps=ps, ly=ly: e.tensor_scalar(out=arg[0:64, :], in0=ps[0:64, :], scalar1=fv[0:64, 3:4],
                                                                scalar2=fv[0:64, 4 + ly:5 + ly], op0=ALU.mult, op1=ALU.add),
                 reads=[pb, b_f], writes=[b_arg])
            if ly < 2:
                dst, db = hcur[ly % 2][0:64, :], b_h[ly % 2]
            else:
                dst, db = H3[0:64, pt * 512:(pt + 1) * 512], b_H3
            S.op("act", lambda e: e.activation(out=sA[0:64, :], in_=arg[0:64, :], func=AF.Sin, scale=0.5),
                 reads=[b_arg], writes=[b_sA])
            S.op("act", lambda e: e.activation(out=sB[0:64, :], in_=arg[0:64, :], func=AF.Sin, scale=0.25),
                 reads=[b_arg], writes=[b_sB])
            S.op("dve", lambda e: e.tensor_tensor(out=sB[0:64, :], in0=sB[0:64, :], in1=sB[0:64, :], op=ALU.mult),
                 reads=[b_sB], writes=[b_sB])
            S.op("dve", lambda e: e.tensor_scalar(out=sB[0:64, :], in0=sB[0:64, :], scalar1=-4.0, scalar2=2.0,
                                                  op0=ALU.mult, op1=ALU.add), reads=[b_sB], writes=[b_sB])
            S.op("dve", lambda e, dst=dst: e.tensor_tensor(out=dst, in0=sA[0:64, :], in1=sB[0:64, :], op=ALU.mult),
                 reads=[b_sA, b_sB], writes=[db])
    dec = [AFF.take(CH) for _ in range(2)]
    b_dec = [Buf() for _ in range(2)]
    hfb = [AFF.take(2048)] * 2
    b_hfb = [Buf()] * 2
    hpm = [ABF.take(2048) for _ in range(2)]
    b_hpm = [Buf() for _ in range(2)]
    for pc in range(32):
        s = pc % 2
        for ct in range(4):
            S.op("pe", lambda e, pc=pc, ct=ct: e.matmul(PS[ct][:, :], lhsT=H3[0:64, pc * 128:(pc + 1) * 128],
                                                        rhs=w4[0:64, ct * 512:(ct + 1) * 512], start=True, stop=True),
                 reads=[b_H3, b_f], writes=[b_ps[ct]])
        S.op("act", lambda e, s=s, pc=pc: e.activation(out=dec[s], in_=dl, func=AF.Exp, scale=ngt[:, pc:pc + 1]),
             reads=[b_f], writes=[b_dec[s]])
        for ct in range(4):
            S.op("dve", lambda e, s=s, ct=ct: e.tensor_tensor(out=hfb[s][:, ct * 512:(ct + 1) * 512], in0=PS[ct][:, :],
                                                              in1=dec[s][:, (ct % 2) * 512:(ct % 2 + 1) * 512], op=ALU.mult),
                 reads=[b_ps[ct], b_dec[s]], writes=[b_hfb[s]])
        S.op("pool", lambda e, s=s: e.tensor_tensor(out=hpm[s][:, 0:1024], in0=hfb[s][:, 0:1024], in1=hfb[s][:, 1024:2048],
                                                    op=ALU.add), reads=[b_hfb[s]], writes=[b_hpm[s]])
        S.op("pool", lambda e, s=s: e.tensor_tensor(out=hpm[s][:, 1024:2048], in0=hfb[s][:, 0:1024], in1=hfb[s][:, 1024:2048],
                                                    op=ALU.subtract), reads=[b_hfb[s]], writes=[b_hpm[s]])
        S.dma("sp", sig_d.ap()[pc * 128:(pc + 1) * 128, 1024:3072], hpm[s], reads=[b_hpm[s]], writes=[b_sig])

    b_spec = Buf("spec")
    KBF = 28672
    ABF_K = Arena(a_bf_t, KBF, base=NBF - KBF)
    AFF_K = Arena(a_f_t, 2048, base=NF - 2048)

    def fcs_of(ct):
        if ct < 2:
            return list(range(64))
        if ct < 4:
            return list(range(32)) + [32]
        return list(range(32, 64))

    def kpass():
        sig1 = ABF_K.take(32 * 512, "p (s c) -> p s c", c=512)
        b_sig1 = Buf()
        ftk = [ABF_K.take(32 * 128, "p (s g) -> p s g", g=128) for _ in range(3)]
        b_ftk = [Buf() for _ in range(3)]
        sok = [AFF_K.take(512) for _ in range(4)]
        b_sok = [Buf() for _ in range(4)]
        ctsK = (4, 5, 2, 3)
        itsK = [(ct, fc) for ct in ctsK for fc in fcs_of(ct)]

        def load_ftK(n):
            S.dma("sp", ftk[n % 3], ffwd_t.ap()[itsK[n][1]].rearrange("p (s g) -> p s g", g=128), writes=[b_ftk[n % 3]])

        load_ftK(0)
        load_ftK(1)
        for n, (ct, fc) in enumerate(itsK):
            fs = n % 3
            if n + 2 < len(itsK):
                load_ftK(n + 2)
            if fc == fcs_of(ct)[0]:
                S.dma("sp", sig1, sig_d.ap()[:, ct * 512:(ct + 1) * 512].rearrange("(s p) c -> p s c", p=128),
                      reads=[b_sig], writes=[b_sig1])
            ps = PS[n % 4]
            pb = b_ps[n % 4]
            for sc in range(32):
                S.op("pe", lambda e, ps=ps, fs=fs, sc=sc: e.matmul(ps[:, :], lhsT=ftk[fs][:, sc, :], rhs=sig1[:, sc, :],
                                                                  start=(sc == 0), stop=(sc == 31)),
                     reads=[b_ftk[fs], b_sig1], writes=[pb])
            o, ob = sok[n % 4], b_sok[n % 4]
            S.op("act", lambda e, o=o, ps=ps: e.activation(out=o, in_=ps[:, :], func=AF.Copy), reads=[pb], writes=[ob])
            if ct in (2, 3) and fc == 32:
                S.dma("sp", spec_d.ap()[3, 0, 0:1, (ct % 2) * 512:(ct % 2 + 1) * 512], o[0:1, :], reads=[ob], writes=[b_spec])
            else:
                S.dma("sp", spec_d.ap()[2 if ct < 4 else 3, fc % 32, :, (ct % 2) * 512:(ct % 2 + 1) * 512], o, reads=[ob], writes=[b_spec])
            yield

    kgen = kpass()

    def k_steps(k=3):
        for _ in range(k):
            next(kgen, None)

    b_x = Buf("x")
    b_hT = Buf("hT")
    norm_T(x_t, g_mix_t, hT_d, b_hT, b_x, interleave=k_steps)
    b_xmy = Buf("xmy")
    b_hTm = Buf("hTm")
    norm_T(xmy_t, g_mix_t, hTm_d, b_hTm, b_xmy, nrows=LQ, interleave=k_steps)
    for _ in kgen:
        pass

    b_q, b_k, b_v, b_hyraw, b_gsig = Buf("q"), Buf("k"), Buf("v"), Buf("hyraw"), Buf("gsig")
    ev = {}

    def mk_out_slots(n, dt_arena, width):
        tiles = [dt_arena.take(width) for _ in range(n)]
        bufs = [Buf() for _ in range(n)]
        return tiles, bufs

    def qk_stage(c0, src_t, src_buf, dst_t, dst_buf, scale, T, TT, chain=False, cgw=512):
        cnt = [0]
        slots = {}

        def pro():
            slots["t"], slots["b"] = mk_out_slots(4, ABF, 512)

        def evac(ci, ti, ps, pb):
            s = cnt[0] % 4
            cnt[0] += 1
            o, ob = slots["t"][s], slots["b"][s]
            S.op("act", lambda e: e.activation(out=o[:, 0:TT], in_=ps, func=AF.Copy, scale=scale), reads=[pb], writes=[ob])
            S.dma("sp", dst_t.ap()[ci, :, ti * TT:(ti + 1) * TT], o[:, 0:TT], reads=[ob], writes=[dst_buf])

        linear([(src_t, src_buf, w_in_t, c0, 16)], 1024, "fm", evac, prologue=pro, T=T, TT=TT, resident=(T == LQ), chain=chain, cgw=cgw)


    def v_stage():
        cnt = [0]
        slots = {}

        def pro():
            slots["t"], slots["b"] = mk_out_slots(4, ABF, 512)

        def evac(cg, tb, ps, pb):
            s = cnt[0] % 4
            cnt[0] += 1
            o, ob = slots["t"][s], slots["b"][s]
            S.op("act", lambda e: e.activation(out=o, in_=ps, func=AF.Copy), reads=[pb], writes=[ob])
            S.dma("sp", v_d.ap()[tb * 128:(tb + 1) * 128, cg * 512:(cg + 1) * 512], o, reads=[ob], writes=[b_v])

        linear([(hT_d, b_hT, w_in_t, 2048, 16)], 1024, "tm", evac, prologue=pro)


    def raw_stage(src_t, src_buf, w_t, c0, ncols, dst_t, dst_buf, func=AF.Copy, pad=1, T=L, TT=512, chain=False, cgw=512):
        cnt = [0]
        slots = {}

        def pro():
            slots["t"], slots["b"] = mk_out_slots(4, AFF, 512)
            if pad:
                z = AFF.take(2)
                bz = Buf()
                S.op("dve", lambda e: e.memset(z, 0.0), writes=[bz])
                nchunks = ncols // 128
                for c in range(nchunks):
                    S.dma("sp", dst_t.ap()[c, :, 0:1], z[:, 0:1], reads=[bz], writes=[dst_buf], slow=True)
                    S.dma("sp", dst_t.ap()[c, :, T + 1:T + 2], z[:, 1:2], reads=[bz], writes=[dst_buf], slow=True)

        def evac(ci, ti, ps, pb):
            s = cnt[0] % 4
            cnt[0] += 1
            o, ob = slots["t"][s], slots["b"][s]
            S.op("act", lambda e: e.activation(out=o[:, 0:TT], in_=ps, func=func), reads=[pb], writes=[ob])
            S.dma("sp", dst_t.ap()[ci, :, pad + ti * TT:pad + (ti + 1) * TT], o[:, 0:TT], reads=[ob], writes=[dst_buf])

        linear([(src_t, src_buf, w_t, c0, 16)], ncols, "fm", evac, prologue=pro, T=T, TT=TT, resident=(T == LQ), chain=chain, cgw=cgw)

    b_hyrawm = Buf("hyrawm")
    qk_stage(1024, hT_d, b_hT, kT_d, b_k, 1.0, L, 512, cgw=1024)
    raw_stage(hT_d, b_hT, w_in_t, 4096, 2048, hyraw_d, b_hyraw, chain=True, cgw=1024)
    v_stage()
    qk_stage(0, hTm_d, b_hTm, qT_d, b_q, 0.125, LQ, TQ)
    raw_stage(hTm_d, b_hTm, w_in_t, 3072, 3072, hyrawm_d, b_hyrawm, T=LQ, TT=TQ, chain=True)
    raw_stage(hTm_d, b_hTm, w_in_t, 6144, 4096, gsig_d, b_gsig, func=AF.Sigmoid, pad=0, T=LQ, TT=TQ, chain=True)

    b_x0c, b_uT = Buf("x0c"), Buf("uT")
    b_cw = Buf()

    def conv3(eng, dst, src, wts_, c, b_src, b_dst, n=512):
        dst = dst[:, 0:n]
        S.op(eng, lambda e: e.tensor_scalar(out=dst, in0=src[:, 0:n], scalar1=wts_[:, c, 0:1], scalar2=wts_[:, c, 3:4],
                                            op0=ALU.mult, op1=ALU.add), reads=[b_src, b_cw], writes=[b_dst])
        S.op(eng, lambda e: e.scalar_tensor_tensor(out=dst, in0=src[:, 1:n + 1], scalar=wts_[:, c, 1:2], in1=dst,
                                                   op0=ALU.mult, op1=ALU.add), reads=[b_src, b_cw, b_dst], writes=[b_dst])
        S.op(eng, lambda e: e.scalar_tensor_tensor(out=dst, in0=src[:, 2:n + 2], scalar=wts_[:, c, 2:3], in1=dst,
                                                   op0=ALU.mult, op1=ALU.add), reads=[b_src, b_cw, b_dst], writes=[b_dst])

    def conv_gen():
        ABF_C = Arena(a_bf_t, 12288, base=NBF - 12288)
        AFF_C = Arena(a_f_t, 8192, base=NF - 8192)
        cw = AFF_C.take(24 * 4, "p (c j) -> p c j", j=4)
        S.dma("sp", cw, hycw_t.ap(), writes=[b_cw])
        raw = [[AFF_C.take(514) for _ in range(3)] for _ in range(2)]
        b_raw = [[Buf() for _ in range(3)] for _ in range(2)]
        cv = [[AFF_C.take(512) for _ in range(3)] for _ in range(2)]
        b_cv = [[Buf() for _ in range(3)] for _ in range(2)]
        ubf = [ABF_C.take(512) for _ in range(2)]
        b_ubf = [Buf() for _ in range(2)]
        utm = [ABF_C.take(4 * 1024, "p (b c) -> p b c", c=1024) for _ in range(2)]
        b_utm = [Buf() for _ in range(2)]
        tiles_a = [(tt, c) for tt in range(8) for c in range(8)]

        def load_a(n):
            tt, c = tiles_a[n]
            for j in (1, 2):
                S.dma("sp", raw[n % 2][j], hyraw_d.ap()[(j - 1) * 8 + c, :, tt * 512:tt * 512 + 514],
                      reads=[b_hyraw], writes=[b_raw[n % 2][j]])

        def finish_a(n):
            tt, c = tiles_a[n]
            s, us = n % 2, tt % 2
            for tb in range(4):
                S.op("pe", lambda e, s=s, tb=tb: e.transpose(out=PSB[:, tb * 128:(tb + 1) * 128],
                                                            in_=ubf[s][:, tb * 128:(tb + 1) * 128], identity=ident[:, :]),
                     reads=[b_ubf[s], b_const], writes=[b_psb])
            S.op("act", lambda e, us=us, c=c: e.activation(out=utm[us][:, :, c * 128:(c + 1) * 128],
                                                           in_=PSB[:, 0:512].rearrange("p (b c) -> p b c", c=128),
                                                           func=AF.Copy), reads=[b_psb], writes=[b_utm[us]])
            if c == 7:
                S.dma("sp", sig_d.ap()[tt * 512:(tt + 1) * 512, 0:1024].rearrange("(b p) c -> p b c", p=128), utm[us],
                      reads=[b_utm[us]], writes=[b_sig])

        load_a(0)
        for n, (tt, c) in enumerate(tiles_a):
            s = n % 2
            if n + 1 < len(tiles_a):
                load_a(n + 1)
            conv3("dve", cv[s][1], raw[s][1], cw, 8 + c, b_raw[s][1], b_cv[s][1])
            conv3("dve", cv[s][2], raw[s][2], cw, 16 + c, b_raw[s][2], b_cv[s][2])
            S.op("pool", lambda e, s=s: e.tensor_tensor(out=ubf[s], in0=cv[s][1], in1=cv[s][2], op=ALU.mult),
                 reads=[b_cv[s][1], b_cv[s][2]], writes=[b_ubf[s]])
            if n > 0:
                finish_a(n - 1)
            yield
        finish_a(len(tiles_a) - 1)
        yield
        it = 0
        for tt in range(LQ // TQ):
            for c in range(8):
                s = it % 2
                it += 1
                for j in range(3):
                    S.dma("sp", raw[s][j][:, 0:TQ + 2], hyrawm_d.ap()[j * 8 + c, :, tt * TQ:tt * TQ + TQ + 2],
                          reads=[b_hyrawm], writes=[b_raw[s][j]])
                conv3("dve", cv[s][0], raw[s][0], cw, c, b_raw[s][0], b_cv[s][0], n=TQ)
                conv3("dve", cv[s][1], raw[s][1], cw, 8 + c, b_raw[s][1], b_cv[s][1], n=TQ)
                conv3("dve", cv[s][2], raw[s][2], cw, 16 + c, b_raw[s][2], b_cv[s][2], n=TQ)
                S.dma("sp", x0c_d.ap()[c, :, tt * TQ:(tt + 1) * TQ], cv[s][0][:, 0:TQ], reads=[b_cv[s][0]], writes=[b_x0c])
                S.op("pool", lambda e, s=s: e.tensor_tensor(out=cv[s][1][:, 0:TQ], in0=cv[s][1][:, 0:TQ], in1=cv[s][2][:, 0:TQ], op=ALU.mult),
                     reads=[b_cv[s][1], b_cv[s][2]], writes=[b_cv[s][1]])
                S.dma("sp", uT_d.ap()[c, :, tt * TQ:(tt + 1) * TQ], cv[s][1][:, 0:TQ], reads=[b_cv[s][1]], writes=[b_uT])
                yield

    b_att = Buf("att")
    new_stage()
    lam = AFF.take(64 * 4 + 8)
    b_lam = Buf()
    for j in range(4):
        S.dma("sp", lam[:, j * 64:(j + 1) * 64], dram_bc(lam_t, j * 64, 64), writes=[b_lam])
    gs = AFF.take(2)
    S.dma("sp", gs[:, 0:1], g_subln_t.ap(), writes=[b_lam])
    S.op("dve", lambda e: e.tensor_scalar(out=gs[:, 1:2], in0=gs[:, 0:1], scalar1=1.0 - LAMBDA_INIT, scalar2=None, op0=ALU.mult),
         reads=[b_lam], writes=[b_lam])
    for j in range(2):
        S.op("dve", lambda e, j=j: e.tensor_tensor(out=lam[:, j * 128:j * 128 + 64], in0=lam[:, j * 128:j * 128 + 64],
                                                   in1=lam[:, j * 128 + 64:j * 128 + 128], op=ALU.mult), reads=[b_lam], writes=[b_lam])
        S.op("dve", lambda e, j=j: e.reduce_sum(out=lam[:, 256 + j:257 + j], in_=lam[:, j * 128:j * 128 + 64], axis=AX.X),
             reads=[b_lam], writes=[b_lam])
        S.op("act", lambda e, j=j: e.activation(out=lam[:, 258 + j:259 + j], in_=lam[:, 256 + j:257 + j], func=AF.Exp),
             reads=[b_lam], writes=[b_lam])
    S.op("dve", lambda e: e.tensor_tensor(out=lam[:, 260:261], in0=lam[:, 259:260], in1=lam[:, 258:259], op=ALU.subtract),
         reads=[b_lam], writes=[b_lam])
    S.op("dve", lambda e: e.tensor_scalar(out=lam[:, 260:261], in0=lam[:, 260:261], scalar1=-LAMBDA_INIT, scalar2=None, op0=ALU.add),
         reads=[b_lam], writes=[b_lam])
    neglam = lam[:, 260:261]

    qh = [ABF.take(2 * LQ, "p (m t) -> p m t", m=2) for _ in range(2)]
    kh = [ABF.take(L) for _ in range(2)]
    vh = [ABF.take(32 * 128, "p (k e) -> p k e", e=128) for _ in range(2)]
    wth = [ABF.take(WT_LEN) for _ in range(2)]
    b_hd_ = [Buf() for _ in range(2)]
    E = [ABF.take(512) for _ in range(3)]
    b_E = [Buf() for _ in range(3)]
    atth = [ABF.take(LQ) for _ in range(2)]
    b_atth = [Buf() for _ in range(2)]
    sq = [ABF.take(256) for _ in range(2)]
    b_sq = [Buf() for _ in range(2)]
    rz = [AFF.take(512) for _ in range(2)]
    o12 = [AFF.take(512) for _ in range(2)]
    at_ = [AFF.take(256) for _ in range(2)]
    rs = [AFF.take(256) for _ in range(2)]
    b_ev = [Buf() for _ in range(2)]
    QT = 192
    W2 = 2 * QT

    b_qz = Buf()
    for hs_ in range(2):
        S.op("dve", lambda e, hs_=hs_: e.memset(qh[hs_][0:64, 1, :], 0.0), writes=[b_qz])
        S.op("dve", lambda e, hs_=hs_: e.memset(qh[hs_][64:128, 0, :], 0.0), writes=[b_qz])

    def load_head(h):
        hs = h % 2
        S.dma("sp", qh[hs][0:64, 0, :], qT_d.ap()[h, 0:64, :], reads=[b_q], writes=[b_hd_[hs]])
        S.dma("sp", qh[hs][64:128, 1, :], qT_d.ap()[h, 64:128, :], reads=[b_q], writes=[b_hd_[hs]])
        S.dma("sp", kh[hs], kT_d.ap()[h], reads=[b_k], writes=[b_hd_[hs]])
        S.dma("sp", vh[hs], v_d.ap()[:, h * 128:(h + 1) * 128].rearrange("(k p) e -> p k e", p=128), reads=[b_v], writes=[b_hd_[hs]])
        S.dma("pool", wth[hs], wt_t.ap()[h], writes=[b_hd_[hs]])

    iters = [(h, qt, kc) for h in range(NH) for qt in range(LQ // QT) for kc in range(32)]

    def is_near(qt, kc):
        return True

    def emit_S(i):
        h, qt, kc = iters[i]
        hs = h % 2
        pS, bS = PS[i % 2], b_ps[i % 2]
        near = is_near(qt, kc)
        c0 = WT_M0 - kc * 128 + qt * QT
        assert 0 <= c0 <= WT_LEN - QT or not near
        S.op("pe", lambda e: e.matmul(
            pS[:, 0:W2].rearrange("p (m q) -> p m q", m=2), lhsT=kh[hs][:, kc * 128:(kc + 1) * 128],
            rhs=qh[hs][:, :, qt * QT:(qt + 1) * QT], start=True, stop=(not near)),
            reads=[b_hd_[hs], b_qz], writes=[bS])
        if near:
            for mp in range(2):
                S.op("pe", lambda e, mp=mp: e.matmul(
                    pS[:, mp * QT:(mp + 1) * QT], lhsT=ident[:, :], rhs=wth[hs][:, c0:c0 + QT], start=False, stop=(mp == 1)),
                    reads=[b_hd_[hs], b_const], writes=[bS])

    def emit_post1(h, qt):
        hs = h % 2
        os_ = qt % 2
        pO, bO = PS[2 + os_], b_ps[2 + os_]
        pZ, bZ = PS[4 + os_], b_ps[4 + os_]
        S.op("dve", lambda e: e.reciprocal(out=rz[os_][:, 0:W2], in_=pZ[:, 0:W2]), reads=[bZ], writes=[b_ev[os_]])
        S.op("dve", lambda e: e.tensor_tensor(out=o12[os_][:, 0:W2], in0=pO[:, 0:W2], in1=rz[os_][:, 0:W2], op=ALU.mult),
             reads=[bO, b_ev[os_]], writes=[b_ev[os_]])
        S.op("dve", lambda e: e.scalar_tensor_tensor(out=at_[os_][:, 0:QT], in0=o12[os_][:, QT:2 * QT], scalar=neglam,
                                                     in1=o12[os_][:, 0:QT], op0=ALU.mult, op1=ALU.add),
             reads=[b_ev[os_], b_lam], writes=[b_ev[os_]])
        S.op("dve", lambda e: e.tensor_tensor(out=sq[os_][:, 0:QT], in0=at_[os_][:, 0:QT], in1=at_[os_][:, 0:QT], op=ALU.mult),
             reads=[b_ev[os_]], writes=[b_sq[os_]])

    def emit_post2(h, qt):
        hs = h % 2
        os_ = qt % 2
        S.op("pe", lambda e: e.matmul(PS[6][:, 0:QT], lhsT=ones[:, :], rhs=sq[os_][:, 0:QT], start=True, stop=True),
             reads=[b_const, b_sq[os_]], writes=[b_ps[6]])
        S.op("act", lambda e: e.activation(out=rs[os_][:, 0:QT], in_=PS[6][:, 0:QT], func=AF.Sqrt, bias=cst[:, 1:2], scale=1.0 / 128),
             reads=[b_ps[6], b_const], writes=[b_ev[os_]])
        S.op("dve", lambda e: e.reciprocal(out=rs[os_][:, 0:QT], in_=rs[os_][:, 0:QT]), reads=[b_ev[os_]], writes=[b_ev[os_]])
        S.op("dve", lambda e: e.scalar_tensor_tensor(out=atth[hs][:, qt * QT:(qt + 1) * QT], in0=at_[os_][:, 0:QT],
                                                     scalar=gs[:, 1:2], in1=rs[os_][:, 0:QT], op0=ALU.mult, op1=ALU.mult),
             reads=[b_ev[os_], b_lam], writes=[b_atth[hs]])
        if qt == LQ // QT - 1:
            S.dma("sp", attT_d.ap()[h], atth[hs], reads=[b_atth[hs]], writes=[b_att])

    load_head(0)
    emit_S(0)
    pending = []
    cgen = conv_gen()
    for i, (h, qt, kc) in enumerate(iters):
        hs = h % 2
        if i % 12 == 6:
            next(cgen, None)
        if qt == 0 and kc == 0 and h + 1 < NH:
            load_head(h + 1)
        if i + 1 < len(iters):
            emit_S(i + 1)
        os_ = qt % 2
        pS, bS = PS[i % 2], b_ps[i % 2]
        pO, bO = PS[2 + os_], b_ps[2 + os_]
        pZ, bZ = PS[4 + os_], b_ps[4 + os_]
        es = i % 3
        S.op("act", lambda e, es=es, pS=pS: e.activation(out=E[es][:, 0:W2], in_=pS[:, 0:W2], func=AF.Exp), reads=[bS], writes=[b_E[es]])
        S.op("pe", lambda e, pO=pO, hs=hs, kc=kc, es=es: e.matmul(pO[:, 0:W2], lhsT=vh[hs][:, kc, :], rhs=E[es][:, 0:W2],
                                                                 start=(kc == 0), stop=(kc == 31)),
             reads=[b_hd_[hs], b_E[es]], writes=[bO])
        S.op("pe", lambda e, pZ=pZ, es=es, kc=kc: e.matmul(pZ[:, 0:W2], lhsT=ones[:, :], rhs=E[es][:, 0:W2],
                                                          start=(kc == 0), stop=(kc == 31)),
             reads=[b_const, b_E[es]], writes=[bZ])
        if pending and pending[0][0] <= i:
            _, hh, qq = pending.pop(0)
            emit_post2(hh, qq)
        if kc == 31:
            emit_post1(h, qt)
            pending.append((i + 3, h, qt))
    for _, hh, qq in pending:
        emit_post2(hh, qq)
    for _ in cgen:
        pass

    b_Y = Buf("Y")
    new_stage()
    sigt = [ABF.take(32 * 512, "p (s c) -> p s c", c=512) for _ in range(2)]
    b_sigt = [Buf() for _ in range(2)]
    ft = [ABF.take(32 * 128, "p (s g) -> p s g", g=128) for _ in range(4)]
    b_ft = [Buf() for _ in range(4)]
    so = [AFF.take(512) for _ in range(4)]
    b_so = [Buf() for _ in range(4)]
    cts = (0, 1)

    def load_sig(cti):
        ct = cts[cti]
        S.dma("sp", sigt[cti % 2], sig_d.ap()[:, ct * 512:(ct + 1) * 512].rearrange("(s p) c -> p s c", p=128),
              reads=[b_sig], writes=[b_sigt[cti % 2]])

    load_sig(0)

    kin = [[AFF.take(512) for _ in range(2)] for _ in range(3)]
    b_kin = [[Buf() for _ in range(2)] for _ in range(3)]
    tq = [[AFF.take(512) for _ in range(4)] for _ in range(2)]
    b_tq = [[Buf() for _ in range(4)] for _ in range(2)]
    yo = [[ABF.take(512) for _ in range(2)] for _ in range(2)]
    b_yo = [[Buf() for _ in range(2)] for _ in range(2)]
    itsU = [(cti, ct, gc) for cti, ct in ((0, 0), (1, 1)) for gc in range(32)]

    def load_U(n):
        cti, ct, gc = itsU[n]
        for j, fc in enumerate((gc, 32 + gc)):
            sl = (n % 2) * 2 + j
            S.dma("sp", ft[sl], ffwd_t.ap()[fc].rearrange("p (s g) -> p s g", g=128), writes=[b_ft[sl]])
        for w_ in range(2):
            S.dma("sp", kin[n % 3][w_], spec_d.ap()[2 + w_, gc, :, ct * 512:(ct + 1) * 512], reads=[b_spec], writes=[b_kin[n % 3][w_]])

    load_U(0)
    for n, (cti, ct, gc) in enumerate(itsU):
        ss_ = cti % 2
        if n + 1 < len(itsU):
            load_U(n + 1)
        if gc == 0 and ct == 0:
            load_sig(1)
        pp = n % 2
        for j in range(2):
            ps, pb = PS[2 * pp + j], b_ps[2 * pp + j]
            sl = pp * 2 + j
            for sc in range(32):
                S.op("pe", lambda e, ps=ps, sl=sl, sc=sc, ss_=ss_: e.matmul(ps[:, :], lhsT=ft[sl][:, sc, :], rhs=sigt[ss_][:, sc, :],
                                                                           start=(sc == 0), stop=(sc == 31)),
                     reads=[b_ft[sl], b_sigt[ss_]], writes=[pb])
        pUc, bUc = PS[2 * pp], b_ps[2 * pp]
        pUs, bUs = PS[2 * pp + 1], b_ps[2 * pp + 1]
        Kc, Ks = kin[n % 3]
        bKc, bKs = b_kin[n % 3]
        t = tq[pp]
        bt = b_tq[pp]
        S.op("dve", lambda e, t=t, pUc=pUc, Kc=Kc: e.tensor_tensor(out=t[0], in0=pUc[:, :], in1=Kc, op=ALU.mult), reads=[bUc, bKc], writes=[bt[0]])
        S.op("dve", lambda e, t=t, pUs=pUs, Ks=Ks: e.tensor_tensor(out=t[1], in0=pUs[:, :], in1=Ks, op=ALU.mult), reads=[bUs, bKs], writes=[bt[1]])
        S.op("dve", lambda e, t=t, pUc=pUc, Ks=Ks: e.tensor_tensor(out=t[2], in0=pUc[:, :], in1=Ks, op=ALU.mult), reads=[bUc, bKs], writes=[bt[2]])
        S.op("dve", lambda e, t=t, pUs=pUs, Kc=Kc: e.tensor_tensor(out=t[3], in0=pUs[:, :], in1=Kc, op=ALU.mult), reads=[bUs, bKc], writes=[bt[3]])
        S.op("pool", lambda e, t=t, pp=pp: e.tensor_tensor(out=yo[pp][0], in0=t[0], in1=t[1], op=ALU.subtract),
             reads=[bt[0], bt[1]], writes=[b_yo[pp][0]])
        S.op("pool", lambda e, t=t, pp=pp: e.tensor_tensor(out=yo[pp][1], in0=t[2], in1=t[3], op=ALU.add),
             reads=[bt[2], bt[3]], writes=[b_yo[pp][1]])
        if gc == 0:
            S.op("pool", lambda e, t=t, pp=pp: e.tensor_copy(out=yo[pp][0][0:1, :], in_=t[0][0:1, :]), reads=[bt[0], b_yo[pp][0]], writes=[b_yo[pp][0]])
            S.op("pool", lambda e, t=t, pp=pp: e.tensor_copy(out=yo[pp][1][0:1, :], in_=t[1][0:1, :]), reads=[bt[1], b_yo[pp][1]], writes=[b_yo[pp][1]])
        S.dma("sp", Y_d.ap()[ct * 4:(ct + 1) * 4, :, gc, :].rearrange("c p e -> p c e"),
              yo[pp][0].rearrange("p (c e) -> p c e", e=128), reads=[b_yo[pp][0]], writes=[b_Y])
        S.dma("sp", Y_d.ap()[ct * 4:(ct + 1) * 4, :, 32 + gc, :].rearrange("c p e -> p c e"),
              yo[pp][1].rearrange("p (c e) -> p c e", e=128), reads=[b_yo[pp][1]], writes=[b_Y])

    b_yhy = Buf("yhy")
    new_stage()
    fv_ = ABF.take(64 * TQ, "p (f t) -> p f t", t=TQ)
    b_fv = Buf()
    yc = [ABF.take(64 * 128, "p (f c) -> p f c", c=128) for _ in range(2)]
    b_yc = [Buf() for _ in range(2)]
    hd = AFF.take(8)
    b_hd = Buf()
    S.dma("sp", hd, hyd_t.ap(), writes=[b_hd])
    xin = [[AFF.take(512) for _ in range(2)] for _ in range(2)]
    b_xin = [[Buf() for _ in range(2)] for _ in range(2)]
    yout = [ABF.take(512) for _ in range(2)]
    b_yout = [Buf() for _ in range(2)]
    it = 0

    def load7(n):
        tt_, c_ = divmod(n, 8)
        sl = n % 2
        S.dma("sp", yc[sl], Y_d.ap()[c_], reads=[b_Y], writes=[b_yc[sl]])
        S.dma("sp", xin[sl][0][:, 0:TQ], uT_d.ap()[c_, :, tt_ * TQ:(tt_ + 1) * TQ], reads=[b_uT], writes=[b_xin[sl][0]])
        S.dma("sp", xin[sl][1][:, 0:TQ], x0c_d.ap()[c_, :, tt_ * TQ:(tt_ + 1) * TQ], reads=[b_x0c], writes=[b_xin[sl][1]])

    for tt in range(LQ // TQ):
        S.dma("sp", fv_, finv_t.ap()[tt].rearrange("p (f t) -> p f t", t=TQ), writes=[b_fv])
        for c in range(8):
            s = it % 2
            if it == 0:
                load7(0)
            if it + 1 < 8 * (LQ // TQ):
                load7(it + 1)
            it += 1
            ps = PS[s]
            pb = b_ps[s]
            for f in range(64):
                S.op("pe", lambda e, ps=ps, s=s, f=f: e.matmul(ps[:, 0:TQ], lhsT=yc[s][:, f, :], rhs=fv_[:, f, :],
                                                               start=(f == 0), stop=(f == 63)),
                     reads=[b_yc[s], b_fv], writes=[pb])
            S.op("dve", lambda e, ps=ps, s=s, c=c: e.scalar_tensor_tensor(out=xin[s][0][:, 0:TQ], in0=xin[s][0][:, 0:TQ], scalar=hd[:, c:c + 1],
                                                                         in1=ps[:, 0:TQ], op0=ALU.mult, op1=ALU.add),
                 reads=[pb, b_xin[s][0], b_hd], writes=[b_xin[s][0]])
            S.op("dve", lambda e, s=s: e.tensor_tensor(out=yout[s][:, 0:TQ], in0=xin[s][0][:, 0:TQ], in1=xin[s][1][:, 0:TQ], op=ALU.mult),
                 reads=[b_xin[s][0], b_xin[s][1]], writes=[b_yout[s]])
            S.dma("sp", yhyT_d.ap()[c, :, tt * TQ:(tt + 1) * TQ], yout[s][:, 0:TQ], reads=[b_yout[s]], writes=[b_yhy])

    b_mrg = Buf("mrg")
    b_gsig_r = b_gsig
    b_stash = Buf("stash")
    new_stage()
    NT9 = LQ // TQ
    aT9 = [ABF.take(8 * LQ, "p (k t) -> p k t", t=LQ) for _ in range(2)]
    b_aT9 = [Buf() for _ in range(2)]
    S.dma("sp", aT9[0], attT_d.ap().rearrange("k p t -> p k t"), reads=[b_att], writes=[b_aT9[0]])
    S.dma("sp", aT9[1], yhyT_d.ap().rearrange("k p t -> p k t"), reads=[b_yhy], writes=[b_aT9[1]])
    w9 = [[ABF.take(8 * 512, "p (k c) -> p k c", c=512) for _ in range(2)] for _ in range(2)]
    b_w9 = [[Buf() for _ in range(2)] for _ in range(2)]
    NS9 = 6
    g9 = [[AFF.take(TQ) for _ in range(NS9)] for _ in range(2)]
    b_g9 = [[Buf() for _ in range(NS9)] for _ in range(2)]
    t9 = [[AFF.take(TQ) for _ in range(2)] for _ in range(2)]
    b_t9 = [[Buf() for _ in range(2)] for _ in range(2)]
    o9 = [ABF.take(TQ) for _ in range(3)]
    b_o9 = [Buf() for _ in range(3)]
    order9 = [(cg, tt, cc) for cg in range(4) for tt in range(NT9) for cc in range(4)]
    issued9 = [0]

    def prefetch9(upto):
        while issued9[0] <= min(upto, len(order9) - 1):
            n = issued9[0]
            cg_, tt_, cc_ = order9[n]
            ci = cg_ * 4 + cc_
            for br in range(2):
                S.dma("sp", g9[br][n % NS9], gsig_d.ap()[br * 16 + ci, :, tt_ * TQ:(tt_ + 1) * TQ], reads=[b_gsig],
                      writes=[b_g9[br][n % NS9]])
            issued9[0] += 1

    def load_w9(cg):
        for br, w_t in enumerate((wa_t, wh_t)):
            S.dma("pool", w9[br][cg % 2], w_t.ap()[:, cg * 512:(cg + 1) * 512].rearrange("(k p) c -> p k c", p=128),
                  writes=[b_w9[br][cg % 2]])

    load_w9(0)
    prefetch9(2)
    for n, (cg, tt, cc) in enumerate(order9):
        if tt == 0 and cc == 0 and cg + 1 < 4:
            load_w9(cg + 1)
        prefetch9(n + 3)
        ci = cg * 4 + cc
        pp = n % 2
        for br in range(2):
            ps, pb = PS[2 * pp + br], b_ps[2 * pp + br]
            for k in range(8):
                S.op("pe", lambda e, ps=ps, br=br, cg=cg, k=k, cc=cc, tt=tt: e.matmul(
                    ps[:, 0:TQ], lhsT=w9[br][cg % 2][:, k, cc * 128:(cc + 1) * 128], rhs=aT9[br][:, k, tt * TQ:(tt + 1) * TQ],
                    start=(k == 0), stop=(k == 7)), reads=[b_w9[br][cg % 2], b_aT9[br]], writes=[pb])
        for br in range(2):
            ps, pb = PS[2 * pp + br], b_ps[2 * pp + br]
            S.op("dve", lambda e, ps=ps, br=br, pp=pp, n=n: e.tensor_tensor(out=t9[pp][br], in0=ps[:, 0:TQ], in1=g9[br][n % NS9], op=ALU.mult),
                 reads=[pb, b_g9[br][n % NS9]], writes=[b_t9[pp][br]])
        S.op("pool", lambda e, pp=pp, n=n: e.tensor_tensor(out=o9[n % 3], in0=t9[pp][0], in1=t9[pp][1], op=ALU.add),
             reads=[b_t9[pp][0], b_t9[pp][1]], writes=[b_o9[n % 3]])
        S.dma("sp", mrgT_d.ap()[ci, :, tt * TQ:(tt + 1) * TQ], o9[n % 3], reads=[b_o9[n % 3]], writes=[b_mrg])

    b_xmid = Buf("xmid")

    def resid_stage(src_t, src_buf, w_t, kc, res_t, res_buf, dst_t, dst_buf, cgw, at_slots, TT=512, at_ap=None):
        cnt = [0]
        slots = {}

        NSR = 6
        orderR = [(cg, tt * (TT // 128) + tb) for cg in range(D // cgw) for tt in range(LQ // TT) for tb in range(TT // 128)]
        issued = [0]

        def pro():
            slots["r"], slots["rb"] = mk_out_slots(NSR, AFF, cgw)

        def prefetch(upto):
            while issued[0] <= min(upto, len(orderR) - 1):
                n = issued[0]
                cg_, tb_ = orderR[n]
                S.dma("sp", slots["r"][n % NSR], res_t.ap()[tb_ * 128:(tb_ + 1) * 128, cg_ * cgw:(cg_ + 1) * cgw],
                      reads=[res_buf], writes=[slots["rb"][n % NSR]])
                issued[0] += 1

        def evac(cg, tb, ps, pb):
            n = cnt[0]
            cnt[0] += 1
            assert orderR[n] == (cg, tb)
            prefetch(n + 3)
            r, rb = slots["r"][n % NSR], slots["rb"][n % NSR]
            S.op("dve", lambda e: e.tensor_tensor(out=r, in0=ps, in1=r, op=ALU.add), reads=[pb, rb], writes=[rb])
            S.dma("sp", dst_t.ap()[tb * 128:(tb + 1) * 128, cg * cgw:(cg + 1) * cgw], r, reads=[rb], writes=[dst_buf])

        linear([(src_t, src_buf, w_t, 0, kc)], D, "tm", evac, cgw=cgw, at_slots=at_slots, prologue=pro, TT=TT, T=LQ, resident=(kc <= 16), at_ap=at_ap)

    resid_stage(mrgT_d, b_mrg, wo_t, 16, xmy_t, b_xmy, xmid_d, b_xmid, 512, 2, TT=TQ)

    b_hfT = Buf("hfT")
    norm_T(xmid_d, g_ffn_t, hfT_d, b_hfT, b_xmid, nrows=LQ, mask=mask_t)
    b_act = Buf("act")
    new_stage()
    NT11 = LQ // TQ
    fcw = AFF.take(88 * 4, "p (c j) -> p c j", j=4)
    S.dma("sp", fcw, ffcw_t.ap(), writes=[b_cw])
    hf11 = ABF.take(16 * LQ, "p (k t) -> p k t", t=LQ)
    b_hf11 = Buf()
    S.dma("sp", hf11, hfT_d.ap().rearrange("k p t -> p k t"), reads=[b_hfT], writes=[b_hf11])
    w11 = [ABF.take(16 * 512, "p (k c) -> p k c", c=512) for _ in range(2)]
    b_w11 = [Buf() for _ in range(2)]
    rawb = [[AFF.take(LQ + 2) for _ in range(4)] for _ in range(2)]
    b_rawb = [[Buf() for _ in range(4)] for _ in range(2)]
    for sl_ in range(2):
        for cc_ in range(4):
            S.op("dve", lambda e, sl_=sl_, cc_=cc_: e.memset(rawb[sl_][cc_][:, 0:1], 0.0), writes=[b_rawb[sl_][cc_]])
            S.op("dve", lambda e, sl_=sl_, cc_=cc_: e.memset(rawb[sl_][cc_][:, LQ + 1:LQ + 2], 0.0), writes=[b_rawb[sl_][cc_]])
    cvg = AFF.take(LQ)
    cvv = AFF.take(LQ)
    sil = AFF.take(LQ)
    b_cvg, b_cvv, b_sil = Buf(), Buf(), Buf()
    ao = [ABF.take(LQ) for _ in range(2)]
    b_ao = [Buf() for _ in range(2)]

    def load_w11(cg):
        wv = wup_t.ap()[:, cg * 512:(cg + 1) * 512].rearrange("(k p) c -> p k c", p=128)
        S.dma("pool", w11[cg % 2][:, 0:8, :], wv[:, 0:8, :], writes=[b_w11[cg % 2]])
        S.dma("pool", w11[cg % 2][:, 8:16, :], wv[:, 8:16, :], writes=[b_w11[cg % 2]])

    def conv_full(dst, src, cidx, b_src, b_dst):
        n = LQ
        S.op("dve", lambda e: e.tensor_scalar(out=dst, in0=src[:, 0:n], scalar1=fcw[:, cidx, 0:1], scalar2=fcw[:, cidx, 3:4],
                                              op0=ALU.mult, op1=ALU.add), reads=[b_src, b_cw], writes=[b_dst])
        S.op("dve", lambda e: e.scalar_tensor_tensor(out=dst, in0=src[:, 1:n + 1], scalar=fcw[:, cidx, 1:2], in1=dst,
                                                     op0=ALU.mult, op1=ALU.add), reads=[b_src, b_cw, b_dst], writes=[b_dst])
        S.op("dve", lambda e: e.scalar_tensor_tensor(out=dst, in0=src[:, 2:n + 2], scalar=fcw[:, cidx, 2:3], in1=dst,
                                                     op0=ALU.mult, op1=ALU.add), reads=[b_src, b_cw, b_dst], writes=[b_dst])

    load_w11(0)
    pi11 = 0
    npair = 0
    for cg in range(22):
        sl = cg % 2
        if cg + 1 < 22:
            load_w11(cg + 1)
        for tt in range(NT11):
            for cc in range(4):
                ps, pb = PS[pi11 % 4], b_ps[pi11 % 4]
                pi11 += 1
                for k in range(16):
                    S.op("pe", lambda e, ps=ps, sl=sl, k=k, cc=cc, tt=tt: e.matmul(
                        ps[:, 0:TQ], lhsT=w11[sl][:, k, cc * 128:(cc + 1) * 128], rhs=hf11[:, k, tt * TQ:(tt + 1) * TQ],
                        start=(k == 0), stop=(k == 15)), reads=[b_w11[sl], b_hf11], writes=[pb])
                S.op("act", lambda e, ps=ps, sl=sl, cc=cc, tt=tt: e.activation(out=rawb[sl][cc][:, 1 + tt * TQ:1 + (tt + 1) * TQ],
                                                                              in_=ps[:, 0:TQ], func=AF.Copy),
                     reads=[pb], writes=[b_rawb[sl][cc]])
        for pr in range(2):
            c = cg * 2 + pr
            conv_full(cvg, rawb[sl][2 * pr], c, b_rawb[sl][2 * pr], b_cvg)
            conv_full(cvv, rawb[sl][2 * pr + 1], 44 + c, b_rawb[sl][2 * pr + 1], b_cvv)
            S.op("act", lambda e: e.activation(out=sil, in_=cvg, func=AF.Silu), reads=[b_cvg], writes=[b_sil])
            S.op("pool", lambda e, npair=npair: e.tensor_tensor(out=ao[npair % 2], in0=sil, in1=cvv, op=ALU.mult),
                 reads=[b_sil, b_cvv], writes=[b_ao[npair % 2]])
            S.dma("sp", actT_d.ap()[:, :, c, :].rearrange("b p t -> p b t"), ao[npair % 2].rearrange("p (b t) -> p b t", t=128),
                  reads=[b_ao[npair % 2]], writes=[b_act])
            npair += 1

    b_xout = b_xmid
    resid_stage(actT_d, b_act, wdn_t, 44, xmid_d, b_xmid, xmid_d, b_xout, 512, 2, TT=128, at_ap=lambda si, tt_: actT_d.ap()[tt_])

    b_out = Buf("out")
    norm_T(xmid_d, g_fin_t, None, b_out, b_xout, tok_out=out_t, nrows=NOUT, row_off=2)
    S.barrier()

    with nc.Block() as block:
        @block.tensor
        def _(e):
            for f in S.ops["pe"]:
                f(e)

        @block.scalar
        def _(e):
            for f in S.ops["act"]:
                f(e)

        @block.vector
        def _(e):
            for f in S.ops["dve"]:
                f(e)

        @block.gpsimd
        def _(e):
            for f in S.ops["pool"]:
                f(e)

        @block.sync
        def _(e):
            for f in S.ops["sp"]:
                f(e)
    st.close()
    return nc


_CONST = {}


def _t5_bucket(rel):
    half, max_exact = 16, 8
    ret = (rel > 0).astype(np.int32) * half
    n = np.abs(rel)
    nf = np.maximum(n, 1).astype(np.float32)
    large = max_exact + (np.log(nf / max_exact) / math.log(128 / max_exact) * (half - max_exact)).astype(np.int32)
    large = np.minimum(large, half - 1)
    return ret + np.where(n < max_exact, n, large)


def _constants():
    if _CONST:
        return _CONST
    bf = ml_dtypes.bfloat16
    N = 2 * L
    ang = 2.0 * np.pi * np.arange(N) / N
    ctab = np.cos(ang)
    stab = np.sin(ang)
    g = np.arange(L).reshape(32, 1, 1, 128)
    s = (np.arange(32).reshape(1, 1, 32, 1) * 128 + np.arange(128).reshape(1, 128, 1, 1))
    idx = (g * s) % N
    fc_cos = ctab[idx]
    fc_sin = stab[idx]
    nyq = np.where(s % 2 == 0, 1.0, -1.0)[0, :, :, 0]
    fc_sin[0, :, :, 0] = nyq
    ffwd = np.concatenate([fc_cos, fc_sin], 0).astype(np.float32).astype(bf).reshape(64, 128, 32 * 128)
    gg = (np.arange(32).reshape(1, 1, 32, 1) * 128 + np.arange(128).reshape(1, 128, 1, 1))
    finvs, masks, buckets = [], [], []
    for j in range(4):
        mloc = (np.arange(LQ // TQ).reshape(-1, 1, 1, 1) * TQ + np.arange(TQ).reshape(1, 1, 1, TQ))
        t = 1024 * j - 2 + mloc
        valid = (t >= 0) & (t < L) & (mloc < NOUT + 4)
        tc = np.where(valid, t, 0)
        idx = (gg * tc) % N
        ic = ctab[idx] * (2.0 / N)
        isn = stab[idx] * (2.0 / N)
        ic[:, 0:1, 0:1, :] = 1.0 / N
        isn[:, 0:1, 0:1, :] = (np.where(tc % 2 == 0, 1.0, -1.0) / N)
        ic = ic * valid
        isn = isn * valid
        finvs.append(np.concatenate([ic, isn], 2).astype(np.float32).astype(bf).reshape(LQ // TQ, 128, 64 * TQ))
        mv = valid.reshape(-1).astype(np.float32)
        masks.append(np.ascontiguousarray(mv.reshape(LQ // 128, 128).T))
        rel = np.arange(128).reshape(128, 1) - np.arange(WT_LEN).reshape(1, WT_LEN) + WT_M0 - (1024 * j - 2)
        buckets.append(_t5_bucket(np.clip(rel, -(L - 1), L - 1)))
    f32 = np.float32
    tt_ = np.linspace(0.0, 1.0, L, dtype=f32)[:, None]
    tr = np.arange(L, dtype=f32)[:, None]
    an = (f32(2.0 * math.pi) * tr / f32(L)).astype(f32)
    bands = np.linspace(1e-4, 15, 16, dtype=f32)[None, :]
    emb = np.concatenate([tt_, np.cos(bands * an), -np.sin(bands * an)], axis=-1).astype(f32)
    max_decay = math.log(1e-2) / 0.3
    min_decay = math.log(1e-2) / 1.5
    deltas = np.abs(np.linspace(min_decay, max_decay, CH, dtype=f32)).astype(f32)
    negt = (-tt_[:, 0]).reshape(32, 128).T.copy()
    _CONST.update(dict(c_ffwd=ffwd, finvs=finvs, masks=masks, buckets=buckets, c_embT=np.ascontiguousarray(emb.T),
                       c_delta=deltas.reshape(1, CH), c_negt=negt.astype(f32), c_ident=np.eye(128, dtype=f32).astype(bf)))
    return _CONST


_NC = {}


def kernel(x, g_mix, w_in, lambda_q1, lambda_k1, lambda_q2, lambda_k2, g_subln, rel_bias,
           hy_conv_w, hy_conv_b, hy_f_w1, hy_f_b1, hy_f_w2, hy_f_b2, hy_f_w3, hy_f_b3,
           hy_f_w4, hy_freq, hy_d, w_attn_branch, w_hyena_branch, w_out, g_ffn, w_up,
           ffn_conv_w, ffn_conv_b, w_down, g_final):
    f = lambda a: np.ascontiguousarray(np.asarray(a, dtype=np.float32))
    C = _constants()
    rb = f(rel_bias)
    wt_biases = [np.ascontiguousarray(np.transpose(rb[bk], (2, 0, 1))) for bk in C["buckets"]]
    hcw = f(hy_conv_w)[0]
    hcb = f(hy_conv_b)[0]
    hycw = np.ascontiguousarray(np.concatenate([hcw, hcb[None]], 0).reshape(4, 24, 128).transpose(2, 1, 0))
    fw = f(ffn_conv_w)[0]
    fb = f(ffn_conv_b)[0]
    ffcw = np.ascontiguousarray(np.concatenate([fw, fb[None]], 0).reshape(4, 88, 128).transpose(2, 1, 0))
    common = {
        "g_mix": f(g_mix), "w_in": f(w_in)[0],
        "lam4": np.ascontiguousarray(np.stack([f(lambda_q1)[0], f(lambda_k1)[0], f(lambda_q2)[0], f(lambda_k2)[0]], 0)),
        "g_subln": f(g_subln)[0].reshape(128, 1), "hycw": hycw,
        "fw1": f(hy_f_w1)[0], "fw2": f(hy_f_w2)[0], "fw3": f(hy_f_w3)[0], "fw4": f(hy_f_w4)[0],
        "fvec": np.ascontiguousarray(np.stack([f(hy_f_b1)[0], f(hy_f_b2)[0], f(hy_f_b3)[0], f(hy_freq)[0]], 1)),
        "hyd": np.ascontiguousarray(f(hy_d)[0].reshape(8, 128).T),
        "w_attn_branch": f(w_attn_branch)[0], "w_hyena_branch": f(w_hyena_branch)[0], "w_out": f(w_out)[0],
        "g_ffn": f(g_ffn), "w_up": np.ascontiguousarray(f(w_up)[0].reshape(D, 2, 44, 128).transpose(0, 2, 1, 3).reshape(D, 2 * DFF)), "ffcw": ffcw, "w_down": f(w_down)[0],
        "g_final": f(g_final).reshape(1, D),
    }
    for k in ("c_embT", "c_delta", "c_negt", "c_ffwd", "c_ident"):
        common[k] = C[k]
    xs = f(x)
    in_maps = []
    for c in range(NCORES):
        b, j = divmod(c, 4)
        m = dict(common)
        m["x"] = xs[b]
        xm = np.zeros((LQ, D), np.float32)
        lo, hi = 1024 * j - 2, 1024 * j - 2 + NOUT + 4
        slo, shi = max(lo, 0), min(hi, L)
        xm[slo - lo:shi - lo] = xs[b, slo:shi]
        m["x_my"] = xm
        m["mask_my"] = C["masks"][j]
        m["wt_bias"] = wt_biases[j]
        m["c_finv"] = C["finvs"][j]
        in_maps.append(m)
    if "nc" not in _NC:
        _NC["nc"] = build_program()
    res = run_bass_kernel_spmd(_NC["nc"], in_maps, core_ids=list(range(NCORES)))
    out = np.empty((2, L, D), np.float32)
    for c in range(NCORES):
        b, j = divmod(c, 4)
        out[b, 1024 * j:1024 * (j + 1)] = np.asarray(res.results[c]["out"], dtype=np.float32)
    return out
```

```python
import math
from contextlib import ExitStack
import numpy as np
import ml_dtypes
import concourse.bass as bass
import concourse.mybir as mybir
from concourse.bass_utils import run_bass_kernel_spmd

F32 = mybir.dt.float32
BF = mybir.dt.bfloat16
AF = mybir.ActivationFunctionType
ALU = mybir.AluOpType
AX = mybir.AxisListType

D = 2048
L = 4096
NH = 8
DFF = 5632
CH = 1024
NCORES = 8
LQ = 1152
TQ = 384
NOUT = 1024
LAMBDA_INIT = 0.8 - 0.6 * math.exp(0.0)
WT_M0 = 3968
WT_LEN = 5120


class Buf:
    __slots__ = ("w", "r", "name")

    def __init__(self, name=""):
        self.w = {}
        self.r = {}
        self.name = name


class Sched:
    ENG = ("pe", "act", "dve", "pool", "sp")

    def __init__(self, nc, stack, n_dma_sems=12):
        self.nc = nc
        self.ops = {e: [] for e in self.ENG}
        self.esem = {e: stack.enter_context(nc.semaphore("s_" + e)) for e in self.ENG}
        self.ecnt = {e: 0 for e in self.ENG}
        self.dsem = {}
        self.dcnt = {}
        self.dnext = {}
        for q in ("sp", "pool"):
            self.dsem[q] = [stack.enter_context(nc.semaphore("d_%s%d" % (q, i))) for i in range(n_dma_sems)]
            self.dcnt[q] = [0] * n_dma_sems
            self.dnext[q] = 0
        self.waited = {e: {} for e in self.ENG}
        self.allsems = {}

    def _waits(self, eng, deps):
        out = []
        wd = self.waited[eng]
        for key, (sem, val) in deps.items():
            if wd.get(key, 0) < val:
                wd[key] = val
                out.append((sem, val))
        return out

    @staticmethod
    def _merge(dst, src):
        for k, (s, v) in src.items():
            if k not in dst or dst[k][1] < v:
                dst[k] = (s, v)

    def _deps(self, reads, writes):
        deps = {}
        for b in reads:
            self._merge(deps, b.w)
        for b in writes:
            self._merge(deps, b.w)
            self._merge(deps, b.r)
        return deps

    def _mark(self, reads, writes, key, tok):
        for b in writes:
            b.w[key] = tok
        for b in reads:
            b.r[key] = tok
        self.allsems[key] = tok

    def op(self, eng, fn, reads=(), writes=()):
        deps = self._deps(reads, writes)
        if eng == "pe":
            deps.pop("e_pe", None)
        waits = self._waits(eng, deps)
        sem = self.esem[eng]
        self.ecnt[eng] += 1
        val = self.ecnt[eng]

        def run(e, waits=waits, fn=fn, sem=sem):
            for s, v in waits:
                e.wait_ge(s, v)
            fn(e).then_inc(sem, 1)

        self.ops[eng].append(run)
        key = "e_" + eng
        self._mark(reads, writes, key, (sem, val))

    def dma(self, q, out, in_, reads=(), writes=(), slow=False):
        deps = self._deps(reads, writes)
        i = self.dnext[q]
        self.dnext[q] = (i + 1) % len(self.dsem[q])
        sem = self.dsem[q][i]
        key = "d_%s%d" % (q, i)
        if self.dcnt[q][i] > 0:
            self._merge(deps, {key: (sem, self.dcnt[q][i])})
        waits = self._waits(q, deps)
        self.dcnt[q][i] += 16
        val = self.dcnt[q][i]

        def run(e, waits=waits, out=out, in_=in_, sem=sem, slow=slow):
            for s, v in waits:
                e.wait_ge(s, v)
            if slow:
                e.dma_start(out=out, in_=in_, allow_slow_non_contiguous=True).then_inc(sem, 16)
            else:
                e.dma_start(out=out, in_=in_).then_inc(sem, 16)

        self.ops[q].append(run)
        self._mark(reads, writes, key, (sem, val))

    def barrier(self, engs=None):
        for eng in (engs or self.ENG):
            waits = self._waits(eng, dict(self.allsems))
            if waits:
                def run(e, waits=waits):
                    for s, v in waits:
                        e.wait_ge(s, v)
                self.ops[eng].append(run)


class Arena:
    def __init__(self, tensor, n, base=0):
        self.t = tensor
        self.n = n
        self.off = 0
        self.base = base

    def reset(self):
        self.off = 0

    def take(self, n, pattern=None, **kw):
        n_al = (n + 15) // 16 * 16
        assert self.off + n_al <= self.n, ("arena overflow", self.off, n, self.n)
        ap = self.t[:, self.base + self.off:self.base + self.off + n]
        self.off += n_al
        if pattern:
            ap = ap.rearrange(pattern, **kw)
        return ap


QT_ATT = 192


def att_tile_far(qt, kc):
    for j in range(4):
        qlo = 1024 * j - 2 + QT_ATT * qt
        qhi = qlo + QT_ATT - 1
        klo, khi = 128 * kc, 128 * kc + 127
        gap = max(klo - qhi, qlo - khi, 0)
        if gap < 91:
            return False
    return True


def dram_bc(ap1d_tensor, offset, n, parts=128):
    return bass.AP(ap1d_tensor, offset, [[0, parts], [1, n]])


def build_program():
    nc = bass.Bass("TRN2", target_bir_lowering=False)
    st = ExitStack()

    def din(name, shape, dt=F32):
        return nc.dram_tensor(name, list(shape), dt, kind="ExternalInput")

    def dscr(name, shape, dt):
        return nc.dram_tensor(name, list(shape), dt)

    x_t = din("x", [L, D])
    xmy_t = din("x_my", [LQ, D])
    mask_t = din("mask_my", [128, LQ // 128])
    g_mix_t = din("g_mix", [1, D])
    w_in_t = din("w_in", [D, 10240])
    lam_t = din("lam4", [4, 64])
    g_subln_t = din("g_subln", [128, 1])
    wt_t = din("wt_bias", [NH, 128, WT_LEN])
    farb_t = din("farb", [128, NH * 6 * 32])
    hycw_t = din("hycw", [128, 24, 4])
    fw1_t = din("fw1", [33, 64])
    fw2_t = din("fw2", [64, 64])
    fw3_t = din("fw3", [64, 64])
    fw4_t = din("fw4", [64, 2048])
    fvec_t = din("fvec", [64, 4])
    hyd_t = din("hyd", [128, 8])
    wa_t = din("w_attn_branch", [1024, D])
    wh_t = din("w_hyena_branch", [1024, D])
    wo_t = din("w_out", [D, D])
    g_ffn_t = din("g_ffn", [1, D])
    wup_t = din("w_up", [D, 2 * DFF])
    ffcw_t = din("ffcw", [128, 88, 4])
    wdn_t = din("w_down", [DFF, D])
    g_fin_t = din("g_final", [1, D])
    embT_t = din("c_embT", [33, L])
    delta_t = din("c_delta", [1, CH])
    negt_t = din("c_negt", [128, 32])
    ffwd_t = din("c_ffwd", [64, 128, 32 * 128], BF)
    finv_t = din("c_finv", [LQ // TQ, 128, 64 * TQ], BF)
    ident_t = din("c_ident", [128, 128], BF)
    out_t = nc.dram_tensor("out", [NOUT, D], F32, kind="ExternalOutput")

    hT_d = dscr("hT_d", [16, 128, L], BF)
    qT_d = dscr("qT_d", [8, 128, LQ], BF)
    hTm_d = dscr("hTm_d", [16, 128, LQ], BF)
    kT_d = dscr("kT_d", [8, 128, L], BF)
    v_d = dscr("v_d", [L, 1024], BF)
    hyraw_d = dscr("hyraw_d", [16, 128, L + 2], F32)
    hyrawm_d = dscr("hyrawm_d", [24, 128, LQ + 2], F32)
    gsig_d = dscr("gsig_d", [32, 128, LQ], F32)
    x0c_d = dscr("x0c_d", [8, 128, LQ], F32)
    uT_d = dscr("uT_d", [8, 128, LQ], F32)
    sig_d = dscr("sig_d", [L, 3072], BF)
    spec_d = dscr("spec_d", [4, 32, 128, CH], F32)
    Y_d = dscr("Y_d", [8, 128, 64, 128], BF)
    yhyT_d = dscr("yhyT_d", [8, 128, LQ], BF)
    attT_d = dscr("attT_d", [8, 128, LQ], BF)
    mrgT_d = dscr("mrgT_d", [16, 128, LQ], BF)
    xmid_d = dscr("xmid_d", [LQ, D], F32)
    hfT_d = dscr("hfT_d", [16, 128, LQ], BF)
    upraw_d = dscr("upraw_d", [88, 128, LQ + 2], F32)
    actT_d = dscr("actT_d", [LQ // 128, 128, 44, 128], BF)

    NBF = 57344
    NF = 16384
    a_bf_t = st.enter_context(nc.sbuf_tensor("a_bf", [128, NBF], BF))
    a_f_t = st.enter_context(nc.sbuf_tensor("a_f", [128, NF], F32))
    ident = st.enter_context(nc.sbuf_tensor("ident", [128, 128], BF))
    ones = st.enter_context(nc.sbuf_tensor("ones", [128, 128], BF))
    cst = st.enter_context(nc.sbuf_tensor("cst", [128, 16], F32))
    PS = [st.enter_context(nc.psum_tensor("ps%d" % i, [128, 512], F32)) for i in range(7)]
    PSB = st.enter_context(nc.psum_tensor("psb", [128, 1024], BF))
    S = Sched(nc, st)
    ABF = Arena(a_bf_t, NBF)
    AFF = Arena(a_f_t, NF)
    b_ps = [Buf("ps%d" % i) for i in range(7)]
    b_psb = Buf("psb")
    b_const = Buf("const")

    def new_stage():
        S.barrier()
        ABF.reset()
        AFF.reset()

    S.op("dve", lambda e: e.memset(ones[:, :], 1.0), writes=[b_const])
    S.op("dve", lambda e: e.memset(cst[:, 0:1], 1e-6), writes=[b_const])
    S.op("dve", lambda e: e.memset(cst[:, 1:2], 1e-5), writes=[b_const])
    S.op("dve", lambda e: e.memset(cst[:, 2:3], -math.pi), writes=[b_const])
    S.op("dve", lambda e: e.memset(cst[:, 3:4], 0.0), writes=[b_const])
    S.dma("sp", ident[:, :], ident_t.ap(), writes=[b_const])

    def norm_T(src_t, g_t, dst_t, dst_buf, src_buf, tok_out=None, nrows=L, row_off=0, mask=None, interleave=None):
        new_stage()
        gbc = AFF.take(D)
        b_g = Buf()
        S.dma("sp", gbc, dram_bc(g_t, 0, D), writes=[b_g])
        xt = [AFF.take(D) for _ in range(2)]
        b_xt = [Buf() for _ in range(2)]
        junk = ABF.take(D)
        b_junk = Buf()
        st_ = [AFF.take(4) for _ in range(2)]
        b_st = [Buf() for _ in range(2)]
        G = 4 if nrows % 512 == 0 else 3
        mk = None
        if mask is not None:
            mk = AFF.take(16)
            S.dma("sp", mk[:, 0:LQ // 128], mask.ap(), writes=[b_g])
        if tok_out is None:
            hb = [ABF.take(D) for _ in range(2)]
            hTt = [ABF.take(16 * G * 128, "p (k t) -> p k t", t=G * 128) for _ in range(2)]
            b_hT = [Buf() for _ in range(2)]
        else:
            hb = [AFF.take(D) for _ in range(2)]
        b_hb = [Buf() for _ in range(2)]
        src = src_t.ap()
        for i in range(nrows // 128):
            s = i % 2
            S.dma("sp", xt[s], src[row_off + i * 128:row_off + (i + 1) * 128, :], reads=[src_buf], writes=[b_xt[s]])
            S.op("act", lambda e, s=s: e.activation(out=junk, in_=xt[s], func=AF.Square, accum_out=st_[s][:, 0:1]),
                 reads=[b_xt[s]], writes=[b_junk, b_st[s]])
            S.op("act", lambda e, s=s: e.activation(out=st_[s][:, 1:2], in_=st_[s][:, 0:1], func=AF.Sqrt,
                                                    bias=cst[:, 0:1], scale=1.0 / D),
                 reads=[b_st[s], b_const], writes=[b_st[s]])
            S.op("dve", lambda e, s=s: e.reciprocal(out=st_[s][:, 2:3], in_=st_[s][:, 1:2]),
                 reads=[b_st[s]], writes=[b_st[s]])
            if mk is not None:
                S.op("dve", lambda e, s=s, i=i: e.tensor_tensor(out=st_[s][:, 2:3], in0=st_[s][:, 2:3], in1=mk[:, i:i + 1], op=ALU.mult),
                     reads=[b_st[s], b_g], writes=[b_st[s]])
            S.op("dve", lambda e, s=s: e.scalar_tensor_tensor(out=hb[s], in0=xt[s], scalar=st_[s][:, 2:3], in1=gbc,
                                                              op0=ALU.mult, op1=ALU.mult),
                 reads=[b_xt[s], b_st[s], b_g], writes=[b_hb[s]])
            if tok_out is not None:
                S.dma("sp", tok_out.ap()[i * 128:(i + 1) * 128, :], hb[s], reads=[b_hb[s]], writes=[dst_buf])
                continue
            if interleave:
                interleave()
            g4 = i // G
            hs = g4 % 2
            tb = i % G
            for half in range(2):
                for kk in range(8):
                    k = half * 8 + kk
                    S.op("pe", lambda e, s=s, k=k, kk=kk: e.transpose(out=PSB[:, kk * 128:(kk + 1) * 128],
                                                                     in_=hb[s][:, k * 128:(k + 1) * 128],
                                                                     identity=ident[:, :]),
                         reads=[b_hb[s], b_const], writes=[b_psb])
                S.op("act", lambda e, hs=hs, half=half, tb=tb: e.activation(
                    out=hTt[hs][:, half * 8:(half + 1) * 8, tb * 128:(tb + 1) * 128],
                    in_=PSB[:, :].rearrange("p (k t) -> p k t", t=128), func=AF.Copy),
                    reads=[b_psb], writes=[b_hT[hs]])
            if tb == G - 1:
                S.dma("sp", dst_t.ap()[:, :, g4 * G * 128:(g4 + 1) * G * 128].rearrange("k p t -> p k t"), hTt[hs],
                      reads=[b_hT[hs]], writes=[dst_buf])

    lin_cache = {}

    def linear(srcs, ncols, mode, evac, cgw=512, TT=512, at_slots=2, prologue=None, T=L, resident=False, chain=False, at_ap=None):
        KCs = [s_[4] for s_ in srcs]
        if resident:
            at_slots = T // TT
        key = (tuple(KCs), cgw, TT, at_slots, T, resident, mode, tuple(s_[0].name for s_ in srcs))
        already_resident = False
        if chain and key in lin_cache:
            wts, b_wt, ats, b_at = lin_cache[key]
            already_resident = resident
            if prologue:
                prologue()
        else:
            new_stage()
            lin_cache.clear()
            if prologue:
                prologue()
            wts = [[ABF.take(kc * cgw, "p (k c) -> p k c", c=cgw) for _ in range(2)] for kc in KCs]
            b_wt = [[Buf() for _ in range(2)] for _ in KCs]
            ats = [[ABF.take(kc * TT, "p (k t) -> p k t", t=TT) for _ in range(at_slots)] for kc in KCs]
            b_at = [[Buf() for _ in range(at_slots)] for _ in KCs]
            lin_cache[key] = (wts, b_wt, ats, b_at)
        ncg = ncols // cgw
        ntt = T // TT
        pi = 0
        it = 0
        def load_w(cg):
            ws = cg % 2
            for si, (a_t, a_buf, w_t, c0, kc) in enumerate(srcs):
                wv = w_t.ap()[:, c0 + cg * cgw:c0 + (cg + 1) * cgw].rearrange("(k p) c -> p k c", p=128)
                half = (kc + 1) // 2
                S.dma("pool", wts[si][ws][:, 0:half, :], wv[:, 0:half, :], writes=[b_wt[si][ws]])
                if half < kc:
                    S.dma("pool", wts[si][ws][:, half:kc, :], wv[:, half:kc, :], writes=[b_wt[si][ws]])

        def load_at(n):
            tt_ = n % ntt
            sl = n % at_slots
            for si, (a_t, a_buf, w_t, c0, kc) in enumerate(srcs):
                src_ap = at_ap(si, tt_) if at_ap else a_t.ap()[:, :, tt_ * TT:(tt_ + 1) * TT].rearrange("k p t -> p k t")
                S.dma("sp", ats[si][sl], src_ap, reads=[a_buf], writes=[b_at[si][sl]])

        load_w(0)
        for cg in range(ncg):
            ws = cg % 2
            if cg + 1 < ncg:
                load_w(cg + 1)
            for tt in range(ntt):
                as_ = it % at_slots
                if resident:
                    if it == 0 and not already_resident:
                        for n_ in range(ntt):
                            load_at(n_)
                else:
                    if it == 0:
                        load_at(0)
                    if at_slots > 1 and it + 1 < ncg * ntt:
                        load_at(it + 1)
                    elif at_slots == 1 and it > 0:
                        load_at(it)
                it += 1
                if mode == "fm":
                    for cc in range(cgw // 128):
                        ps = PS[pi % 4]
                        pb = b_ps[pi % 4]
                        pi += 1
                        n_mm = sum(KCs)
                        j = 0
                        for si, kc in enumerate(KCs):
                            for k in range(kc):
                                S.op("pe", lambda e, si=si, ws=ws, as_=as_, k=k, cc=cc, j=j, n_mm=n_mm, ps=ps:
                                     e.matmul(ps[:, 0:TT], lhsT=wts[si][ws][:, k, cc * 128:(cc + 1) * 128],
                                              rhs=ats[si][as_][:, k, :], start=(j == 0), stop=(j == n_mm - 1)),
                                     reads=[b_wt[si][ws], b_at[si][as_]], writes=[pb])
                                j += 1
                        evac(cg * (cgw // 128) + cc, tt, ps[:, 0:TT], pb)
                else:
                    for tb in range(TT // 128):
                        ps = PS[pi % 4]
                        pb = b_ps[pi % 4]
                        pi += 1
                        n_mm = sum(KCs)
                        j = 0
                        for si, kc in enumerate(KCs):
                            for k in range(kc):
                                S.op("pe", lambda e, si=si, ws=ws, as_=as_, k=k, tb=tb, j=j, n_mm=n_mm, ps=ps:
                                     e.matmul(ps[:, 0:cgw], lhsT=ats[si][as_][:, k, tb * 128:(tb + 1) * 128],
                                              rhs=wts[si][ws][:, k, :], start=(j == 0), stop=(j == n_mm - 1)),
                                     reads=[b_wt[si][ws], b_at[si][as_]], writes=[pb])
                                j += 1
                        evac(cg, tt * (TT // 128) + tb, ps[:, 0:cgw], pb)

    b_sig = Buf("sig")
    new_stage()
    embT = [AFF.take(512) for _ in range(2)]
    b_emb = [Buf() for _ in range(2)]
    w1 = AFF.take(64)
    w2 = AFF.take(64)
    w3 = AFF.take(64)
    w4 = AFF.take(2048)
    fv = AFF.take(8)
    dl = AFF.take(CH)
    ngt = AFF.take(32)
    b_f = Buf()
    S.dma("sp", w1[0:33, :], fw1_t.ap(), writes=[b_f])
    S.dma("sp", w2[0:64, :], fw2_t.ap(), writes=[b_f])
    S.dma("sp", w3[0:64, :], fw3_t.ap(), writes=[b_f])
    S.dma("sp", w4[0:64, :], fw4_t.ap(), writes=[b_f])
    S.dma("sp", fv[0:64, 0:4], fvec_t.ap(), writes=[b_f])
    S.dma("sp", dl, dram_bc(delta_t, 0, CH), writes=[b_f])
    S.dma("sp", ngt, negt_t.ap(), writes=[b_f])
    for j in range(3):
        S.op("dve", lambda e, j=j: e.tensor_tensor(out=fv[0:64, 4 + j:5 + j], in0=fv[0:64, j:j + 1], in1=fv[0:64, 3:4],
                                                   op=ALU.mult), reads=[b_f], writes=[b_f])
    hcur = [AFF.take(512) for _ in range(2)]
    b_h = [Buf() for _ in range(2)]
    H3 = AFF.take(L)
    b_H3 = Buf()
    arg = AFF.take(512)
    b_arg = Buf()
    sA = AFF.take(512)
    sB = AFF.take(512)
    b_sA, b_sB = Buf(), Buf()
    for pt in range(8):
        S.dma("sp", embT[pt % 2][0:33, :], embT_t.ap()[:, pt * 512:(pt + 1) * 512], writes=[b_emb[pt % 2]])
        srcs_ = [(embT[pt % 2][0:33, :], w1[0:33, 0:64]), None, None]
        for ly in range(3):
            ps = PS[(pt * 3 + ly) % 4]
            pb = b_ps[(pt * 3 + ly) % 4]
            if ly == 0:
                rhs, lhsT = srcs_[0]
                rb = b_emb[pt % 2]
            else:
                rhs = hcur[(ly - 1) % 2][0:64, :]
                lhsT = (w2 if ly == 1 else w3)[0:64, 0:64]
                rb = b_h[(ly - 1) % 2]
            S.op("pe", lambda e, ps=ps, lhsT=lhsT, rhs=rhs: e.matmul(ps[0:64, :], lhsT=lhsT, rhs=rhs, start=True, stop=True),
                 reads=[b_f, rb], writes=[pb])
            S.op("dve", lambda e, ps=ps, ly=ly: e.tensor_scalar(out=arg[0:64, :], in0=ps[0:64, :], scalar1=fv[0:64, 3:4],
                                                                scalar2=fv[0:64, 4 + ly:5 + ly], op0=ALU.mult, op1=ALU.add),
                 reads=[pb, b_f], writes=[b_arg])
            if ly < 2:
                dst, db = hcur[ly % 2][0:64, :], b_h[ly % 2]
            else:
                dst, db = H3[0:64, pt * 512:(pt + 1) * 512], b_H3
            S.op("act", lambda e: e.activation(out=sA[0:64, :], in_=arg[0:64, :], func=AF.Sin, scale=0.5),
                 reads=[b_arg], writes=[b_sA])
            S.op("act", lambda e: e.activation(out=sB[0:64, :], in_=arg[0:64, :], func=AF.Sin, scale=0.25),
                 reads=[b_arg], writes=[b_sB])
            S.op("dve", lambda e: e.tensor_tensor(out=sB[0:64, :], in0=sB[0:64, :], in1=sB[0:64, :], op=ALU.mult),
                 reads=[b_sB], writes=[b_sB])
            S.op("dve", lambda e: e.tensor_scalar(out=sB[0:64, :], in0=sB[0:64, :], scalar1=-4.0, scalar2=2.0,
                                                  op0=ALU.mult, op1=ALU.add), reads=[b_sB], writes=[b_sB])
            S.op("dve", lambda e, dst=dst: e.tensor_tensor(out=dst, in0=sA[0:64, :], in1=sB[0:64, :], op=ALU.mult),
                 reads=[b_sA, b_sB], writes=[db])
    dec = [AFF.take(CH) for _ in range(2)]
    b_dec = [Buf() for _ in range(2)]
    hfb = [AFF.take(2048)] * 2
    b_hfb = [Buf()] * 2
    hpm = [ABF.take(2048) for _ in range(2)]
    b_hpm = [Buf() for _ in range(2)]
    for pc in range(32):
        s = pc % 2
        for ct in range(4):
            S.op("pe", lambda e, pc=pc, ct=ct: e.matmul(PS[ct][:, :], lhsT=H3[0:64, pc * 128:(pc + 1) * 128],
                                                        rhs=w4[0:64, ct * 512:(ct + 1) * 512], start=True, stop=True),
                 reads=[b_H3, b_f], writes=[b_ps[ct]])
        S.op("act", lambda e, s=s, pc=pc: e.activation(out=dec[s], in_=dl, func=AF.Exp, scale=ngt[:, pc:pc + 1]),
             reads=[b_f], writes=[b_dec[s]])
        for ct in range(4):
            S.op("dve", lambda e, s=s, ct=ct: e.tensor_tensor(out=hfb[s][:, ct * 512:(ct + 1) * 512], in0=PS[ct][:, :],
                                                              in1=dec[s][:, (ct % 2) * 512:(ct % 2 + 1) * 512], op=ALU.mult),
                 reads=[b_ps[ct], b_dec[s]], writes=[b_hfb[s]])
        S.op("pool", lambda e, s=s: e.tensor_tensor(out=hpm[s][:, 0:1024], in0=hfb[s][:, 0:1024], in1=hfb[s][:, 1024:2048],
                                                    op=ALU.add), reads=[b_hfb[s]], writes=[b_hpm[s]])
        S.op("pool", lambda e, s=s: e.tensor_tensor(out=hpm[s][:, 1024:2048], in0=hfb[s][:, 0:1024], in1=hfb[s][:, 1024:2048],
                                                    op=ALU.subtract), reads=[b_hfb[s]], writes=[b_hpm[s]])
        S.dma("sp", sig_d.ap()[pc * 128:(pc + 1) * 128, 1024:3072], hpm[s], reads=[b_hpm[s]], writes=[b_sig])

    b_spec = Buf("spec")
    KBF = 28672
    ABF_K = Arena(a_bf_t, KBF, base=NBF - KBF)
    AFF_K = Arena(a_f_t, 2048, base=NF - 2048)

    def fcs_of(ct):
        if ct < 2:
            return list(range(64))
        if ct < 4:
            return list(range(32)) + [32]
        return list(range(32, 64))

    def kpass():
        sig1 = ABF_K.take(32 * 512, "p (s c) -> p s c", c=512)
        b_sig1 = Buf()
        ftk = [ABF_K.take(32 * 128, "p (s g) -> p s g", g=128) for _ in range(3)]
        b_ftk = [Buf() for _ in range(3)]
        sok = [AFF_K.take(512) for _ in range(4)]
        b_sok = [Buf() for _ in range(4)]
        ctsK = (4, 5, 2, 3)
        itsK = [(ct, fc) for ct in ctsK for fc in fcs_of(ct)]

        def load_ftK(n):
            S.dma("sp", ftk[n % 3], ffwd_t.ap()[itsK[n][1]].rearrange("p (s g) -> p s g", g=128), writes=[b_ftk[n % 3]])

        load_ftK(0)
        load_ftK(1)
        for n, (ct, fc) in enumerate(itsK):
            fs = n % 3
            if n + 2 < len(itsK):
                load_ftK(n + 2)
            if fc == fcs_of(ct)[0]:
                S.dma("sp", sig1, sig_d.ap()[:, ct * 512:(ct + 1) * 512].rearrange("(s p) c -> p s c", p=128),
                      reads=[b_sig], writes=[b_sig1])
            ps = PS[n % 4]
            pb = b_ps[n % 4]
            for sc in range(32):
                S.op("pe", lambda e, ps=ps, fs=fs, sc=sc: e.matmul(ps[:, :], lhsT=ftk[fs][:, sc, :], rhs=sig1[:, sc, :],
                                                                  start=(sc == 0), stop=(sc == 31)),
                     reads=[b_ftk[fs], b_sig1], writes=[pb])
            o, ob = sok[n % 4], b_sok[n % 4]
            S.op("act", lambda e, o=o, ps=ps: e.activation(out=o, in_=ps[:, :], func=AF.Copy), reads=[pb], writes=[ob])
            if ct in (2, 3) and fc == 32:
                S.dma("sp", spec_d.ap()[3, 0, 0:1, (ct % 2) * 512:(ct % 2 + 1) * 512], o[0:1, :], reads=[ob], writes=[b_spec])
            else:
                S.dma("sp", spec_d.ap()[2 if ct < 4 else 3, fc % 32, :, (ct % 2) * 512:(ct % 2 + 1) * 512], o, reads=[ob], writes=[b_spec])
            yield

    kgen = kpass()

    def k_steps(k=3):
        for _ in range(k):
            next(kgen, None)

    b_x = Buf("x")
    b_hT = Buf("hT")
    norm_T(x_t, g_mix_t, hT_d, b_hT, b_x, interleave=k_steps)
    b_xmy = Buf("xmy")
    b_hTm = Buf("hTm")
    norm_T(xmy_t, g_mix_t, hTm_d, b_hTm, b_xmy, nrows=LQ, interleave=k_steps)
    for _ in kgen:
        pass

    b_q, b_k, b_v, b_hyraw, b_gsig = Buf("q"), Buf("k"), Buf("v"), Buf("hyraw"), Buf("gsig")
    ev = {}

    def mk_out_slots(n, dt_arena, width):
        tiles = [dt_arena.take(width) for _ in range(n)]
        bufs = [Buf() for _ in range(n)]
        return tiles, bufs

    def qk_stage(c0, src_t, src_buf, dst_t, dst_buf, scale, T, TT, chain=False, cgw=512):
        cnt = [0]
        slots = {}

        def pro():
            slots["t"], slots["b"] = mk_out_slots(4, ABF, 512)

        def evac(ci, ti, ps, pb):
            s = cnt[0] % 4
            cnt[0] += 1
            o, ob = slots["t"][s], slots["b"][s]
            S.op("act", lambda e: e.activation(out=o[:, 0:TT], in_=ps, func=AF.Copy, scale=scale), reads=[pb], writes=[ob])
            S.dma("sp", dst_t.ap()[ci, :, ti * TT:(ti + 1) * TT], o[:, 0:TT], reads=[ob], writes=[dst_buf])

        linear([(src_t, src_buf, w_in_t, c0, 16)], 1024, "fm", evac, prologue=pro, T=T, TT=TT, resident=(T == LQ), chain=chain, cgw=cgw)


    def v_stage():
        cnt = [0]
        slots = {}

        def pro():
            slots["t"], slots["b"] = mk_out_slots(4, ABF, 512)

        def evac(cg, tb, ps, pb):
            s = cnt[0] % 4
            cnt[0] += 1
            o, ob = slots["t"][s], slots["b"][s]
            S.op("act", lambda e: e.activation(out=o, in_=ps, func=AF.Copy), reads=[pb], writes=[ob])
            S.dma("sp", v_d.ap()[tb * 128:(tb + 1) * 128, cg * 512:(cg + 1) * 512], o, reads=[ob], writes=[b_v])

        linear([(hT_d, b_hT, w_in_t, 2048, 16)], 1024, "tm", evac, prologue=pro)


    def raw_stage(src_t, src_buf, w_t, c0, ncols, dst_t, dst_buf, func=AF.Copy, pad=1, T=L, TT=512, chain=False, cgw=512):
        cnt = [0]
        slots = {}

        def pro():
            slots["t"], slots["b"] = mk_out_slots(4, AFF, 512)
            if pad:
                z = AFF.take(2)
                bz = Buf()
                S.op("dve", lambda e: e.memset(z, 0.0), writes=[bz])
                nchunks = ncols // 128
                for c in range(nchunks):
                    S.dma("sp", dst_t.ap()[c, :, 0:1], z[:, 0:1], reads=[bz], writes=[dst_buf], slow=True)
                    S.dma("sp", dst_t.ap()[c, :, T + 1:T + 2], z[:, 1:2], reads=[bz], writes=[dst_buf], slow=True)

        def evac(ci, ti, ps, pb):
            s = cnt[0] % 4
            cnt[0] += 1
            o, ob = slots["t"][s], slots["b"][s]
            S.op("act", lambda e: e.activation(out=o[:, 0:TT], in_=ps, func=func), reads=[pb], writes=[ob])
            S.dma("sp", dst_t.ap()[ci, :, pad + ti * TT:pad + (ti + 1) * TT], o[:, 0:TT], reads=[ob], writes=[dst_buf])

        linear([(src_t, src_buf, w_t, c0, 16)], ncols, "fm", evac, prologue=pro, T=T, TT=TT, resident=(T == LQ), chain=chain, cgw=cgw)

    b_hyrawm = Buf("hyrawm")
    qk_stage(1024, hT_d, b_hT, kT_d, b_k, 1.0, L, 512, cgw=1024)
    raw_stage(hT_d, b_hT, w_in_t, 4096, 2048, hyraw_d, b_hyraw, chain=True, cgw=1024)
    v_stage()
    qk_stage(0, hTm_d, b_hTm, qT_d, b_q, 0.125, LQ, TQ)
    raw_stage(hTm_d, b_hTm, w_in_t, 3072, 3072, hyrawm_d, b_hyrawm, T=LQ, TT=TQ, chain=True)
    raw_stage(hTm_d, b_hTm, w_in_t, 6144, 4096, gsig_d, b_gsig, func=AF.Sigmoid, pad=0, T=LQ, TT=TQ, chain=True)

    b_x0c, b_uT = Buf("x0c"), Buf("uT")
    b_cw = Buf()

    def conv3(eng, dst, src, wts_, c, b_src, b_dst, n=512):
        dst = dst[:, 0:n]
        S.op(eng, lambda e: e.tensor_scalar(out=dst, in0=src[:, 0:n], scalar1=wts_[:, c, 0:1], scalar2=wts_[:, c, 3:4],
                                            op0=ALU.mult, op1=ALU.add), reads=[b_src, b_cw], writes=[b_dst])
        S.op(eng, lambda e: e.scalar_tensor_tensor(out=dst, in0=src[:, 1:n + 1], scalar=wts_[:, c, 1:2], in1=dst,
                                                   op0=ALU.mult, op1=ALU.add), reads=[b_src, b_cw, b_dst], writes=[b_dst])
        S.op(eng, lambda e: e.scalar_tensor_tensor(out=dst, in0=src[:, 2:n + 2], scalar=wts_[:, c, 2:3], in1=dst,
                                                   op0=ALU.mult, op1=ALU.add), reads=[b_src, b_cw, b_dst], writes=[b_dst])

    def conv_gen():
        ABF_C = Arena(a_bf_t, 12288, base=NBF - 12288)
        AFF_C = Arena(a_f_t, 8192, base=NF - 8192)
        cw = AFF_C.take(24 * 4, "p (c j) -> p c j", j=4)
        S.dma("sp", cw, hycw_t.ap(), writes=[b_cw])
        raw = [[AFF_C.take(514) for _ in range(3)] for _ in range(2)]
        b_raw = [[Buf() for _ in range(3)] for _ in range(2)]
        cv = [[AFF_C.take(512) for _ in range(3)] for _ in range(2)]
        b_cv = [[Buf() for _ in range(3)] for _ in range(2)]
        ubf = [ABF_C.take(512) for _ in range(2)]
        b_ubf = [Buf() for _ in range(2)]
        utm = [ABF_C.take(4 * 1024, "p (b c) -> p b c", c=1024) for _ in range(2)]
        b_utm = [Buf() for _ in range(2)]
        tiles_a = [(tt, c) for tt in range(8) for c in range(8)]

        def load_a(n):
            tt, c = tiles_a[n]
            for j in (1, 2):
                S.dma("sp", raw[n % 2][j], hyraw_d.ap()[(j - 1) * 8 + c, :, tt * 512:tt * 512 + 514],
                      reads=[b_hyraw], writes=[b_raw[n % 2][j]])

        def finish_a(n):
            tt, c = tiles_a[n]
            s, us = n % 2, tt % 2
            for tb in range(4):
                S.op("pe", lambda e, s=s, tb=tb: e.transpose(out=PSB[:, tb * 128:(tb + 1) * 128],
                                                            in_=ubf[s][:, tb * 128:(tb + 1) * 128], identity=ident[:, :]),
                     reads=[b_ubf[s], b_const], writes=[b_psb])
            S.op("act", lambda e, us=us, c=c: e.activation(out=utm[us][:, :, c * 128:(c + 1) * 128],
                                                           in_=PSB[:, 0:512].rearrange("p (b c) -> p b c", c=128),
                                                           func=AF.Copy), reads=[b_psb], writes=[b_utm[us]])
            if c == 7:
                S.dma("sp", sig_d.ap()[tt * 512:(tt + 1) * 512, 0:1024].rearrange("(b p) c -> p b c", p=128), utm[us],
                      reads=[b_utm[us]], writes=[b_sig])

        load_a(0)
        for n, (tt, c) in enumerate(tiles_a):
            s = n % 2
            if n + 1 < len(tiles_a):
                load_a(n + 1)
            conv3("dve", cv[s][1], raw[s][1], cw, 8 + c, b_raw[s][1], b_cv[s][1])
            conv3("dve", cv[s][2], raw[s][2], cw, 16 + c, b_raw[s][2], b_cv[s][2])
            S.op("pool", lambda e, s=s: e.tensor_tensor(out=ubf[s], in0=cv[s][1], in1=cv[s][2], op=ALU.mult),
                 reads=[b_cv[s][1], b_cv[s][2]], writes=[b_ubf[s]])
            if n > 0:
                finish_a(n - 1)
            yield
        finish_a(len(tiles_a) - 1)
        yield
        it = 0
        for tt in range(LQ // TQ):
            for c in range(8):
                s = it % 2
                it += 1
                for j in range(3):
                    S.dma("sp", raw[s][j][:, 0:TQ + 2], hyrawm_d.ap()[j * 8 + c, :, tt * TQ:tt * TQ + TQ + 2],
                          reads=[b_hyrawm], writes=[b_raw[s][j]])
                conv3("dve", cv[s][0], raw[s][0], cw, c, b_raw[s][0], b_cv[s][0], n=TQ)
                conv3("dve", cv[s][1], raw[s][1], cw, 8 + c, b_raw[s][1], b_cv[s][1], n=TQ)
                conv3("dve", cv[s][2], raw[s][2], cw, 16 + c, b_raw[s][2], b_cv[s][2], n=TQ)
                S.dma("sp", x0c_d.ap()[c, :, tt * TQ:(tt + 1) * TQ], cv[s][0][:, 0:TQ], reads=[b_cv[s][0]], writes=[b_x0c])
                S.op("pool", lambda e, s=s: e.tensor_tensor(out=cv[s][1][:, 0:TQ], in0=cv[s][1][:, 0:TQ], in1=cv[s][2][:, 0:TQ], op=ALU.mult),
                     reads=[b_cv[s][1], b_cv[s][2]], writes=[b_cv[s][1]])
                S.dma("sp", uT_d.ap()[c, :, tt * TQ:(tt + 1) * TQ], cv[s][1][:, 0:TQ], reads=[b_cv[s][1]], writes=[b_uT])
                yield

    b_att = Buf("att")
    new_stage()
    lam = AFF.take(64 * 4 + 8)
    b_lam = Buf()
    for j in range(4):
        S.dma("sp", lam[:, j * 64:(j + 1) * 64], dram_bc(lam_t, j * 64, 64), writes=[b_lam])
    gs = AFF.take(2)
    S.dma("sp", gs[:, 0:1], g_subln_t.ap(), writes=[b_lam])
    S.op("dve", lambda e: e.tensor_scalar(out=gs[:, 1:2], in0=gs[:, 0:1], scalar1=1.0 - LAMBDA_INIT, scalar2=None, op0=ALU.mult),
         reads=[b_lam], writes=[b_lam])
    for j in range(2):
        S.op("dve", lambda e, j=j: e.tensor_tensor(out=lam[:, j * 128:j * 128 + 64], in0=lam[:, j * 128:j * 128 + 64],
                                                   in1=lam[:, j * 128 + 64:j * 128 + 128], op=ALU.mult), reads=[b_lam], writes=[b_lam])
        S.op("dve", lambda e, j=j: e.reduce_sum(out=lam[:, 256 + j:257 + j], in_=lam[:, j * 128:j * 128 + 64], axis=AX.X),
             reads=[b_lam], writes=[b_lam])
        S.op("act", lambda e, j=j: e.activation(out=lam[:, 258 + j:259 + j], in_=lam[:, 256 + j:257 + j], func=AF.Exp),
             reads=[b_lam], writes=[b_lam])
    S.op("dve", lambda e: e.tensor_tensor(out=lam[:, 260:261], in0=lam[:, 259:260], in1=lam[:, 258:259], op=ALU.subtract),
         reads=[b_lam], writes=[b_lam])
    S.op("dve", lambda e: e.tensor_scalar(out=lam[:, 260:261], in0=lam[:, 260:261], scalar1=-LAMBDA_INIT, scalar2=None, op0=ALU.add),
         reads=[b_lam], writes=[b_lam])
    neglam = lam[:, 260:261]

    qh = [ABF.take(2 * LQ, "p (m t) -> p m t", m=2) for _ in range(2)]
    kh = [ABF.take(L) for _ in range(2)]
    vh = [ABF.take(32 * 128, "p (k e) -> p k e", e=128) for _ in range(2)]
    wth = [ABF.take(WT_LEN) for _ in range(2)]
    b_hd_ = [Buf() for _ in range(2)]
    E = [ABF.take(512) for _ in range(3)]
    b_E = [Buf() for _ in range(3)]
    atth = [ABF.take(LQ) for _ in range(2)]
    b_atth = [Buf() for _ in range(2)]
    sq = [ABF.take(256) for _ in range(2)]
    b_sq = [Buf() for _ in range(2)]
    rz = [AFF.take(512) for _ in range(2)]
    o12 = [AFF.take(512) for _ in range(2)]
    at_ = [AFF.take(256) for _ in range(2)]
    rs = [AFF.take(256) for _ in range(2)]
    b_ev = [Buf() for _ in range(2)]
    QT = QT_ATT
    W2 = 2 * QT
    farb = AFF.take(NH * 6 * 32)
    S.dma("sp", farb, farb_t.ap(), writes=[b_lam])

    b_qz = Buf()
    for hs_ in range(2):
        S.op("dve", lambda e, hs_=hs_: e.memset(qh[hs_][0:64, 1, :], 0.0), writes=[b_qz])
        S.op("dve", lambda e, hs_=hs_: e.memset(qh[hs_][64:128, 0, :], 0.0), writes=[b_qz])

    def load_head(h):
        hs = h % 2
        S.dma("sp", qh[hs][0:64, 0, :], qT_d.ap()[h, 0:64, :], reads=[b_q], writes=[b_hd_[hs]])
        S.dma("sp", qh[hs][64:128, 1, :], qT_d.ap()[h, 64:128, :], reads=[b_q], writes=[b_hd_[hs]])
        S.dma("sp", kh[hs], kT_d.ap()[h], reads=[b_k], writes=[b_hd_[hs]])
        S.dma("sp", vh[hs], v_d.ap()[:, h * 128:(h + 1) * 128].rearrange("(k p) e -> p k e", p=128), reads=[b_v], writes=[b_hd_[hs]])
        S.dma("pool", wth[hs], wt_t.ap()[h], writes=[b_hd_[hs]])

    iters = [(h, qt, kc) for h in range(NH) for qt in range(LQ // QT) for kc in range(32)]

    def is_near(qt, kc):
        return not att_tile_far(qt, kc)

    def emit_S(i):
        h, qt, kc = iters[i]
        hs = h % 2
        pS, bS = PS[i % 2], b_ps[i % 2]
        near = is_near(qt, kc)
        c0 = WT_M0 - kc * 128 + qt * QT
        assert 0 <= c0 <= WT_LEN - QT or not near
        S.op("pe", lambda e: e.matmul(
            pS[:, 0:W2].rearrange("p (m q) -> p m q", m=2), lhsT=kh[hs][:, kc * 128:(kc + 1) * 128],
            rhs=qh[hs][:, :, qt * QT:(qt + 1) * QT], start=True, stop=(not near)),
            reads=[b_hd_[hs], b_qz], writes=[bS])
        if near:
            for mp in range(2):
                S.op("pe", lambda e, mp=mp: e.matmul(
                    pS[:, mp * QT:(mp + 1) * QT], lhsT=ident[:, :], rhs=wth[hs][:, c0:c0 + QT], start=False, stop=(mp == 1)),
                    reads=[b_hd_[hs], b_const], writes=[bS])

    def emit_post1(h, qt):
        hs = h % 2
        os_ = qt % 2
        pO, bO = PS[2 + os_], b_ps[2 + os_]
        pZ, bZ = PS[4 + os_], b_ps[4 + os_]
        S.op("dve", lambda e: e.reciprocal(out=rz[os_][:, 0:W2], in_=pZ[:, 0:W2]), reads=[bZ], writes=[b_ev[os_]])
        S.op("dve", lambda e: e.tensor_tensor(out=o12[os_][:, 0:W2], in0=pO[:, 0:W2], in1=rz[os_][:, 0:W2], op=ALU.mult),
             reads=[bO, b_ev[os_]], writes=[b_ev[os_]])
        S.op("dve", lambda e: e.scalar_tensor_tensor(out=at_[os_][:, 0:QT], in0=o12[os_][:, QT:2 * QT], scalar=neglam,
                                                     in1=o12[os_][:, 0:QT], op0=ALU.mult, op1=ALU.add),
             reads=[b_ev[os_], b_lam], writes=[b_ev[os_]])
        S.op("dve", lambda e: e.tensor_tensor(out=sq[os_][:, 0:QT], in0=at_[os_][:, 0:QT], in1=at_[os_][:, 0:QT], op=ALU.mult),
             reads=[b_ev[os_]], writes=[b_sq[os_]])

    def emit_post2(h, qt):
        hs = h % 2
        os_ = qt % 2
        S.op("pe", lambda e: e.matmul(PS[6][:, 0:QT], lhsT=ones[:, :], rhs=sq[os_][:, 0:QT], start=True, stop=True),
             reads=[b_const, b_sq[os_]], writes=[b_ps[6]])
        S.op("act", lambda e: e.activation(out=rs[os_][:, 0:QT], in_=PS[6][:, 0:QT], func=AF.Sqrt, bias=cst[:, 1:2], scale=1.0 / 128),
             reads=[b_ps[6], b_const], writes=[b_ev[os_]])
        S.op("dve", lambda e: e.reciprocal(out=rs[os_][:, 0:QT], in_=rs[os_][:, 0:QT]), reads=[b_ev[os_]], writes=[b_ev[os_]])
        S.op("dve", lambda e: e.scalar_tensor_tensor(out=atth[hs][:, qt * QT:(qt + 1) * QT], in0=at_[os_][:, 0:QT],
                                                     scalar=gs[:, 1:2], in1=rs[os_][:, 0:QT], op0=ALU.mult, op1=ALU.mult),
             reads=[b_ev[os_], b_lam], writes=[b_atth[hs]])
        if qt == LQ // QT - 1:
            S.dma("sp", attT_d.ap()[h], atth[hs], reads=[b_atth[hs]], writes=[b_att])

    load_head(0)
    emit_S(0)
    pending = []
    cgen = conv_gen()
    for i, (h, qt, kc) in enumerate(iters):
        hs = h % 2
        if i % 12 == 6:
            next(cgen, None)
        if qt == 0 and kc == 0 and h + 1 < NH:
            load_head(h + 1)
        if i + 1 < len(iters):
            emit_S(i + 1)
        os_ = qt % 2
        pS, bS = PS[i % 2], b_ps[i % 2]
        pO, bO = PS[2 + os_], b_ps[2 + os_]
        pZ, bZ = PS[4 + os_], b_ps[4 + os_]
        es = i % 3
        if is_near(qt, kc):
            S.op("act", lambda e, es=es, pS=pS: e.activation(out=E[es][:, 0:W2], in_=pS[:, 0:W2], func=AF.Exp), reads=[bS], writes=[b_E[es]])
        else:
            fcol = (h * 6 + qt) * 32 + kc
            S.op("act", lambda e, es=es, pS=pS, fcol=fcol: e.activation(out=E[es][:, 0:W2], in_=pS[:, 0:W2], func=AF.Exp,
                                                                        bias=farb[:, fcol:fcol + 1]),
                 reads=[bS, b_lam], writes=[b_E[es]])
        S.op("pe", lambda e, pO=pO, hs=hs, kc=kc, es=es: e.matmul(pO[:, 0:W2], lhsT=vh[hs][:, kc, :], rhs=E[es][:, 0:W2],
                                                                 start=(kc == 0), stop=(kc == 31)),
             reads=[b_hd_[hs], b_E[es]], writes=[bO])
        S.op("pe", lambda e, pZ=pZ, es=es, kc=kc: e.matmul(pZ[:, 0:W2], lhsT=ones[:, :], rhs=E[es][:, 0:W2],
                                                          start=(kc == 0), stop=(kc == 31)),
             reads=[b_const, b_E[es]], writes=[bZ])
        if pending and pending[0][0] <= i:
            _, hh, qq = pending.pop(0)
            emit_post2(hh, qq)
        if kc == 31:
            emit_post1(h, qt)
            pending.append((i + 3, h, qt))
    for _, hh, qq in pending:
        emit_post2(hh, qq)
    for _ in cgen:
        pass

    b_Y = Buf("Y")
    new_stage()
    sigt = [ABF.take(32 * 512, "p (s c) -> p s c", c=512) for _ in range(2)]
    b_sigt = [Buf() for _ in range(2)]
    ft = [ABF.take(32 * 128, "p (s g) -> p s g", g=128) for _ in range(4)]
    b_ft = [Buf() for _ in range(4)]
    so = [AFF.take(512) for _ in range(4)]
    b_so = [Buf() for _ in range(4)]
    cts = (0, 1)

    def load_sig(cti):
        ct = cts[cti]
        S.dma("sp", sigt[cti % 2], sig_d.ap()[:, ct * 512:(ct + 1) * 512].rearrange("(s p) c -> p s c", p=128),
              reads=[b_sig], writes=[b_sigt[cti % 2]])

    load_sig(0)

    kin = [[AFF.take(512) for _ in range(2)] for _ in range(3)]
    b_kin = [[Buf() for _ in range(2)] for _ in range(3)]
    tq = [[AFF.take(512) for _ in range(4)] for _ in range(2)]
    b_tq = [[Buf() for _ in range(4)] for _ in range(2)]
    yo = [[ABF.take(512) for _ in range(2)] for _ in range(2)]
    b_yo = [[Buf() for _ in range(2)] for _ in range(2)]
    itsU = [(cti, ct, gc) for cti, ct in ((0, 0), (1, 1)) for gc in range(32)]

    def load_U(n):
        cti, ct, gc = itsU[n]
        for j, fc in enumerate((gc, 32 + gc)):
            sl = (n % 2) * 2 + j
            S.dma("sp", ft[sl], ffwd_t.ap()[fc].rearrange("p (s g) -> p s g", g=128), writes=[b_ft[sl]])
        for w_ in range(2):
            S.dma("sp", kin[n % 3][w_], spec_d.ap()[2 + w_, gc, :, ct * 512:(ct + 1) * 512], reads=[b_spec], writes=[b_kin[n % 3][w_]])

    load_U(0)
    for n, (cti, ct, gc) in enumerate(itsU):
        ss_ = cti % 2
        if n + 1 < len(itsU):
            load_U(n + 1)
        if gc == 0 and ct == 0:
            load_sig(1)
        pp = n % 2
        for j in range(2):
            ps, pb = PS[2 * pp + j], b_ps[2 * pp + j]
            sl = pp * 2 + j
            for sc in range(32):
                S.op("pe", lambda e, ps=ps, sl=sl, sc=sc, ss_=ss_: e.matmul(ps[:, :], lhsT=ft[sl][:, sc, :], rhs=sigt[ss_][:, sc, :],
                                                                           start=(sc == 0), stop=(sc == 31)),
                     reads=[b_ft[sl], b_sigt[ss_]], writes=[pb])
        pUc, bUc = PS[2 * pp], b_ps[2 * pp]
        pUs, bUs = PS[2 * pp + 1], b_ps[2 * pp + 1]
        Kc, Ks = kin[n % 3]
        bKc, bKs = b_kin[n % 3]
        t = tq[pp]
        bt = b_tq[pp]
        S.op("dve", lambda e, t=t, pUc=pUc, Kc=Kc: e.tensor_tensor(out=t[0], in0=pUc[:, :], in1=Kc, op=ALU.mult), reads=[bUc, bKc], writes=[bt[0]])
        S.op("dve", lambda e, t=t, pUs=pUs, Ks=Ks: e.tensor_tensor(out=t[1], in0=pUs[:, :], in1=Ks, op=ALU.mult), reads=[bUs, bKs], writes=[bt[1]])
        S.op("dve", lambda e, t=t, pUc=pUc, Ks=Ks: e.tensor_tensor(out=t[2], in0=pUc[:, :], in1=Ks, op=ALU.mult), reads=[bUc, bKs], writes=[bt[2]])
        S.op("dve", lambda e, t=t, pUs=pUs, Kc=Kc: e.tensor_tensor(out=t[3], in0=pUs[:, :], in1=Kc, op=ALU.mult), reads=[bUs, bKc], writes=[bt[3]])
        S.op("pool", lambda e, t=t, pp=pp: e.tensor_tensor(out=yo[pp][0], in0=t[0], in1=t[1], op=ALU.subtract),
             reads=[bt[0], bt[1]], writes=[b_yo[pp][0]])
        S.op("pool", lambda e, t=t, pp=pp: e.tensor_tensor(out=yo[pp][1], in0=t[2], in1=t[3], op=ALU.add),
             reads=[bt[2], bt[3]], writes=[b_yo[pp][1]])
        if gc == 0:
            S.op("pool", lambda e, t=t, pp=pp: e.tensor_copy(out=yo[pp][0][0:1, :], in_=t[0][0:1, :]), reads=[bt[0], b_yo[pp][0]], writes=[b_yo[pp][0]])
            S.op("pool", lambda e, t=t, pp=pp: e.tensor_copy(out=yo[pp][1][0:1, :], in_=t[1][0:1, :]), reads=[bt[1], b_yo[pp][1]], writes=[b_yo[pp][1]])
        S.dma("sp", Y_d.ap()[ct * 4:(ct + 1) * 4, :, gc, :].rearrange("c p e -> p c e"),
              yo[pp][0].rearrange("p (c e) -> p c e", e=128), reads=[b_yo[pp][0]], writes=[b_Y])
        S.dma("sp", Y_d.ap()[ct * 4:(ct + 1) * 4, :, 32 + gc, :].rearrange("c p e -> p c e"),
              yo[pp][1].rearrange("p (c e) -> p c e", e=128), reads=[b_yo[pp][1]], writes=[b_Y])

    b_yhy = Buf("yhy")
    new_stage()
    fv_ = ABF.take(64 * TQ, "p (f t) -> p f t", t=TQ)
    b_fv = Buf()
    yc = [ABF.take(64 * 128, "p (f c) -> p f c", c=128) for _ in range(2)]
    b_yc = [Buf() for _ in range(2)]
    hd = AFF.take(8)
    b_hd = Buf()
    S.dma("sp", hd, hyd_t.ap(), writes=[b_hd])
    xin = [[AFF.take(512) for _ in range(2)] for _ in range(2)]
    b_xin = [[Buf() for _ in range(2)] for _ in range(2)]
    yout = [ABF.take(512) for _ in range(2)]
    b_yout = [Buf() for _ in range(2)]
    it = 0

    def load7(n):
        tt_, c_ = divmod(n, 8)
        sl = n % 2
        S.dma("sp", yc[sl], Y_d.ap()[c_], reads=[b_Y], writes=[b_yc[sl]])
        S.dma("sp", xin[sl][0][:, 0:TQ], uT_d.ap()[c_, :, tt_ * TQ:(tt_ + 1) * TQ], reads=[b_uT], writes=[b_xin[sl][0]])
        S.dma("sp", xin[sl][1][:, 0:TQ], x0c_d.ap()[c_, :, tt_ * TQ:(tt_ + 1) * TQ], reads=[b_x0c], writes=[b_xin[sl][1]])

    for tt in range(LQ // TQ):
        S.dma("sp", fv_, finv_t.ap()[tt].rearrange("p (f t) -> p f t", t=TQ), writes=[b_fv])
        for c in range(8):
            s = it % 2
            if it == 0:
                load7(0)
            if it + 1 < 8 * (LQ // TQ):
                load7(it + 1)
            it += 1
            ps = PS[s]
            pb = b_ps[s]
            for f in range(64):
                S.op("pe", lambda e, ps=ps, s=s, f=f: e.matmul(ps[:, 0:TQ], lhsT=yc[s][:, f, :], rhs=fv_[:, f, :],
                                                               start=(f == 0), stop=(f == 63)),
                     reads=[b_yc[s], b_fv], writes=[pb])
            S.op("dve", lambda e, ps=ps, s=s, c=c: e.scalar_tensor_tensor(out=xin[s][0][:, 0:TQ], in0=xin[s][0][:, 0:TQ], scalar=hd[:, c:c + 1],
                                                                         in1=ps[:, 0:TQ], op0=ALU.mult, op1=ALU.add),
                 reads=[pb, b_xin[s][0], b_hd], writes=[b_xin[s][0]])
            S.op("dve", lambda e, s=s: e.tensor_tensor(out=yout[s][:, 0:TQ], in0=xin[s][0][:, 0:TQ], in1=xin[s][1][:, 0:TQ], op=ALU.mult),
                 reads=[b_xin[s][0], b_xin[s][1]], writes=[b_yout[s]])
            S.dma("sp", yhyT_d.ap()[c, :, tt * TQ:(tt + 1) * TQ], yout[s][:, 0:TQ], reads=[b_yout[s]], writes=[b_yhy])

    b_mrg = Buf("mrg")
    b_gsig_r = b_gsig
    b_stash = Buf("stash")
    new_stage()
    NT9 = LQ // TQ
    aT9 = [ABF.take(8 * LQ, "p (k t) -> p k t", t=LQ) for _ in range(2)]
    b_aT9 = [Buf() for _ in range(2)]
    S.dma("sp", aT9[0], attT_d.ap().rearrange("k p t -> p k t"), reads=[b_att], writes=[b_aT9[0]])
    S.dma("sp", aT9[1], yhyT_d.ap().rearrange("k p t -> p k t"), reads=[b_yhy], writes=[b_aT9[1]])
    w9 = [[ABF.take(8 * 512, "p (k c) -> p k c", c=512) for _ in range(2)] for _ in range(2)]
    b_w9 = [[Buf() for _ in range(2)] for _ in range(2)]
    NS9 = 6
    g9 = [[AFF.take(TQ) for _ in range(NS9)] for _ in range(2)]
    b_g9 = [[Buf() for _ in range(NS9)] for _ in range(2)]
    t9 = [[AFF.take(TQ) for _ in range(2)] for _ in range(2)]
    b_t9 = [[Buf() for _ in range(2)] for _ in range(2)]
    o9 = [ABF.take(TQ) for _ in range(3)]
    b_o9 = [Buf() for _ in range(3)]
    order9 = [(cg, tt, cc) for cg in range(4) for tt in range(NT9) for cc in range(4)]
    issued9 = [0]

    def prefetch9(upto):
        while issued9[0] <= min(upto, len(order9) - 1):
            n = issued9[0]
            cg_, tt_, cc_ = order9[n]
            ci = cg_ * 4 + cc_
            for br in range(2):
                S.dma("sp", g9[br][n % NS9], gsig_d.ap()[br * 16 + ci, :, tt_ * TQ:(tt_ + 1) * TQ], reads=[b_gsig],
                      writes=[b_g9[br][n % NS9]])
            issued9[0] += 1

    def load_w9(cg):
        for br, w_t in enumerate((wa_t, wh_t)):
            S.dma("pool", w9[br][cg % 2], w_t.ap()[:, cg * 512:(cg + 1) * 512].rearrange("(k p) c -> p k c", p=128),
                  writes=[b_w9[br][cg % 2]])

    load_w9(0)
    prefetch9(2)
    for n, (cg, tt, cc) in enumerate(order9):
        if tt == 0 and cc == 0 and cg + 1 < 4:
            load_w9(cg + 1)
        prefetch9(n + 3)
        ci = cg * 4 + cc
        pp = n % 2
        for br in range(2):
            ps, pb = PS[2 * pp + br], b_ps[2 * pp + br]
            for k in range(8):
                S.op("pe", lambda e, ps=ps, br=br, cg=cg, k=k, cc=cc, tt=tt: e.matmul(
                    ps[:, 0:TQ], lhsT=w9[br][cg % 2][:, k, cc * 128:(cc + 1) * 128], rhs=aT9[br][:, k, tt * TQ:(tt + 1) * TQ],
                    start=(k == 0), stop=(k == 7)), reads=[b_w9[br][cg % 2], b_aT9[br]], writes=[pb])
        for br in range(2):
            ps, pb = PS[2 * pp + br], b_ps[2 * pp + br]
            S.op("dve", lambda e, ps=ps, br=br, pp=pp, n=n: e.tensor_tensor(out=t9[pp][br], in0=ps[:, 0:TQ], in1=g9[br][n % NS9], op=ALU.mult),
                 reads=[pb, b_g9[br][n % NS9]], writes=[b_t9[pp][br]])
        S.op("pool", lambda e, pp=pp, n=n: e.tensor_tensor(out=o9[n % 3], in0=t9[pp][0], in1=t9[pp][1], op=ALU.add),
             reads=[b_t9[pp][0], b_t9[pp][1]], writes=[b_o9[n % 3]])
        S.dma("sp", mrgT_d.ap()[ci, :, tt * TQ:(tt + 1) * TQ], o9[n % 3], reads=[b_o9[n % 3]], writes=[b_mrg])

    b_xmid = Buf("xmid")

    def resid_stage(src_t, src_buf, w_t, kc, res_t, res_buf, dst_t, dst_buf, cgw, at_slots, TT=512, at_ap=None):
        cnt = [0]
        slots = {}

        NSR = 6
        orderR = [(cg, tt * (TT // 128) + tb) for cg in range(D // cgw) for tt in range(LQ // TT) for tb in range(TT // 128)]
        issued = [0]

        def pro():
            slots["r"], slots["rb"] = mk_out_slots(NSR, AFF, cgw)

        def prefetch(upto):
            while issued[0] <= min(upto, len(orderR) - 1):
                n = issued[0]
                cg_, tb_ = orderR[n]
                S.dma("sp", slots["r"][n % NSR], res_t.ap()[tb_ * 128:(tb_ + 1) * 128, cg_ * cgw:(cg_ + 1) * cgw],
                      reads=[res_buf], writes=[slots["rb"][n % NSR]])
                issued[0] += 1

        def evac(cg, tb, ps, pb):
            n = cnt[0]
            cnt[0] += 1
            assert orderR[n] == (cg, tb)
            prefetch(n + 3)
            r, rb = slots["r"][n % NSR], slots["rb"][n % NSR]
            S.op("dve", lambda e: e.tensor_tensor(out=r, in0=ps, in1=r, op=ALU.add), reads=[pb, rb], writes=[rb])
            S.dma("sp", dst_t.ap()[tb * 128:(tb + 1) * 128, cg * cgw:(cg + 1) * cgw], r, reads=[rb], writes=[dst_buf])

        linear([(src_t, src_buf, w_t, 0, kc)], D, "tm", evac, cgw=cgw, at_slots=at_slots, prologue=pro, TT=TT, T=LQ, resident=(kc <= 16), at_ap=at_ap)

    resid_stage(mrgT_d, b_mrg, wo_t, 16, xmy_t, b_xmy, xmid_d, b_xmid, 512, 2, TT=TQ)

    b_hfT = Buf("hfT")
    norm_T(xmid_d, g_ffn_t, hfT_d, b_hfT, b_xmid, nrows=LQ, mask=mask_t)
    b_act = Buf("act")
    new_stage()
    NT11 = LQ // TQ
    fcw = AFF.take(88 * 4, "p (c j) -> p c j", j=4)
    S.dma("sp", fcw, ffcw_t.ap(), writes=[b_cw])
    hf11 = ABF.take(16 * LQ, "p (k t) -> p k t", t=LQ)
    b_hf11 = Buf()
    S.dma("sp", hf11, hfT_d.ap().rearrange("k p t -> p k t"), reads=[b_hfT], writes=[b_hf11])
    w11 = [ABF.take(16 * 512, "p (k c) -> p k c", c=512) for _ in range(2)]
    b_w11 = [Buf() for _ in range(2)]
    rawb = [[AFF.take(LQ + 2) for _ in range(4)] for _ in range(2)]
    b_rawb = [[Buf() for _ in range(4)] for _ in range(2)]
    for sl_ in range(2):
        for cc_ in range(4):
            S.op("dve", lambda e, sl_=sl_, cc_=cc_: e.memset(rawb[sl_][cc_][:, 0:1], 0.0), writes=[b_rawb[sl_][cc_]])
            S.op("dve", lambda e, sl_=sl_, cc_=cc_: e.memset(rawb[sl_][cc_][:, LQ + 1:LQ + 2], 0.0), writes=[b_rawb[sl_][cc_]])
    cvg = AFF.take(LQ)
    cvv = AFF.take(LQ)
    sil = AFF.take(LQ)
    b_cvg, b_cvv, b_sil = Buf(), Buf(), Buf()
    ao = [ABF.take(LQ) for _ in range(2)]
    b_ao = [Buf() for _ in range(2)]

    def load_w11(cg):
        wv = wup_t.ap()[:, cg * 512:(cg + 1) * 512].rearrange("(k p) c -> p k c", p=128)
        S.dma("pool", w11[cg % 2][:, 0:8, :], wv[:, 0:8, :], writes=[b_w11[cg % 2]])
        S.dma("pool", w11[cg % 2][:, 8:16, :], wv[:, 8:16, :], writes=[b_w11[cg % 2]])

    def conv_full(dst, src, cidx, b_src, b_dst):
        n = LQ
        S.op("dve", lambda e: e.tensor_scalar(out=dst, in0=src[:, 0:n], scalar1=fcw[:, cidx, 0:1], scalar2=fcw[:, cidx, 3:4],
                                              op0=ALU.mult, op1=ALU.add), reads=[b_src, b_cw], writes=[b_dst])
        S.op("dve", lambda e: e.scalar_tensor_tensor(out=dst, in0=src[:, 1:n + 1], scalar=fcw[:, cidx, 1:2], in1=dst,
                                                     op0=ALU.mult, op1=ALU.add), reads=[b_src, b_cw, b_dst], writes=[b_dst])
        S.op("dve", lambda e: e.scalar_tensor_tensor(out=dst, in0=src[:, 2:n + 2], scalar=fcw[:, cidx, 2:3], in1=dst,
                                                     op0=ALU.mult, op1=ALU.add), reads=[b_src, b_cw, b_dst], writes=[b_dst])

    load_w11(0)
    pi11 = 0
    npair = 0
    for cg in range(22):
        sl = cg % 2
        if cg + 1 < 22:
            load_w11(cg + 1)
        for tt in range(NT11):
            for cc in range(4):
                ps, pb = PS[pi11 % 4], b_ps[pi11 % 4]
                pi11 += 1
                for k in range(16):
                    S.op("pe", lambda e, ps=ps, sl=sl, k=k, cc=cc, tt=tt: e.matmul(
                        ps[:, 0:TQ], lhsT=w11[sl][:, k, cc * 128:(cc + 1) * 128], rhs=hf11[:, k, tt * TQ:(tt + 1) * TQ],
                        start=(k == 0), stop=(k == 15)), reads=[b_w11[sl], b_hf11], writes=[pb])
                S.op("act", lambda e, ps=ps, sl=sl, cc=cc, tt=tt: e.activation(out=rawb[sl][cc][:, 1 + tt * TQ:1 + (tt + 1) * TQ],
                                                                              in_=ps[:, 0:TQ], func=AF.Copy),
                     reads=[pb], writes=[b_rawb[sl][cc]])
        for pr in range(2):
            c = cg * 2 + pr
            conv_full(cvg, rawb[sl][2 * pr], c, b_rawb[sl][2 * pr], b_cvg)
            conv_full(cvv, rawb[sl][2 * pr + 1], 44 + c, b_rawb[sl][2 * pr + 1], b_cvv)
            S.op("act", lambda e: e.activation(out=sil, in_=cvg, func=AF.Silu), reads=[b_cvg], writes=[b_sil])
            S.op("pool", lambda e, npair=npair: e.tensor_tensor(out=ao[npair % 2], in0=sil, in1=cvv, op=ALU.mult),
                 reads=[b_sil, b_cvv], writes=[b_ao[npair % 2]])
            S.dma("sp", actT_d.ap()[:, :, c, :].rearrange("b p t -> p b t"), ao[npair % 2].rearrange("p (b t) -> p b t", t=128),
                  reads=[b_ao[npair % 2]], writes=[b_act])
            npair += 1

    b_xout = b_xmid
    resid_stage(actT_d, b_act, wdn_t, 44, xmid_d, b_xmid, xmid_d, b_xout, 512, 2, TT=128, at_ap=lambda si, tt_: actT_d.ap()[tt_])

    b_out = Buf("out")
    norm_T(xmid_d, g_fin_t, None, b_out, b_xout, tok_out=out_t, nrows=NOUT, row_off=2)
    S.barrier()

    with nc.Block() as block:
        @block.tensor
        def _(e):
            for f in S.ops["pe"]:
                f(e)

        @block.scalar
        def _(e):
            for f in S.ops["act"]:
                f(e)

        @block.vector
        def _(e):
            for f in S.ops["dve"]:
                f(e)

        @block.gpsimd
        def _(e):
            for f in S.ops["pool"]:
                f(e)

        @block.sync
        def _(e):
            for f in S.ops["sp"]:
                f(e)
    st.close()
    return nc


_CONST = {}


def _t5_bucket(rel):
    half, max_exact = 16, 8
    ret = (rel > 0).astype(np.int32) * half
    n = np.abs(rel)
    nf = np.maximum(n, 1).astype(np.float32)
    large = max_exact + (np.log(nf / max_exact) / math.log(128 / max_exact) * (half - max_exact)).astype(np.int32)
    large = np.minimum(large, half - 1)
    return ret + np.where(n < max_exact, n, large)


def _constants():
    if _CONST:
        return _CONST
    bf = ml_dtypes.bfloat16
    N = 2 * L
    ang = 2.0 * np.pi * np.arange(N) / N
    ctab = np.cos(ang)
    stab = np.sin(ang)
    g = np.arange(L).reshape(32, 1, 1, 128)
    s = (np.arange(32).reshape(1, 1, 32, 1) * 128 + np.arange(128).reshape(1, 128, 1, 1))
    idx = (g * s) % N
    fc_cos = ctab[idx]
    fc_sin = stab[idx]
    nyq = np.where(s % 2 == 0, 1.0, -1.0)[0, :, :, 0]
    fc_sin[0, :, :, 0] = nyq
    ffwd = np.concatenate([fc_cos, fc_sin], 0).astype(np.float32).astype(bf).reshape(64, 128, 32 * 128)
    gg = (np.arange(32).reshape(1, 1, 32, 1) * 128 + np.arange(128).reshape(1, 128, 1, 1))
    finvs, masks, buckets, farbk = [], [], [], []
    far_ = np.arange(91, L)
    assert (_t5_bucket(far_) == 31).all() and (_t5_bucket(-far_) == 15).all()
    for j in range(4):
        mloc = (np.arange(LQ // TQ).reshape(-1, 1, 1, 1) * TQ + np.arange(TQ).reshape(1, 1, 1, TQ))
        t = 1024 * j - 2 + mloc
        valid = (t >= 0) & (t < L) & (mloc < NOUT + 4)
        tc = np.where(valid, t, 0)
        idx = (gg * tc) % N
        ic = ctab[idx] * (2.0 / N)
        isn = stab[idx] * (2.0 / N)
        ic[:, 0:1, 0:1, :] = 1.0 / N
        isn[:, 0:1, 0:1, :] = (np.where(tc % 2 == 0, 1.0, -1.0) / N)
        ic = ic * valid
        isn = isn * valid
        finvs.append(np.concatenate([ic, isn], 2).astype(np.float32).astype(bf).reshape(LQ // TQ, 128, 64 * TQ))
        mv = valid.reshape(-1).astype(np.float32)
        masks.append(np.ascontiguousarray(mv.reshape(LQ // 128, 128).T))
        rel = np.arange(128).reshape(128, 1) - np.arange(WT_LEN).reshape(1, WT_LEN) + WT_M0 - (1024 * j - 2)
        buckets.append(_t5_bucket(np.clip(rel, -(L - 1), L - 1)))
        fb = np.zeros((6, 32), np.int64)
        for qt_ in range(6):
            for kc_ in range(32):
                if att_tile_far(qt_, kc_):
                    qlo_ = 1024 * j - 2 + QT_ATT * qt_
                    fb[qt_, kc_] = 15 if 128 * kc_ + 127 < qlo_ else 31
        farbk.append(fb)
    f32 = np.float32
    tt_ = np.linspace(0.0, 1.0, L, dtype=f32)[:, None]
    tr = np.arange(L, dtype=f32)[:, None]
    an = (f32(2.0 * math.pi) * tr / f32(L)).astype(f32)
    bands = np.linspace(1e-4, 15, 16, dtype=f32)[None, :]
    emb = np.concatenate([tt_, np.cos(bands * an), -np.sin(bands * an)], axis=-1).astype(f32)
    max_decay = math.log(1e-2) / 0.3
    min_decay = math.log(1e-2) / 1.5
    deltas = np.abs(np.linspace(min_decay, max_decay, CH, dtype=f32)).astype(f32)
    negt = (-tt_[:, 0]).reshape(32, 128).T.copy()
    _CONST.update(dict(c_ffwd=ffwd, finvs=finvs, masks=masks, buckets=buckets, farbk=farbk, c_embT=np.ascontiguousarray(emb.T),
                       c_delta=deltas.reshape(1, CH), c_negt=negt.astype(f32), c_ident=np.eye(128, dtype=f32).astype(bf)))
    return _CONST


_NC = {}


def kernel(x, g_mix, w_in, lambda_q1, lambda_k1, lambda_q2, lambda_k2, g_subln, rel_bias,
           hy_conv_w, hy_conv_b, hy_f_w1, hy_f_b1, hy_f_w2, hy_f_b2, hy_f_w3, hy_f_b3,
           hy_f_w4, hy_freq, hy_d, w_attn_branch, w_hyena_branch, w_out, g_ffn, w_up,
           ffn_conv_w, ffn_conv_b, w_down, g_final):
    f = lambda a: np.ascontiguousarray(np.asarray(a, dtype=np.float32))
    C = _constants()
    rb = f(rel_bias)
    wt_biases = [np.ascontiguousarray(np.transpose(rb[bk], (2, 0, 1))) for bk in C["buckets"]]
    hcw = f(hy_conv_w)[0]
    hcb = f(hy_conv_b)[0]
    hycw = np.ascontiguousarray(np.concatenate([hcw, hcb[None]], 0).reshape(4, 24, 128).transpose(2, 1, 0))
    fw = f(ffn_conv_w)[0]
    fb = f(ffn_conv_b)[0]
    ffcw = np.ascontiguousarray(np.concatenate([fw, fb[None]], 0).reshape(4, 88, 128).transpose(2, 1, 0))
    common = {
        "g_mix": f(g_mix), "w_in": f(w_in)[0],
        "lam4": np.ascontiguousarray(np.stack([f(lambda_q1)[0], f(lambda_k1)[0], f(lambda_q2)[0], f(lambda_k2)[0]], 0)),
        "g_subln": f(g_subln)[0].reshape(128, 1), "hycw": hycw,
        "fw1": f(hy_f_w1)[0], "fw2": f(hy_f_w2)[0], "fw3": f(hy_f_w3)[0], "fw4": f(hy_f_w4)[0],
        "fvec": np.ascontiguousarray(np.stack([f(hy_f_b1)[0], f(hy_f_b2)[0], f(hy_f_b3)[0], f(hy_freq)[0]], 1)),
        "hyd": np.ascontiguousarray(f(hy_d)[0].reshape(8, 128).T),
        "w_attn_branch": f(w_attn_branch)[0], "w_hyena_branch": f(w_hyena_branch)[0], "w_out": f(w_out)[0],
        "g_ffn": f(g_ffn), "w_up": np.ascontiguousarray(f(w_up)[0].reshape(D, 2, 44, 128).transpose(0, 2, 1, 3).reshape(D, 2 * DFF)), "ffcw": ffcw, "w_down": f(w_down)[0],
        "g_final": f(g_final).reshape(1, D),
    }
    for k in ("c_embT", "c_delta", "c_negt", "c_ffwd", "c_ident"):
        common[k] = C[k]
    xs = f(x)
    in_maps = []
    for c in range(NCORES):
        b, j = divmod(c, 4)
        m = dict(common)
        m["x"] = xs[b]
        xm = np.zeros((LQ, D), np.float32)
        lo, hi = 1024 * j - 2, 1024 * j - 2 + NOUT + 4
        slo, shi = max(lo, 0), min(hi, L)
        xm[slo - lo:shi - lo] = xs[b, slo:shi]
        m["x_my"] = xm
        m["mask_my"] = C["masks"][j]
        m["wt_bias"] = wt_biases[j]
        fbv = rb[C["farbk"][j]]
        m["farb"] = np.ascontiguousarray(np.tile(np.transpose(fbv, (2, 0, 1)).reshape(1, -1), (128, 1)))
        m["c_finv"] = C["finvs"][j]
        in_maps.append(m)
    if "nc" not in _NC:
        _NC["nc"] = build_program()
    res = run_bass_kernel_spmd(_NC["nc"], in_maps, core_ids=list(range(NCORES)))
    out = np.empty((2, L, D), np.float32)
    for c in range(NCORES):
        b, j = divmod(c, 4)
        out[b, 1024 * j:1024 * (j + 1)] = np.asarray(res.results[c]["out"], dtype=np.float32)
    return out
```
